# Optimizing a Trainium2 kernel written in Bass

```python
import functools
import jax, jax.numpy as jnp
from jax import lax
import numpy as np

D_MODEL = 1024
BATCH = 8
SEQ = 2048
DEPTH = 1
DEC_BATCH = 32
DEC_SEQ = 4
PAST_LEN = 16384
PAGE_SIZE = 128

N_HEADS = 8
HEAD_DIM = 64
ATTN_WIDTH = N_HEADS * HEAD_DIM
IDX_HEADS = 8
IDX_DIM = 64
TOPK_MAX = 256
Q_BLOCK = 128
LRU_WIDTH = 512
LRU_BLOCKS = 8
LRU_BW = LRU_WIDTH // LRU_BLOCKS
LRU_CONV_W = 4
LRU_C = 8.0
D_FF = 3 * D_MODEL
FFN_CONV_W = 3
ROPE_THETA = 10000.0
EPS = 1e-6
IN_SPLITS = (ATTN_WIDTH, ATTN_WIDTH, ATTN_WIDTH, IDX_HEADS * IDX_DIM, IDX_DIM, IDX_HEADS,
             LRU_WIDTH, LRU_WIDTH, D_MODEL, D_MODEL)
D_IN = sum(IN_SPLITS)

kernel_name = "hybrid_dsa_rglru_convffn_step"


def split_cols(a, sizes):
    out, start = [], 0
    for s in sizes:
        out.append(a[..., start:start + s])
        start += s
    return out


def rmsnorm(x, g):
    xf = x.astype(jnp.float32)
    r = lax.rsqrt(jnp.mean(xf * xf, axis=-1, keepdims=True) + EPS)
    return (xf * r * g.astype(jnp.float32)).astype(x.dtype)


def rope(x, pos):
    half = x.shape[-1] // 2
    inv = jnp.power(ROPE_THETA, -jnp.arange(half, dtype=jnp.float32) / half)
    ang = pos.astype(jnp.float32)[:, None] * inv[None, :]
    cos = jnp.cos(ang)[None, :, None, :]
    sin = jnp.sin(ang)[None, :, None, :]
    xf = x.astype(jnp.float32)
    x1, x2 = xf[..., :half], xf[..., half:]
    return jnp.concatenate([x1 * cos - x2 * sin, x2 * cos + x1 * sin], axis=-1).astype(x.dtype)


def causal_dwconv(x, buf, w, b):
    T = x.shape[1]
    W = w.shape[0]
    xx = jnp.concatenate([buf.astype(x.dtype), x], axis=1)
    out = b
    for j in range(W):
        out = out + xx[:, j:j + T] * w[j]
    return out.astype(x.dtype), xx[:, xx.shape[1] - (W - 1):]


def rg_lru(x, h0, wa, ba, wx, bx, lam):
    B, T, _ = x.shape
    xb = x.reshape(B, T, LRU_BLOCKS, LRU_BW)
    r = jax.nn.sigmoid(jnp.einsum('btnc,ncd->btnd', xb, wa).reshape(B, T, LRU_WIDTH) + ba)
    i = jax.nn.sigmoid(jnp.einsum('btnc,ncd->btnd', xb, wx).reshape(B, T, LRU_WIDTH) + bx)
    log_a = -LRU_C * r.astype(jnp.float32) * jax.nn.softplus(-lam.astype(jnp.float32))
    a = jnp.exp(log_a)
    mult = jnp.sqrt(-jnp.expm1(2.0 * log_a))
    bt = mult * (i * x).astype(jnp.float32)
    bt = bt.at[:, 0].add(a[:, 0] * h0.astype(jnp.float32))

    def comb(l, r_):
        return (l[0] * r_[0], r_[0] * l[1] + r_[1])

    _, h = lax.associative_scan(comb, (a, bt), axis=1)
    return h.astype(x.dtype), h[:, -1].astype(x.dtype)


def index_scores(qi, wi, ki):
    s = jax.nn.relu(jnp.einsum('bqhd,bsd->bqhs', qi.astype(jnp.float32), ki.astype(jnp.float32)) * IDX_DIM ** -0.5)
    return jnp.einsum('bqhs,bqh->bqs', s, wi.astype(jnp.float32))


def sparse_attend(q, k_sel, v_sel, valid):
    s = jnp.einsum('bqhd,bqkhd->bqhk', q.astype(jnp.float32), k_sel.astype(jnp.float32)) * HEAD_DIM ** -0.5
    s = jnp.where(valid[:, :, None, :], s, -jnp.inf)
    p = jax.nn.softmax(s, axis=-1)
    return jnp.einsum('bqhk,bqkhd->bqhd', p, v_sel.astype(jnp.float32)).astype(q.dtype)


def gather_rows(a, idx):
    return jax.vmap(lambda ab, ib: ab[ib])(a, idx)


def attend_prompt(q, k, v, qi, ki, wi):
    B, T = q.shape[:2]
    topk = min(TOPK_MAX, T // 4)
    nblk = T // Q_BLOCK
    key_pos = jnp.arange(T)

    def to_blocks(a):
        return jnp.moveaxis(a.reshape((B, nblk, Q_BLOCK) + a.shape[2:]), 1, 0)

    def block(args):
        qb, qib, wib, t0 = args
        qpos = t0 + jnp.arange(Q_BLOCK)
        sc = index_scores(qib, wib, ki)
        sc = jnp.where(key_pos[None, None, :] <= qpos[None, :, None], sc, -jnp.inf)
        _, sel = lax.top_k(sc, topk)
        valid = sel <= qpos[None, :, None]
        return sparse_attend(qb, gather_rows(k, sel), gather_rows(v, sel), valid)

    out = lax.map(block, (to_blocks(q), to_blocks(qi), to_blocks(wi), jnp.arange(nblk) * Q_BLOCK))
    return jnp.moveaxis(out, 0, 1).reshape(q.shape)


def attend_sample(q, k, v, qi, ki, wi, cache_k, cache_v, cache_kidx, page_table):
    B, T = q.shape[:2]
    n_pages = page_table.shape[1]
    P = n_pages * PAGE_SIZE
    L = P + T
    topk = min(TOPK_MAX, L // 4)
    ki_past = cache_kidx[page_table].reshape(B, P, IDX_DIM).astype(ki.dtype)
    ki_all = jnp.concatenate([ki_past, ki], axis=1)
    qpos = P + jnp.arange(T)
    sc = index_scores(qi, wi, ki_all)
    sc = jnp.where(jnp.arange(L)[None, None, :] <= qpos[None, :, None], sc, -jnp.inf)
    _, sel = lax.top_k(sc, topk)
    in_past = sel < P
    pidx = jnp.minimum(sel, P - 1)
    phys = jax.vmap(lambda pt, pg: pt[pg])(page_table, pidx // PAGE_SIZE)
    slot = pidx % PAGE_SIZE
    nidx = jnp.clip(sel - P, 0, T - 1)
    k_sel = jnp.where(in_past[..., None, None], cache_k[phys, slot].astype(k.dtype), gather_rows(k, nidx))
    v_sel = jnp.where(in_past[..., None, None], cache_v[phys, slot].astype(v.dtype), gather_rows(v, nidx))
    valid = sel <= qpos[None, :, None]
    return sparse_attend(q, k_sel, v_sel, valid)


def hybrid_layer(x, pos0, attend, lru_buf, lru_h0, ffn_buf, lw):
    (norm_mix_g, w_in, lru_conv_w, lru_conv_b, lru_wa, lru_ba, lru_wx, lru_bx, lru_lambda,
     w_proj_a, w_proj_b, w_out, norm_ffn_g, w_up, ffn_conv_w, ffn_conv_b, w_down) = lw
    B, T = x.shape[:2]
    pos = pos0 + jnp.arange(T)
    h = rmsnorm(x, norm_mix_g)
    q, k, v, qi, ki, wi, xl, gl, ga, gb = split_cols(h @ w_in, IN_SPLITS)
    q = rope(q.reshape(B, T, N_HEADS, HEAD_DIM), pos)
    k = rope(k.reshape(B, T, N_HEADS, HEAD_DIM), pos)
    v = v.reshape(B, T, N_HEADS, HEAD_DIM)
    qi = rope(qi.reshape(B, T, IDX_HEADS, IDX_DIM), pos)
    ki = rope(ki[:, :, None, :], pos)[:, :, 0]
    wi = wi * IDX_HEADS ** -0.5
    y_a = attend(q, k, v, qi, ki, wi).reshape(B, T, ATTN_WIDTH) @ w_proj_a
    xc, lru_buf_new = causal_dwconv(xl, lru_buf, lru_conv_w, lru_conv_b)
    hl, h_last = rg_lru(xc, lru_h0, lru_wa, lru_ba, lru_wx, lru_bx, lru_lambda)
    y_b = (hl * jax.nn.gelu(gl)) @ w_proj_b
    x = x + (jax.nn.sigmoid(ga) * y_a + jax.nn.sigmoid(gb) * y_b) @ w_out
    h = rmsnorm(x, norm_ffn_g)
    ua, ub = split_cols(h @ w_up, (D_FF, D_FF))
    uc, ffn_buf_new = causal_dwconv(ua, ffn_buf, ffn_conv_w, ffn_conv_b)
    x = x + (jax.nn.gelu(uc) * ub) @ w_down
    return x, k, v, ki, lru_buf_new, h_last, ffn_buf_new


def setup_inputs(seed: int = 0) -> dict:
    key = jax.random.key(seed)
    ks = iter(jax.random.split(key, 40))
    n_pages = PAST_LEN // PAGE_SIZE
    n_used = DEC_BATCH * n_pages
    n_phys = n_used + n_used // 4

    def nrm(shape, scale):
        return jax.random.normal(next(ks), shape, jnp.float32) * scale

    a_c = jax.random.uniform(next(ks), (DEPTH, LRU_WIDTH), jnp.float32, 0.9, 0.999)
    a0 = jnp.power(a_c, 1.0 / LRU_C)
    lru_lambda = jnp.log(a0) - jnp.log1p(-a0)
    return {
        "x_prompt": nrm((BATCH, SEQ, D_MODEL), 1.0),
        "x_sample": nrm((DEC_BATCH, DEC_SEQ, D_MODEL), 1.0),
        "cache_k": nrm((DEPTH, n_phys, PAGE_SIZE, N_HEADS, HEAD_DIM), 1.0),
        "cache_v": nrm((DEPTH, n_phys, PAGE_SIZE, N_HEADS, HEAD_DIM), 1.0),
        "cache_kidx": nrm((DEPTH, n_phys, PAGE_SIZE, IDX_DIM), 1.0),
        "page_table": jax.random.permutation(next(ks), n_phys)[:n_used].reshape(DEC_BATCH, n_pages).astype(jnp.int32),
        "state_lru_conv": nrm((DEPTH, DEC_BATCH, LRU_CONV_W - 1, LRU_WIDTH), 1.0),
        "state_lru_h": nrm((DEPTH, DEC_BATCH, LRU_WIDTH), 0.5),
        "state_ffn_conv": nrm((DEPTH, DEC_BATCH, FFN_CONV_W - 1, D_FF), 1.0),
        "norm_mix_g": 1.0 + nrm((DEPTH, D_MODEL), 0.02),
        "w_in": nrm((DEPTH, D_MODEL, D_IN), D_MODEL ** -0.5),
        "lru_conv_w": nrm((DEPTH, LRU_CONV_W, LRU_WIDTH), LRU_CONV_W ** -0.5),
        "lru_conv_b": nrm((DEPTH, LRU_WIDTH), 0.01),
        "lru_wa": nrm((DEPTH, LRU_BLOCKS, LRU_BW, LRU_BW), LRU_BW ** -0.5),
        "lru_ba": nrm((DEPTH, LRU_WIDTH), 0.01),
        "lru_wx": nrm((DEPTH, LRU_BLOCKS, LRU_BW, LRU_BW), LRU_BW ** -0.5),
        "lru_bx": nrm((DEPTH, LRU_WIDTH), 0.01),
        "lru_lambda": lru_lambda,
        "w_proj_a": nrm((DEPTH, ATTN_WIDTH, D_MODEL), ATTN_WIDTH ** -0.5),
        "w_proj_b": nrm((DEPTH, LRU_WIDTH, D_MODEL), LRU_WIDTH ** -0.5),
        "w_out": nrm((DEPTH, D_MODEL, D_MODEL), D_MODEL ** -0.5),
        "norm_ffn_g": 1.0 + nrm((DEPTH, D_MODEL), 0.02),
        "w_up": nrm((DEPTH, D_MODEL, 2 * D_FF), D_MODEL ** -0.5),
        "ffn_conv_w": nrm((DEPTH, FFN_CONV_W, D_FF), FFN_CONV_W ** -0.5),
        "ffn_conv_b": nrm((DEPTH, D_FF), 0.01),
        "w_down": nrm((DEPTH, D_FF, D_MODEL), D_FF ** -0.5),
        "norm_final_g": 1.0 + nrm((D_MODEL,), 0.02),
    }


def reference(x_prompt, x_sample, cache_k, cache_v, cache_kidx, page_table, state_lru_conv, state_lru_h,
              state_ffn_conv, norm_mix_g, w_in, lru_conv_w, lru_conv_b, lru_wa, lru_ba, lru_wx, lru_bx,
              lru_lambda, w_proj_a, w_proj_b, w_out, norm_ffn_g, w_up, ffn_conv_w, ffn_conv_b, w_down,
              norm_final_g):
    B = x_prompt.shape[0]
    past_len = page_table.shape[1] * PAGE_SIZE
    xp, xs = x_prompt, x_sample
    st_p, st_s = [], []
    for l in range(DEPTH):
        lw = (norm_mix_g[l], w_in[l], lru_conv_w[l], lru_conv_b[l], lru_wa[l], lru_ba[l], lru_wx[l], lru_bx[l],
              lru_lambda[l], w_proj_a[l], w_proj_b[l], w_out[l], norm_ffn_g[l], w_up[l], ffn_conv_w[l],
              ffn_conv_b[l], w_down[l])
        xp, *sp = hybrid_layer(
            xp, 0, attend_prompt,
            jnp.zeros((B, LRU_CONV_W - 1, LRU_WIDTH), xp.dtype),
            jnp.zeros((B, LRU_WIDTH), xp.dtype),
            jnp.zeros((B, FFN_CONV_W - 1, D_FF), xp.dtype), lw)
        att_s = functools.partial(attend_sample, cache_k=cache_k[l], cache_v=cache_v[l],
                                  cache_kidx=cache_kidx[l], page_table=page_table)
        xs, *ss = hybrid_layer(xs, past_len, att_s, state_lru_conv[l], state_lru_h[l], state_ffn_conv[l], lw)
        st_p.append(sp)
        st_s.append(ss)
    y_prompt = rmsnorm(xp, norm_final_g)
    y_sample = rmsnorm(xs, norm_final_g)
    k_p, v_p, ki_p, lc_p, lh_p, fc_p = [jnp.stack(a) for a in zip(*st_p)]
    k_s, v_s, ki_s, lc_s, lh_s, fc_s = [jnp.stack(a) for a in zip(*st_s)]
    return (y_prompt, y_sample, k_p, v_p, ki_p, lc_p, lh_p, fc_p, k_s, v_s, ki_s, lc_s, lh_s, fc_s)
```

```python
import numpy as np
import concourse.bass as bass
import concourse.mybir as mybir
from concourse.bass_utils import run_bass_kernel_spmd

F32 = mybir.dt.float32
BF16 = mybir.dt.bfloat16
I32 = mybir.dt.int32
U32 = mybir.dt.uint32
AF = mybir.ActivationFunctionType
ALU = mybir.AluOpType
AX = mybir.AxisListType

_OUTKEYS = ("out", "accum_out", "out_max", "out_indices", "ap")
SEM_LIMIT = 30000


class Buf:
    def __init__(self, t, kind):
        self.t = t
        self.kind = kind
        self.w = {}
        self.r = {}
        self.ld = None
        self.st = None

    def __getitem__(self, idx):
        return self.t[idx]


def _upd(d, sem, val):
    k = sem.name if hasattr(sem, "name") else id(sem)
    if k not in d or d[k][1] < val:
        d[k] = (sem, val)


class Eng:
    def __init__(self, K, name, eng):
        self.K = K
        self.name = name
        self.eng = eng
        self.sem = None
        self.cnt = 0
        self.nsem = 0
        self.waited = {}
        self.ninst = 0
        self.lazy = False
        self.lazy_self_ok = False
        self.pending = None

    def wait(self, deps):
        for k, (sem, val) in deps.items():
            if self.waited.get(k, 0) < val:
                owner = self.K.sem_owner.get(k)
                if owner is not None and owner.pending is not None and owner.sem is sem and val > owner.cnt:
                    if owner is self and self.lazy_self_ok:
                        continue
                    owner.flush()
                self.eng.wait_ge(sem, val)
                self.waited[k] = val

    def flush(self):
        if self.pending is not None:
            self.cnt += 1
            self.pending.then_inc(self.sem, 1)
            self.pending = None

    def tick(self, inst):
        if self.pending is None and (self.sem is None or self.cnt >= SEM_LIMIT):
            self.sem = self.K.nc.alloc_semaphore(f"s_{self.name}{self.nsem}")
            self.K.sem_owner[self.sem.name] = self
            self.nsem += 1
            self.cnt = 0
        self.ninst += 1
        if self.lazy:
            self.pending = inst
            return (self.sem, self.cnt + 1)
        self.cnt += 1
        inst.then_inc(self.sem, 1)
        return (self.sem, self.cnt)


class KB:
    def __init__(self, nc):
        self.nc = nc
        self.bufs = {}
        self.pe = Eng(self, "pe", nc.tensor)
        self.act = Eng(self, "act", nc.scalar)
        self.dve = Eng(self, "dve", nc.vector)
        self.pool = Eng(self, "pool", nc.gpsimd)
        self.sp = Eng(self, "sp", nc.sync)
        self.final = {}
        self.nsem_dma = 0
        self.sem_owner = {}
        self.pe.lazy = True
        self.pe.lazy_self_ok = True

    def sb(self, name, shape, dtype=F32):
        t = self.nc.alloc_sbuf_tensor(name, list(shape), dtype)
        b = Buf(t, "sb")
        self.bufs[t.name] = b
        return b

    def ps(self, name, shape, dtype=F32):
        t = self.nc.alloc_psum_tensor(name, list(shape), dtype)
        b = Buf(t, "ps")
        self.bufs[t.name] = b
        return b

    def dram(self, name, shape, dtype, kind):
        t = self.nc.dram_tensor(name, list(shape), dtype, kind=kind)
        b = Buf(t, "dram")
        self.bufs[t.name] = b
        return b

    def bufof(self, ap):
        return self.bufs[ap.tensor.name]

    def op(self, E, name, *args, **kw):
        reads, writes = [], []
        for k, v in kw.items():
            if hasattr(v, "tensor") and hasattr(v, "partition_size"):
                (writes if k in _OUTKEYS else reads).append(self.bufof(v))
        deps = {}
        for b in reads:
            for k, sv in b.w.items():
                _upd(deps, *sv)
        for b in writes:
            for k, sv in b.w.items():
                _upd(deps, *sv)
            for k, sv in b.r.items():
                _upd(deps, *sv)
        E.wait(deps)
        inst = getattr(E.eng, name)(*args, **kw)
        sv = E.tick(inst)
        for b in reads:
            _upd(b.r, *sv)
        for b in writes:
            _upd(b.w, *sv)
        return inst

    def new_group(self):
        rec = [self.nc.alloc_semaphore(f"grp{self.nsem_dma}"), 0, []]
        self.nsem_dma += 1
        return rec

    def close_group(self, rec):
        for (ob, ib) in rec[2]:
            _upd(ob.w, rec[0], rec[1])
            _upd(ib.r, rec[0], rec[1])
            if ob.kind == "dram":
                _upd(self.final, rec[0], rec[1])

    def dma(self, Q, out, in_, indirect=None, group=None, **kw):
        ob, ib = self.bufof(out), self.bufof(in_)
        extra_reads = []
        if indirect is not None:
            extra_reads.append(self.bufof(indirect))
        deps = {}
        for b in [ib] + extra_reads:
            for k, sv in b.w.items():
                _upd(deps, *sv)
        for k, sv in ob.w.items():
            _upd(deps, *sv)
        for k, sv in ob.r.items():
            _upd(deps, *sv)
        Q.wait(deps)
        if group is not None:
            rec = group
            group[2].append((ob, ib))
        elif ob.kind == "sb":
            if ob.ld is None:
                ob.ld = [self.nc.alloc_semaphore(f"ld{self.nsem_dma}"), 0]
                self.nsem_dma += 1
            rec = ob.ld
        else:
            if ib.st is None:
                ib.st = [self.nc.alloc_semaphore(f"st{self.nsem_dma}"), 0]
                self.nsem_dma += 1
            rec = ib.st
        rec[1] += 16
        if indirect is not None:
            inst = Q.eng.indirect_dma_start(out=out, out_offset=None, in_=in_, in_offset=kw.pop("in_offset"), **kw)
        else:
            inst = Q.eng.dma_start(out=out, in_=in_, **kw)
        inst.then_inc(rec[0], 16)
        sv = (rec[0], rec[1])
        _upd(ob.w, *sv)
        for b in [ib] + extra_reads:
            _upd(b.r, *sv)
        if ob.kind == "dram":
            _upd(self.final, *sv)
        return inst

    def barrier(self):
        deps = {}
        for E in (self.pe, self.act, self.dve, self.pool, self.sp):
            E.flush()
            if E.sem is not None:
                _upd(deps, E.sem, E.cnt)
        for b in self.bufs.values():
            for rec in (b.ld, b.st):
                if rec is not None:
                    _upd(deps, rec[0], rec[1])
        for E in (self.pe, self.act, self.dve, self.pool, self.sp):
            E.wait(deps)

    def scoped(self, stack, name, shape, dtype=F32):
        t = stack.enter_context(self.nc.sbuf_tensor(name, list(shape), dtype))
        b = Buf(t, "sb")
        self.bufs[t.name] = b
        return b

    def finish(self):
        for E in (self.pe, self.act, self.dve, self.pool):
            E.flush()
        self.sp.wait(self.final)
        self.sp.eng.nop() if False else None

import contextlib
import ml_dtypes

D = 1024
SEQ = 2048
NT = SEQ // 128
NSB = 4
NST = 4
NS = NSB * NST
PAST = 16384
NPAGES = 128
DIN = 5192
DFF = 3072
EPS = 1e-6
NBIS = 20
NEG = -1.0e30
CANDR = 4
NC_ = CANDR * 8


def build(n_phys, do_sample=True, do_ffn=True):
    nc = bass.Bass("TRN2", target_bir_lowering=False)
    K = KB(nc)
    op, dma = K.op, K.dma
    PE, ACT, DVE, POOL, SP = K.pe, K.act, K.dve, K.pool, K.sp

    def din(name, shape, dt=F32):
        return K.dram(name, shape, dt, "ExternalInput")

    def dout(name, shape, dt=F32):
        return K.dram(name, shape, dt, "ExternalOutput")

    x_p = din("x_prompt", [SEQ, D])
    x_s = din("x_sample", [NS, D])
    c_k = din("cache_k", [n_phys * 128, 512])
    c_v = din("cache_v", [n_phys * 128, 512])
    c_i = din("cache_kidx", [n_phys * 128, 64])
    p_t = din("page_table", [NSB, NPAGES], I32)
    s_lc = din("state_lru_conv", [NSB, 3, 512])
    s_lh = din("state_lru_h", [NSB, 512])
    s_fc = din("state_ffn_conv", [NSB, 2, DFF])
    g_mix = din("norm_mix_g", [1, D])
    w_in = din("w_in", [D, DIN])
    l_cw = din("lru_conv_w", [4, 512])
    l_cb = din("lru_conv_b", [512])
    l_wa = din("lru_wa", [8, 64, 64])
    l_ba = din("lru_ba", [512])
    l_wx = din("lru_wx", [8, 64, 64])
    l_bx = din("lru_bx", [512])
    l_lam = din("lru_lambda", [512])
    w_pa = din("w_proj_a", [512, D])
    w_pb = din("w_proj_b", [512, D])
    w_o = din("w_out", [D, D])
    g_ffn = din("norm_ffn_g", [1, D])
    w_up = din("w_up", [D, 2 * DFF])
    f_cw = din("ffn_conv_w", [3, DFF])
    f_cb = din("ffn_conv_b", [DFF])
    w_dn = din("w_down", [DFF, D])
    g_fin = din("norm_final_g", [1, D])
    c_idb = din("c_ident_bf", [128, 128], BF16)
    c_idf = din("c_ident_f", [128, 128])
    c_cosp = din("c_cos_p", [SEQ, 32])
    c_sinp = din("c_sin_p", [SEQ, 32])
    c_coss = din("c_cos_s", [NS, 32])
    c_sins = din("c_sin_s", [NS, 32])
    c_cmask = din("c_cmask", [128, 128])
    c_pow2 = din("c_pow2", [128, NBIS + 2])
    c_misc = din("c_misc", [128, 1024])
    c_blk = din("c_blk", [128, 128], BF16)
    c_p0 = din("c_p0", [32, 32 * 128], BF16)
    c_selq = din("c_selq", [NS, NSB * 128])
    y_p = dout("y_prompt", [SEQ, D])
    y_s = dout("y_sample", [NS, D])
    k_p = dout("k_prompt", [SEQ, 512])
    v_p = dout("v_prompt", [SEQ, 512])
    ki_p = dout("kidx_prompt", [SEQ, 64])
    lc_p = dout("lru_conv_prompt", [3, 512])
    lh_p = dout("lru_h_prompt", [512])
    fc_p = dout("ffn_conv_prompt", [2, DFF])
    k_s = dout("k_sample", [NS, 512])
    v_s = dout("v_sample", [NS, 512])
    ki_s = dout("kidx_sample", [NS, 64])
    lc_s = dout("lru_conv_sample", [NSB, 3, 512])
    lh_s = dout("lru_h_sample", [NSB, 512])
    fc_s = dout("ffn_conv_sample", [NSB, 2, DFF])
    x1_d = K.dram("x1_scratch", [SEQ + NS, D], F32, "Internal")
    wi_d = K.dram("wi_scratch", [NS, 8], F32, "Internal")

    pT = [K.ps(f"pT{i}", [128, 1024], BF16) for i in range(2)]
    OB = [K.ps(f"OB{i}", [128, 512], F32) for i in range(2)]
    GB = [K.ps(f"GB{i}", [128, 512], F32) for i in range(4)]
    rr = {"g": 0, "t": 0}

    def gbank():
        rr["g"] = (rr["g"] + 1) % 4
        return GB[rr["g"]]

    tb_mode = {"single": False}

    def tbank():
        if tb_mode["single"]:
            return pT[0]
        rr["t"] = (rr["t"] + 1) % 2
        return pT[rr["t"]]

    idb = K.sb("idb", [128, 128], BF16)
    idf = K.sb("idf", [128, 128], F32)
    cmask = K.sb("cmask", [128, 128], F32)
    pow2 = K.sb("pow2", [128, NBIS + 2], F32)
    gbc = K.sb("gbc", [128, D], F32)
    gbc2 = None
    dma(SP, out=idb[:], in_=c_idb[:])
    dma(SP, out=idf[:], in_=c_idf[:])
    dma(SP, out=cmask[:], in_=c_cmask[:])
    dma(SP, out=pow2[:], in_=c_pow2[:])
    dma(SP, out=gbc[:], in_=g_mix[:].broadcast_to([128, D]))


    lcw = K.sb("lcw", [128, 4, 4], F32)
    lcb = K.sb("lcb", [128, 4], F32)
    lba = K.sb("lba", [128, 4], F32)
    lbx = K.sb("lbx", [128, 4], F32)
    lcl = K.sb("lcl", [128, 4], F32)
    fcw = K.sb("fcw", [128, 3, 24], F32)
    fcb = K.sb("fcb", [128, 24], F32)
    def dma_cols(dram2d, c, sb_ap, load, group=None):
        d = dram2d[:, c * 128:(c + 1) * 128].rearrange("j p -> p j")
        if load:
            dma(SP, out=sb_ap, in_=d, allow_slow_non_contiguous=True, group=group)
        else:
            dma(SP, out=d, in_=sb_ap, allow_slow_non_contiguous=True, group=group)

    for c in range(4):
        dma_cols(l_cw, c, lcw[:, :, c], True)
    dma(SP, out=lcb[:], in_=l_cb[:].rearrange("(c p) -> p c", p=128), allow_slow_non_contiguous=True)
    dma(SP, out=lba[:], in_=l_ba[:].rearrange("(c p) -> p c", p=128), allow_slow_non_contiguous=True)
    dma(SP, out=lbx[:], in_=l_bx[:].rearrange("(c p) -> p c", p=128), allow_slow_non_contiguous=True)
    dma(SP, out=lcl[:], in_=l_lam[:].rearrange("(c p) -> p c", p=128), allow_slow_non_contiguous=True)
    for c in range(24):
        dma_cols(f_cw, c, fcw[:, :, c], True)
    dma(SP, out=fcb[:], in_=f_cb[:].rearrange("(c p) -> p c", p=128), allow_slow_non_contiguous=True)
    op(ACT, "activation", out=lcl[:], in_=lcl[:], func=AF.Exp, scale=-1.0)
    op(ACT, "activation", out=lcl[:], in_=lcl[:], func=AF.Ln, bias=1.0)
    op(DVE, "tensor_scalar", out=lcl[:], in0=lcl[:], scalar1=-8.0, scalar2=None, op0=ALU.mult)

    ss = K.sb("ss", [128, 1], F32)
    rstd = K.sb("rstd", [128, 1], F32)


    def rmsnorm_to_bf(xt, n, g, hb, junk=None):
        junk = hb if junk is None else junk
        op(ACT, "activation", out=junk[0:n, :], in_=xt[0:n, :], func=AF.Square, accum_out=ss[0:n, :])
        op(DVE, "tensor_scalar", out=rstd[0:n, :], in0=ss[0:n, :], scalar1=1.0 / D, scalar2=EPS,
           op0=ALU.mult, op1=ALU.add)
        op(ACT, "activation", out=rstd[0:n, :], in_=rstd[0:n, :], func=AF.Sqrt)
        op(DVE, "reciprocal", out=rstd[0:n, :], in_=rstd[0:n, :])
        op(DVE, "scalar_tensor_tensor", out=hb[0:n, :], in0=xt[0:n, :], scalar=rstd[0:n, 0:1], in1=g[0:n, :],
           op0=ALU.mult, op1=ALU.mult)

    def transpose_cols(src, n, ncols, dst_fn):
        nb = ncols // 128
        for c0 in range(0, nb, 8):
            tb = tbank()
            m = min(8, nb - c0)
            for c in range(m):
                op(PE, "transpose", out=tb[:, c * 128:c * 128 + n], in_=src[0:n, (c0 + c) * 128:(c0 + c + 1) * 128],
                   identity=idb[0:n, 0:n])
            for c in range(m):
                op(ACT, "copy", out=dst_fn(c0 + c), in_=tb[:, c * 128:c * 128 + n])

    def rope(src_ps, n, nh, cosb, sinb, dst, tmp):
        v = src_ps.rearrange("p (h two d) -> p h two d", h=nh, two=2)
        x1, x2 = v[:, :, 0, :], v[:, :, 1, :]
        cb = cosb[0:n, :].unsqueeze(1).broadcast_to([n, nh, 32])
        sb_ = sinb[0:n, :].unsqueeze(1).broadcast_to([n, nh, 32])
        t1, t2 = tmp[0][0:n, 0:nh, :], tmp[1][0:n, 0:nh, :]
        op(DVE, "tensor_tensor", out=t1, in0=x1, in1=cb, op=ALU.mult)
        op(DVE, "tensor_tensor", out=t2, in0=x2, in1=sb_, op=ALU.mult)
        op(DVE, "tensor_tensor", out=dst[:, :, 0:32], in0=t1, in1=t2, op=ALU.subtract)
        op(DVE, "tensor_tensor", out=t1, in0=x2, in1=cb, op=ALU.mult)
        op(DVE, "tensor_tensor", out=t2, in0=x1, in1=sb_, op=ALU.mult)
        op(DVE, "tensor_tensor", out=dst[:, :, 32:64], in0=t1, in1=t2, op=ALU.add)

    with contextlib.ExitStack() as so:
        attA = K.scoped(so, "attA", [128, 4, SEQ], BF16)
        s_q = K.scoped(so, "s_q", [NS, 512], F32)
        s_qi = K.scoped(so, "s_qi", [NS, 8, 128], BF16)
        s_kib = K.scoped(so, "s_kib", [NS, 64], BF16)
        s_wi = K.scoped(so, "s_wi", [NS, 8], F32)
        s_attT = K.scoped(so, "s_attT", [128, 4, NS], BF16)
        xt = K.scoped(so, "xt", [128, D], F32)
        hb = K.scoped(so, "hb", [128, D], BF16)
        hT = K.scoped(so, "hT", [128, 8, 128], BF16)

        def load_norm_T(n, xsrc):
            dma(SP, out=xt[0:n, :], in_=xsrc)
            rmsnorm_to_bf(xt, n, gbc, hb)
            transpose_cols(hb, n, D, lambda c: hT[:, c, 0:n])

        with contextlib.ExitStack() as s1:
            NA = 2120
            win = K.scoped(s1, "win_a", [128, 8, NA], BF16)
            for kc in range(8):
                for c0 in range(0, NA, 1060):
                    dma(POOL, out=win[:, kc, c0:c0 + 1060], in_=w_in[kc * 128:(kc + 1) * 128, c0:c0 + 1060])
            KTb = [K.scoped(s1, f"KT{j}", [128, 4, 128], BF16) for j in range(NT)]
            VAb = [K.scoped(s1, f"VA{j}", [128, 8, 65], BF16) for j in range(NT)]
            kiT = K.scoped(s1, "kiT", [128, SEQ], BF16)
            for j in range(NT):
                op(POOL, "memset", ap=VAb[j][:], constant=1.0)
            rt = [K.scoped(s1, f"rt{i}", [128, 8, 32], F32) for i in range(2)]
            cosb = K.scoped(s1, "cosb", [128, 32], F32)
            sinb = K.scoped(s1, "sinb", [128, 32], F32)
            qb = K.scoped(s1, "qb", [128, 8, 64], BF16)
            qTs = [K.scoped(s1, f"qT{i}", [128, 4, 128], BF16) for i in range(3)]
            kf = K.scoped(s1, "kf", [128, 8, 64], F32)
            kb = K.scoped(s1, "kb", [128, 512], BF16)
            vf = K.scoped(s1, "vf", [128, 512], F32)
            qib = K.scoped(s1, "qib", [128, 8, 64], BF16)
            qiT = K.scoped(s1, "qiT", [128, 4, 128], BF16)
            kif = K.scoped(s1, "kif", [128, 1, 64], F32)
            kib2 = K.scoped(s1, "kib2", [128, 128], BF16)
            wif = K.scoped(s1, "wif", [128, 8], F32)
            dg = K.scoped(s1, "dg", [128, 8, 128], BF16)
            Rh = [K.scoped(s1, f"Rh{i}", [128, 512], BF16) for i in range(4)]
            scs = [K.scoped(s1, f"sc{i}", [128, SEQ], F32) for i in range(2)]
            bis = K.scoped(s1, "bis", [128, 8], F32)
            dl = K.scoped(s1, "dl", [128, NBIS + 2], F32)
            mk = K.scoped(s1, "mk", [128, SEQ], BF16)
            mkTs = [K.scoped(s1, f"mkT{i}", [128, NT, 128], BF16) for i in range(2)]
            PTb = [K.scoped(s1, f"PTb{i}", [128, 512], BF16) for i in range(3)]
            pT1f = pT[1][:].bitcast(F32)
            rcp = K.scoped(s1, "rcp", [128, 8, 1], F32)
            att = K.scoped(s1, "att", [128, 8, 64], BF16)
            rrR = {"r": 0, "p": 0, "ga": 0, "gc": 0}
            GA = [GB[0], GB[1]]
            SCB = GB[2]
            GC = [GB[3], GB[3]]
            tb_mode["single"] = True

            def gbankA():
                rrR["ga"] = (rrR["ga"] + 1) % 2
                return GA[rrR["ga"]]

            def front_a(n, xsrc, cos_d, sin_d, is_prompt, i):
                load_norm_T(n, xsrc)
                dma(SP, out=cosb[0:n, :], in_=cos_d)
                dma(SP, out=sinb[0:n, :], in_=sin_d)
                yield

                def tm_group(c0, c1):
                    g = gbankA()
                    for kc in range(8):
                        op(PE, "matmul", out=g[0:n, 0:c1 - c0], lhsT=hT[:, kc, 0:n], rhs=win[:, kc, c0:c1],
                           start=(kc == 0), stop=(kc == 7))
                    return g

                g = tm_group(0, 512)
                if is_prompt:
                    rope(g[0:n, :], n, 8, cosb, sinb, qb[0:n], rt)
                else:
                    rope(g[0:n, :], n, 8, cosb, sinb, s_q[0:n, :].rearrange("p (h d) -> p h d", h=8), rt)
                yield
                g = tm_group(512, 1024)
                rope(g[0:n, :], n, 8, cosb, sinb, kf[0:n], rt)
                if is_prompt:
                    dma(SP, out=k_p[i * 128:(i + 1) * 128, :], in_=kf[:].rearrange("p h d -> p (h d)"))
                    op(POOL, "tensor_copy", out=kb[0:n, :], in_=kf[:].rearrange("p h d -> p (h d)"))
                else:
                    dma(SP, out=k_s[:, :], in_=kf[0:n].rearrange("p h d -> p (h d)"))
                yield
                g = tm_group(1024, 1536)
                op(ACT, "copy", out=vf[0:n, :], in_=g[0:n, :])
                if is_prompt:
                    dma(SP, out=v_p[i * 128:(i + 1) * 128, :], in_=vf[:, :])
                    op(POOL, "tensor_copy", out=VAb[i][:, :, 0:64], in_=vf[:].rearrange("p (h d) -> p h d", h=8))
                else:
                    dma(SP, out=v_s[:, :], in_=vf[0:n, :])
                yield
                g = tm_group(1536, 2048)
                rope(g[0:n, :], n, 8, cosb, sinb, (qib if is_prompt else s_qi)[0:n], rt)
                yield
                g = tm_group(2048, 2120)
                rope(g[0:n, 0:64], n, 1, cosb, sinb, kif[0:n], rt)
                if is_prompt:
                    dma(SP, out=ki_p[i * 128:(i + 1) * 128, :], in_=kif[:, 0, :])
                    op(POOL, "tensor_copy", out=kib2[0:n, 0:64], in_=kif[0:n, 0, :])
                    op(POOL, "tensor_copy", out=kib2[0:n, 64:128], in_=kif[0:n, 0, :])
                    op(ACT, "copy", out=wif[0:n, :], in_=g[0:n, 64:72])
                else:
                    dma(SP, out=ki_s[:, :], in_=kif[0:n, 0, :])
                    op(POOL, "tensor_copy", out=s_kib[0:n, :], in_=kif[0:n, 0, :])
                    op(POOL, "tensor_copy", out=s_qi[0:n, :, 64:128], in_=s_qi[0:n, :, 0:64])
                    op(ACT, "copy", out=s_wi[0:n, :], in_=g[0:n, 64:72])
                yield

            if do_sample:
                for _ in front_a(NS, x_s[:, :], c_coss[:, :], c_sins[:, :], False, 0):
                    pass

            def stageA(i):
                n = 128
                L = 128 * (i + 1)
                sc = scs[i % 2]
                qT = qTs[i % 3]
                yield from front_a(n, x_p[i * 128:(i + 1) * 128, :], c_cosp[i * 128:(i + 1) * 128, :],
                                   c_sinp[i * 128:(i + 1) * 128, :], True, i)
                transpose_cols(qb[:].rearrange("p h d -> p (h d)"), n, 512, lambda c: qT[:, c, :])
                yield
                transpose_cols(kb, n, 512, lambda c: KTb[i][:, c, :])
                yield
                transpose_cols(qib[:].rearrange("p h d -> p (h d)"), n, 512, lambda c: qiT[:, c, :])
                transpose_cols(kib2, n, 128, lambda c: kiT[:, i * 128:(i + 1) * 128])
                for h in range(8):
                    op(POOL, "tensor_scalar", out=dg[:, h, :], in0=idf[:, :], scalar1=wif[:, h:h + 1], scalar2=1.0,
                       op0=ALU.mult, op1=ALU.mult)
                yield
                for c0 in range(0, L, 512):
                    c1 = min(L, c0 + 512)
                    wd = c1 - c0
                    acc = SCB
                    gs = {}

                    def issueS(h):
                        pr, hf = h // 2, h % 2
                        g = gbankA()
                        op(PE, "matmul", out=g[:, 0:wd], lhsT=qiT[hf * 64:(hf + 1) * 64, pr, :],
                           rhs=kiT[hf * 64:(hf + 1) * 64, c0:c1], start=True, stop=True)
                        gs[h] = g

                    issueS(0)
                    Rs_ = {}
                    for h in range(8):
                        if h + 1 < 8:
                            issueS(h + 1)
                        Rs_[h] = Rh[h % 4]
                        op(ACT, "activation", out=Rs_[h][:, 0:wd], in_=gs[h][:, 0:wd], func=AF.Relu)
                        if h >= 1:
                            op(PE, "matmul", out=acc[:, 0:wd], lhsT=dg[:, h - 1, :], rhs=Rs_[h - 1][:, 0:wd],
                               start=(h - 1 == 0), stop=False)
                        if h % 2 == 1:
                            yield
                    op(PE, "matmul", out=acc[:, 0:wd], lhsT=dg[:, 7, :], rhs=Rs_[7][:, 0:wd], start=False, stop=True)
                    if c1 == L:
                        if wd > 128:
                            op(ACT, "copy", out=sc[:, c0:c1 - 128], in_=acc[:, 0:wd - 128])
                        op(DVE, "tensor_tensor", out=sc[:, c1 - 128:c1], in0=acc[:, wd - 128:wd], in1=cmask[:, :], op=ALU.add)
                    else:
                        op(ACT, "copy", out=sc[:, c0:c1], in_=acc[:, 0:wd])
                    yield

            def stageB(i):
                n = 128
                L = 128 * (i + 1)
                sc = scs[i % 2]
                mkT = mkTs[i % 2]
                tcol = bis[:, 3:4]
                if i < 2:
                    op(DVE, "memset", ap=tcol, constant=-1.0e29)
                else:
                    op(DVE, "tensor_reduce", out=bis[:, 0:1], in_=sc[:, 0:L], axis=AX.X, op=ALU.max)
                    op(DVE, "tensor_reduce", out=bis[:, 1:2], in_=sc[:, 0:L - 128], axis=AX.X, op=ALU.min)
                    yield
                    op(DVE, "tensor_tensor", out=bis[:, 2:3], in0=bis[:, 0:1], in1=bis[:, 1:2], op=ALU.subtract)
                    op(DVE, "tensor_scalar", out=dl[:, :], in0=pow2[:, :], scalar1=bis[:, 2:3], scalar2=None, op0=ALU.mult)
                    op(DVE, "tensor_tensor", out=tcol, in0=bis[:, 1:2], in1=dl[:, 1:2], op=ALU.add)
                    yield
                    for k in range(2, NBIS + 2):
                        op(DVE, "tensor_scalar", out=mk[:, 0:L], in0=sc[:, 0:L], scalar1=tcol, scalar2=None,
                           op0=ALU.is_ge, op1=ALU.add, accum_out=bis[:, 4:5])
                        op(DVE, "tensor_scalar", out=bis[:, 5:6], in0=bis[:, 4:5], scalar1=255.5, scalar2=dl[:, k - 1:k],
                           op0=ALU.is_ge, op1=ALU.mult)
                        op(DVE, "scalar_tensor_tensor", out=tcol, in0=tcol, scalar=dl[:, k:k + 1], in1=bis[:, 5:6],
                           op0=ALU.subtract, op1=ALU.add)
                        yield
                    op(DVE, "tensor_tensor", out=tcol, in0=tcol, in1=dl[:, NBIS + 1:NBIS + 2], op=ALU.subtract)
                    yield
                op(DVE, "tensor_scalar", out=mk[:, 0:L], in0=sc[:, 0:L], scalar1=tcol, scalar2=None, op0=ALU.is_ge)
                yield
                for c0 in range(0, i + 1, 8):
                    transpose_cols(mk[:, c0 * 128:min(L, (c0 + 8) * 128)], n, min(L, (c0 + 8) * 128) - c0 * 128,
                                   lambda c: mkT[:, c0 + c, :])
                    yield

            def stageC(i):
                n = 128
                qT = qTs[i % 3]
                mkT = mkTs[i % 2]
                units = [(h, j0, min(i + 1, j0 + 4)) for h in range(8) for j0 in range(0, i + 1, 4)]
                GC = [GB[3], pT1f]
                sg, Pu = {}, {}

                def issueST(u):
                    h, j0, j1 = units[u]
                    pr, hf = h // 2, h % 2
                    g = GC[u % 2]
                    for j in range(j0, j1):
                        op(PE, "matmul", out=g[:, (j - j0) * 128:(j - j0 + 1) * 128],
                           lhsT=KTb[j][hf * 64:(hf + 1) * 64, pr, :],
                           rhs=qT[hf * 64:(hf + 1) * 64, pr, :], start=True, stop=True)
                    sg[u] = g

                def issuePV(u):
                    h, j0, j1 = units[u]
                    ob = OB[h // 4]
                    for j in range(j0, j1):
                        op(PE, "matmul", out=ob[:, (h % 4) * 65:(h % 4) * 65 + 65], lhsT=Pu[u][:, (j - j0) * 128:(j - j0 + 1) * 128],
                           rhs=VAb[j][:, h, :], start=(j == 0), stop=(j == i))

                issueST(0)
                yield
                for u, (h, j0, j1) in enumerate(units):
                    if u + 1 < len(units):
                        issueST(u + 1)
                    g = sg.pop(u)
                    Pu[u] = PTb[u % 3]
                    wd = (j1 - j0) * 128
                    op(ACT, "activation", out=Pu[u][:, 0:wd], in_=g[:, 0:wd], func=AF.Exp, scale=0.125)
                    op(POOL, "tensor_tensor", out=Pu[u][:, 0:wd], in0=Pu[u][:, 0:wd],
                       in1=mkT[:, j0:j1, :].rearrange("p j q -> p (j q)"), op=ALU.mult)
                    if u >= 1:
                        issuePV(u - 1)
                    yield
                issuePV(len(units) - 1)
                yield
                for hh in range(2):
                    ov = OB[hh][:, 0:260].rearrange("p (h d) -> p h d", h=4)
                    op(DVE, "reciprocal", out=rcp[:, hh * 4:hh * 4 + 4, :], in_=ov[:, :, 64:65])
                    op(DVE, "tensor_tensor", out=att[:, hh * 4:hh * 4 + 4, :], in0=ov[:, :, 0:64],
                       in1=rcp[:, hh * 4:hh * 4 + 4, :].broadcast_to([128, 4, 64]), op=ALU.mult)
                yield
                transpose_cols(att[:].rearrange("p h d -> p (h d)"), n, 512, lambda c: attA[:, c, i * 128:(i + 1) * 128])
                yield

            stages = [stageA, stageB, stageC]
            for step in range(NT + len(stages) - 1):
                alive = [stages[s](step - s) for s in range(len(stages)) if 0 <= step - s < NT]
                while alive:
                    nxt = []
                    for gnr in alive:
                        try:
                            next(gnr)
                            nxt.append(gnr)
                        except StopIteration:
                            pass
                    alive = nxt
            tb_mode["single"] = False
        K.barrier()

        if do_sample:
            with contextlib.ExitStack() as s3:
                NBS = 34
                BR = 16384.0
                misc = K.scoped(s3, "misc", [128, 1024], F32)
                blk = K.scoped(s3, "blk", [128, 128], BF16)
                p0 = K.scoped(s3, "p0", [32, 32 * 128], BF16)
                selq = K.scoped(s3, "selq", [NS, NSB, 128], F32)
                dma(SP, out=misc[:], in_=c_misc[:])
                dma(SP, out=blk[:], in_=c_blk[:])
                dma(SP, out=p0[:], in_=c_p0[:])
                dma(SP, out=selq[:], in_=c_selq[:].rearrange("t (b m) -> t b m", b=NSB))
                slotbase = misc[:, 0:1]
                negnew = misc[:, 1:5]
                iota_pg = misc[:, 16:144]
                agg = misc[:, 160:224].rearrange("p (b t) -> p b t", b=NSB)
                qiTs = K.scoped(s3, "qiTs", [128, NSB, 8 * NST], BF16)
                tb = tbank()
                for h in range(8):
                    op(PE, "transpose", out=tb[:, h * NS:(h + 1) * NS], in_=s_qi[0:NS, h, :], identity=idb[0:NS, 0:NS])
                op(ACT, "copy", out=qiTs[:].rearrange("p b (h t) -> p h b t", h=8), in_=tb[:, 0:8 * NS].rearrange("p (h b t) -> p h b t", h=8, b=NSB))
                kinT = K.scoped(s3, "kinT", [64, NS], BF16)
                tb = tbank()
                op(PE, "transpose", out=tb[0:64, 0:NS], in_=s_kib[0:NS, :], identity=idb[0:NS, 0:NS])
                op(ACT, "copy", out=kinT[:, :], in_=tb[0:64, 0:NS])
                wcol = K.scoped(s3, "wcol", [32, NSB], F32)
                dma(SP, out=wi_d[:, :], in_=s_wi[:, :])
                for h in range(8):
                    dma(SP, out=wcol[h * 4:(h + 1) * 4, :], in_=wi_d[:, h].rearrange("(b t) -> t b", b=NSB),
                        allow_slow_non_contiguous=True)
                ptc = K.scoped(s3, "ptc", [128, NSB], I32)
                for b in range(NSB):
                    dma(SP, out=ptc[:, b:b + 1], in_=p_t[b, :].rearrange("(j o) -> j o", o=1), allow_slow_non_contiguous=True)
                ptri = K.scoped(s3, "ptri", [128, NSB * 128], I32)
                dma(SP, out=ptri[:, :], in_=p_t[:, :].rearrange("(o b) j -> o (b j)", o=1).broadcast_to([128, NSB * 128]))
                ptrf = K.scoped(s3, "ptrf", [128, NSB, 128], F32)
                op(DVE, "tensor_copy", out=ptrf[:].rearrange("p b j -> p (b j)"), in_=ptri[:, :])
                Wp = K.scoped(s3, "Wp", [32, 32, 128], BF16)
                pg = K.scoped(s3, "pg", [128, 8192], F32)
                pgb = K.scoped(s3, "pgb", [128, 8192], BF16)
                kTs = K.scoped(s3, "kTs", [128, 64, 128], BF16)
                Rs = [K.scoped(s3, f"Rs{i}", [32, 512], BF16) for i in range(2)]
                scw = K.scoped(s3, "scw", [128, 512], F32)
                candv = K.scoped(s3, "candv", [128, NSB, 36], F32)
                candi = K.scoped(s3, "candi", [128, NSB, 32], U32)
                rsx = 0
                for b in range(NSB):
                    op(DVE, "tensor_scalar", out=Wp[:].rearrange("p s m -> p (s m)"), in0=p0[:, :], scalar1=wcol[:, b:b + 1],
                       scalar2=None, op0=ALU.mult)
                    dma(POOL, out=pg[:, :], in_=c_i[:, :].rearrange("(n s) d -> n (s d)", s=128), indirect=ptc[:, b:b + 1],
                        in_offset=bass.IndirectOffsetOnAxis(ap=ptc[:, b:b + 1], axis=0))
                    op(DVE, "tensor_copy", out=pgb[:, :], in_=pg[:, :])
                    for s0 in range(0, 64, 8):
                        tb = tbank()
                        for s_ in range(8):
                            op(PE, "transpose", out=tb[:, s_ * 128:(s_ + 1) * 128],
                               in_=pgb[:, (s0 + s_) * 128:(s0 + s_ + 1) * 128], identity=idb[:, :])
                        op(ACT, "copy", out=kTs[:, s0:s0 + 8, :], in_=tb[:, :].rearrange("p (s j) -> p s j", s=8))
                    accb = OB[0]
                    for seg in range(32):
                        gq, e = seg // 2, seg % 2
                        g = gbank()
                        op(PE, "matmul", out=g[0:32, :], lhsT=qiTs[e * 64:(e + 1) * 64, b, :],
                           rhs=kTs[e * 64:(e + 1) * 64, gq * 4:(gq + 1) * 4, :], start=True, stop=True)
                        rsx = (rsx + 1) % 2
                        op(ACT, "activation", out=Rs[rsx][:, :], in_=g[0:32, :], func=AF.Relu)
                        op(PE, "matmul", out=accb[:, :], lhsT=Wp[:, seg, :], rhs=Rs[rsx][:, :], start=(seg == 0), stop=(seg == 31))
                    op(ACT, "copy", out=scw[:, :], in_=accb[:, :])
                    g = gbank()
                    op(PE, "matmul", out=g[0:32, 0:NST], lhsT=qiTs[0:64, b, :],
                       rhs=kinT[:, b * NST:(b + 1) * NST], start=True, stop=True)
                    rsx = (rsx + 1) % 2
                    op(ACT, "activation", out=Rs[rsx][:, 0:NST], in_=g[0:32, 0:NST], func=AF.Relu)
                    g2 = gbank()
                    op(PE, "matmul", out=g2[:, 0:NST], lhsT=Wp[:, 0, :], rhs=Rs[rsx][:, 0:NST], start=True, stop=True)
                    op(DVE, "tensor_tensor", out=candv[:, b, 32:36], in0=g2[:, 0:NST], in1=negnew, op=ALU.add)
                    for r in range(CANDR):
                        op(DVE, "max", out=candv[:, b, r * 8:(r + 1) * 8], in_=scw[:, :])
                        op(DVE, "max_index", out=candi[:, b, r * 8:(r + 1) * 8], in_max=candv[:, b, r * 8:(r + 1) * 8], in_values=scw[:, :])
                        if r + 1 < CANDR:
                            op(DVE, "match_replace", out=scw[:, :], in_to_replace=candv[:, b, r * 8:(r + 1) * 8], in_values=scw[:, :],
                               imm_value=NEG)
                thr = K.scoped(s3, "thr", [128, NSB, 1], F32)
                cmpb = K.scoped(s3, "cmpb", [128, NSB, 36], F32)
                cnt = K.scoped(s3, "cnt", [128, NSB], F32)
                cntb = K.scoped(s3, "cntb", [128, NSB], BF16)
                stp = K.scoped(s3, "stp", [128, NSB], F32)
                op(DVE, "memset", ap=thr[:], constant=0.0)
                for k in range(1, NBS + 1):
                    dlt = BR * (2.0 ** -k)
                    op(DVE, "tensor_tensor", out=cmpb[:], in0=candv[:], in1=thr[:].broadcast_to([128, NSB, 36]), op=ALU.is_ge)
                    op(DVE, "tensor_reduce", out=cnt[:, :], in_=cmpb[:], axis=AX.X, op=ALU.add)
                    op(DVE, "tensor_copy", out=cntb[:, :], in_=cnt[:, :])
                    g = gbank()
                    op(PE, "matmul", out=g[:, 0:NSB], lhsT=blk[:, :], rhs=cntb[:, :], start=True, stop=True)
                    op(DVE, "tensor_scalar", out=stp[:, :], in0=g[:, 0:NSB], scalar1=255.5, scalar2=2.0 * dlt, op0=ALU.is_ge, op1=ALU.mult)
                    op(DVE, "scalar_tensor_tensor", out=thr[:, :, 0], in0=thr[:, :, 0], scalar=-dlt, in1=stp[:, :], op0=ALU.add, op1=ALU.add)
                op(DVE, "tensor_scalar", out=thr[:, :, 0], in0=thr[:, :, 0], scalar1=-BR * (2.0 ** -NBS), scalar2=None, op0=ALU.add)
                sel = K.scoped(s3, "sel", [128, NSB, 36], F32)
                op(DVE, "tensor_tensor", out=sel[:], in0=candv[:], in1=thr[:].broadcast_to([128, NSB, 36]), op=ALU.is_ge)
                idxf = K.scoped(s3, "idxf", [128, NSB, 32], F32)
                spl = K.scoped(s3, "spl", [128, NSB, 32], F32)
                tmp3 = K.scoped(s3, "tmp3", [128, NSB, 32], F32)
                pgf = K.scoped(s3, "pgf", [128, NSB, 32], F32)
                op(DVE, "tensor_copy", out=idxf[:], in_=candi[:])
                op(DVE, "tensor_scalar", out=spl[:], in0=idxf[:], scalar1=127.5, scalar2=None, op0=ALU.is_ge)
                for thv in (255.5, 383.5):
                    op(DVE, "tensor_scalar", out=tmp3[:], in0=idxf[:], scalar1=thv, scalar2=None, op0=ALU.is_ge)
                    op(DVE, "tensor_tensor", out=spl[:], in0=spl[:], in1=tmp3[:], op=ALU.add)
                op(DVE, "scalar_tensor_tensor", out=pgf[:], in0=spl[:], scalar=-128.0, in1=idxf[:], op0=ALU.mult, op1=ALU.add)
                oh = K.scoped(s3, "oh", [128, 32, 128], F32)
                phys = K.scoped(s3, "phys", [128, NSB, 32], F32)
                for b in range(NSB):
                    op(DVE, "tensor_tensor", out=oh[:], in0=pgf[:, b, :].unsqueeze(2).broadcast_to([128, 32, 128]),
                       in1=iota_pg.unsqueeze(1).broadcast_to([128, 32, 128]), op=ALU.is_equal)
                    op(DVE, "tensor_tensor", out=oh[:], in0=oh[:], in1=ptrf[:, b, :].unsqueeze(1).broadcast_to([128, 32, 128]), op=ALU.mult)
                    op(DVE, "tensor_reduce", out=phys[:, b, :], in_=oh[:], axis=AX.X, op=ALU.add)
                rowf = K.scoped(s3, "rowf", [128, NSB, 32], F32)
                op(DVE, "tensor_scalar", out=rowf[:], in0=phys[:], scalar1=128.0, scalar2=slotbase, op0=ALU.mult, op1=ALU.add)
                op(DVE, "scalar_tensor_tensor", out=rowf[:], in0=spl[:], scalar=2.0, in1=rowf[:], op0=ALU.mult, op1=ALU.add)
                BIG = 1.0e9
                op(DVE, "tensor_scalar", out=tmp3[:], in0=sel[:, :, 0:32], scalar1=-BIG, scalar2=BIG, op0=ALU.mult, op1=ALU.add)
                op(DVE, "tensor_tensor", out=rowf[:], in0=rowf[:], in1=sel[:, :, 0:32], op=ALU.mult)
                op(DVE, "tensor_tensor", out=rowf[:], in0=rowf[:], in1=tmp3[:], op=ALU.add)
                rowi = K.scoped(s3, "rowi", [128, NSB, 32], I32)
                op(DVE, "tensor_copy", out=rowi[:], in_=rowf[:])
                CH = 8
                Kg = K.scoped(s3, "Kg", [128, CH, 512], F32)
                Vg = K.scoped(s3, "Vg", [128, CH, 512], F32)
                prod = K.scoped(s3, "prod", [128, CH, 512], F32)
                qsb = K.scoped(s3, "qsb", [128, 512], F32)
                S_ = K.scoped(s3, "S_", [128, CH, 8], F32)
                Pm = K.scoped(s3, "Pm", [128, CH, 8], F32)
                Oacc = K.scoped(s3, "Oacc", [128, 512], F32)
                Opart = K.scoped(s3, "Opart", [128, 512], F32)
                racc = K.scoped(s3, "racc", [128, 8], F32)
                rpart = K.scoped(s3, "rpart", [128, 8], F32)
                op(DVE, "memset", ap=Kg[:], constant=0.0)
                op(DVE, "memset", ap=Vg[:], constant=0.0)
                nrows = n_phys * 128
                bc_reg = nc.gpsimd.to_reg(nrows - 1)
                for b in range(NSB):
                    g = gbank()
                    op(PE, "matmul", out=g[:, :], lhsT=selq[:, b, :], rhs=s_q[:, :], start=True, stop=True)
                    op(ACT, "copy", out=qsb[:, :], in_=g[:, :])
                    op(DVE, "memset", ap=Oacc[:], constant=0.0)
                    op(DVE, "memset", ap=racc[:], constant=0.0)
                    chunks = [(c0, CH, False) for c0 in range(0, 32, CH)] + [(32, NST, True)]
                    for (c0, w_, isnew) in chunks:
                        if isnew:
                            dma(SP, out=Kg[:, 0:NST, :].rearrange("p t f -> p (t f)"),
                                in_=k_s[b * NST:(b + 1) * NST, :].rearrange("(o t) f -> o (t f)", o=1).broadcast_to([128, NST * 512]))
                            dma(SP, out=Vg[:, 0:NST, :].rearrange("p t f -> p (t f)"),
                                in_=v_s[b * NST:(b + 1) * NST, :].rearrange("(o t) f -> o (t f)", o=1).broadcast_to([128, NST * 512]))
                        else:
                            for kk in range(w_):
                                for (dst, src) in ((Kg, c_k), (Vg, c_v)):
                                    dma(POOL, out=dst[:, kk, :], in_=src[:, :], indirect=rowi[:, b, c0 + kk:c0 + kk + 1],
                                        in_offset=bass.IndirectOffsetOnAxis(ap=rowi[:, b, c0 + kk:c0 + kk + 1], axis=0),
                                        bounds_check=bc_reg, oob_is_err=False)
                        selc = sel[:, b, c0:c0 + w_]
                        op(DVE, "tensor_tensor", out=prod[:, 0:w_, :], in0=Kg[:, 0:w_, :],
                           in1=qsb[:, :].unsqueeze(1).broadcast_to([128, w_, 512]), op=ALU.mult)
                        op(DVE, "tensor_reduce", out=S_[:, 0:w_, :], in_=prod[:, 0:w_, :].rearrange("p k (h d) -> p k h d", h=8),
                           axis=AX.X, op=ALU.add)
                        op(DVE, "tensor_tensor", out=S_[:, 0:w_, :], in0=S_[:, 0:w_, :],
                           in1=selc.unsqueeze(2).broadcast_to([128, w_, 8]), op=ALU.mult)
                        op(ACT, "activation", out=Pm[:, 0:w_, :], in_=S_[:, 0:w_, :], func=AF.Exp, scale=0.125)
                        op(DVE, "tensor_tensor", out=Pm[:, 0:w_, :], in0=Pm[:, 0:w_, :],
                           in1=selc.unsqueeze(2).broadcast_to([128, w_, 8]), op=ALU.mult)
                        op(DVE, "tensor_tensor", out=prod[:, 0:w_, :].rearrange("p k (h d) -> p k h d", h=8),
                           in0=Vg[:, 0:w_, :].rearrange("p k (h d) -> p k h d", h=8),
                           in1=Pm[:, 0:w_, :].unsqueeze(3).broadcast_to([128, w_, 8, 64]), op=ALU.mult)
                        op(DVE, "tensor_reduce", out=Opart[:, :], in_=prod[:, 0:w_, :].rearrange("p k f -> p f k"), axis=AX.X, op=ALU.add)
                        op(DVE, "tensor_tensor", out=Oacc[:, :], in0=Oacc[:, :], in1=Opart[:, :], op=ALU.add)
                        op(DVE, "tensor_reduce", out=rpart[:, :], in_=Pm[:, 0:w_, :].rearrange("p k h -> p h k"), axis=AX.X, op=ALU.add)
                        op(DVE, "tensor_tensor", out=racc[:, :], in0=racc[:, :], in1=rpart[:, :], op=ALU.add)
                    op(PE, "matmul", out=OB[0][0:NS, :], lhsT=agg[:, b, :], rhs=Oacc[:, :], start=(b == 0), stop=(b == NSB - 1))
                    op(PE, "matmul", out=OB[1][0:NS, 0:8], lhsT=agg[:, b, :], rhs=racc[:, :], start=(b == 0), stop=(b == NSB - 1))
                rcs = K.scoped(s3, "rcs", [NS, 8, 1], F32)
                atts = K.scoped(s3, "atts", [NS, 8, 64], BF16)
                op(DVE, "reciprocal", out=rcs[:, :, 0], in_=OB[1][0:NS, 0:8])
                op(DVE, "tensor_tensor", out=atts[:], in0=OB[0][0:NS, :].rearrange("p (h d) -> p h d", h=8),
                   in1=rcs[:].broadcast_to([NS, 8, 64]), op=ALU.mult)
                transpose_cols(atts[:].rearrange("p h d -> p (h d)"), NS, 512, lambda c: s_attT[:, c, 0:NS])
            K.barrier()

        with contextlib.ExitStack() as s4:
            NB0 = 2120
            winb = K.scoped(s4, "win_b", [128, 8, 3072], BF16)
            for kc in range(8):
                for c0 in range(0, 3072, 1536):
                    dma(POOL, out=winb[:, kc, c0:c0 + 1536], in_=w_in[kc * 128:(kc + 1) * 128, NB0 + c0:NB0 + c0 + 1536])
            wpa = K.scoped(s4, "wpa", [128, 4, D], BF16)
            wpb = K.scoped(s4, "wpb", [128, 4, D], BF16)
            wo = K.scoped(s4, "wo", [128, 8, D], BF16)
            wabd = K.scoped(s4, "wabd", [128, 4, 128], BF16)
            wxbd = K.scoped(s4, "wxbd", [128, 4, 128], BF16)
            dma(POOL, out=wpa[:], in_=w_pa[:].rearrange("(c p) n -> p c n", p=128))
            dma(POOL, out=wpb[:], in_=w_pb[:].rearrange("(c p) n -> p c n", p=128))
            for kc in range(8):
                dma(POOL, out=wo[:, kc, :], in_=w_o[kc * 128:(kc + 1) * 128, :])
            op(DVE, "memset", ap=wabd[:], constant=0.0)
            op(DVE, "memset", ap=wxbd[:], constant=0.0)
            for blk_ in range(8):
                c, j = blk_ // 2, blk_ % 2
                dma(POOL, out=wabd[j * 64:(j + 1) * 64, c, j * 64:(j + 1) * 64], in_=l_wa[blk_])
                dma(POOL, out=wxbd[j * 64:(j + 1) * 64, c, j * 64:(j + 1) * 64], in_=l_wx[blk_])
            hst = K.scoped(s4, "hst", [128, 4], F32)
            xexts = [K.scoped(s4, f"xext{i}", [128, 4, 131], F32) for i in range(2)]
            exts = K.scoped(s4, "exts", [128, 4, NSB, 7], F32)
            h0T = K.scoped(s4, "h0T", [128, 4, NSB], F32)
            hls = K.scoped(s4, "hls", [128, 4, NSB], F32)
            xtbs = [K.scoped(s4, f"xtb{i}", [128, D], F32) for i in range(2)]
            ggs = [K.scoped(s4, f"gg{i}", [128, 4, 128], BF16) for i in range(2)]
            sgas = [K.scoped(s4, f"sga{i}", [128, 8, 128], BF16) for i in range(2)]
            sgbs = [K.scoped(s4, f"sgb{i}", [128, 8, 128], BF16) for i in range(2)]
            xc = K.scoped(s4, "xc", [128, 4, 128], F32)
            xcb = K.scoped(s4, "xcb", [128, 4, 128], BF16)
            ra = K.scoped(s4, "ra", [128, 4, 128], F32)
            ih = K.scoped(s4, "ih", [128, 4, 128], F32)
            mu = K.scoped(s4, "mu", [128, 4, 128], F32)
            hl = K.scoped(s4, "hl", [128, 4, 128], F32)
            ybi = K.scoped(s4, "ybi", [128, 4, 128], BF16)
            mg = K.scoped(s4, "mg", [128, 8, 128], BF16)
            t1 = K.scoped(s4, "t1", [128, 4, 128], F32)
            t2 = K.scoped(s4, "t2", [128, 4, 128], F32)
            op(DVE, "memset", ap=hst[:], constant=0.0)
            op(DVE, "memset", ap=xexts[0][:], constant=0.0)
            op(DVE, "memset", ap=xexts[1][:], constant=0.0)

            def front_b(n, xsrc, is_prompt, par):
                xt_, gg, sga, sgb, xext = xtbs[par], ggs[par], sgas[par], sgbs[par], xexts[par]
                dma(SP, out=xt_[0:n, :], in_=xsrc)
                rmsnorm_to_bf(xt_, n, gbc, hb)
                transpose_cols(hb, n, D, lambda c: hT[:, c, 0:n])
                yield

                def fm_bank(col0, nchunk):
                    g = gbank()
                    for cc in range(nchunk):
                        for kc in range(8):
                            op(PE, "matmul", out=g[:, cc * n:(cc + 1) * n],
                               lhsT=winb[:, kc, col0 + cc * 128:col0 + (cc + 1) * 128],
                               rhs=hT[:, kc, 0:n], start=(kc == 0), stop=(kc == 7))
                    return g[:, 0:nchunk * n].rearrange("p (c n) -> p c n", c=nchunk)

                gv = fm_bank(0, 4)
                if is_prompt:
                    op(POOL, "tensor_copy", out=xext[:, :, 0:3], in_=xexts[1 - par][:, :, 128:131])
                    op(ACT, "copy", out=xext[:, :, 3:3 + n], in_=gv)
                else:
                    for c in range(4):
                        op(ACT, "copy", out=exts[:, c, :, 3:7], in_=gv[:, c, :].rearrange("p (b t) -> p b t", b=NSB))
                yield
                gv = fm_bank(512, 4)
                op(ACT, "activation", out=gg[:, :, 0:n], in_=gv, func=AF.Gelu_apprx_tanh)
                yield
                for hh in range(2):
                    gv = fm_bank(1024 + hh * 512, 4)
                    op(ACT, "activation", out=sga[:, hh * 4:hh * 4 + 4, 0:n], in_=gv, func=AF.Sigmoid)
                    yield
                for hh in range(2):
                    gv = fm_bank(2048 + hh * 512, 4)
                    op(ACT, "activation", out=sgb[:, hh * 4:hh * 4 + 4, 0:n], in_=gv, func=AF.Sigmoid)
                    yield

            def mixer_back(n, attT_fn, conv_fn, scan_fn, x1row0, par):
                xt_, gg, sga, sgb = xtbs[par], ggs[par], sgas[par], sgbs[par]
                conv_fn()
                op(POOL, "tensor_copy", out=xcb[:, :, 0:n], in_=xc[:, :, 0:n])
                yield
                gb_ = gbank()
                for c in range(4):
                    op(PE, "matmul", out=gb_[:, c * n:(c + 1) * n], lhsT=wabd[:, c, :], rhs=xcb[:, c, 0:n], start=True, stop=True)
                for c in range(4):
                    op(ACT, "activation", out=ra[:, c, 0:n], in_=gb_[:, c * n:(c + 1) * n], func=AF.Sigmoid, bias=lba[:, c:c + 1])
                gb2 = gbank()
                for c in range(4):
                    op(PE, "matmul", out=gb2[:, c * n:(c + 1) * n], lhsT=wxbd[:, c, :], rhs=xcb[:, c, 0:n], start=True, stop=True)
                for c in range(4):
                    op(ACT, "activation", out=ih[:, c, 0:n], in_=gb2[:, c * n:(c + 1) * n], func=AF.Sigmoid, bias=lbx[:, c:c + 1])
                yield
                for c in range(4):
                    op(ACT, "activation", out=ra[:, c, 0:n], in_=ra[:, c, 0:n], func=AF.Exp, scale=lcl[:, c:c + 1])
                yield
                op(DVE, "tensor_tensor", out=mu[:, :, 0:n], in0=ra[:, :, 0:n], in1=ra[:, :, 0:n], op=ALU.mult)
                op(DVE, "tensor_scalar", out=mu[:, :, 0:n], in0=mu[:, :, 0:n], scalar1=-1.0, scalar2=1.0, op0=ALU.mult, op1=ALU.add)
                op(DVE, "tensor_scalar", out=mu[:, :, 0:n], in0=mu[:, :, 0:n], scalar1=0.0, scalar2=None, op0=ALU.max)
                yield
                op(ACT, "activation", out=mu[:, :, 0:n], in_=mu[:, :, 0:n], func=AF.Sqrt)
                yield
                op(DVE, "tensor_tensor", out=mu[:, :, 0:n], in0=mu[:, :, 0:n], in1=ih[:, :, 0:n], op=ALU.mult)
                op(DVE, "tensor_tensor", out=mu[:, :, 0:n], in0=mu[:, :, 0:n], in1=xc[:, :, 0:n], op=ALU.mult)
                yield
                scan_fn()
                yield
                op(DVE, "tensor_tensor", out=ybi[:, :, 0:n], in0=hl[:, :, 0:n], in1=gg[:, :, 0:n], op=ALU.mult)
                yield
                for half in range(2):
                    ga_ = gbank()
                    for cc in range(4):
                        col = half * 4 + cc
                        for c in range(4):
                            op(PE, "matmul", out=ga_[:, cc * n:(cc + 1) * n], lhsT=wpa[:, c, col * 128:(col + 1) * 128],
                               rhs=attT_fn(c), start=(c == 0), stop=(c == 3))
                    op(DVE, "tensor_tensor", out=t1[:, :, 0:n], in0=ga_[:, 0:4 * n].rearrange("p (c n) -> p c n", c=4),
                       in1=sga[:, half * 4:half * 4 + 4, 0:n], op=ALU.mult)
                    gb3 = gbank()
                    for cc in range(4):
                        col = half * 4 + cc
                        for c in range(4):
                            op(PE, "matmul", out=gb3[:, cc * n:(cc + 1) * n], lhsT=wpb[:, c, col * 128:(col + 1) * 128],
                               rhs=ybi[:, c, 0:n], start=(c == 0), stop=(c == 3))
                    op(DVE, "tensor_tensor", out=t2[:, :, 0:n], in0=gb3[:, 0:4 * n].rearrange("p (c n) -> p c n", c=4),
                       in1=sgb[:, half * 4:half * 4 + 4, 0:n], op=ALU.mult)
                    yield
                    op(DVE, "tensor_tensor", out=mg[:, half * 4:half * 4 + 4, 0:n], in0=t1[:, :, 0:n], in1=t2[:, :, 0:n], op=ALU.add)
                    yield
                for half in range(2):
                    ob = OB[half]
                    for kc in range(8):
                        op(PE, "matmul", out=ob[0:n, :], lhsT=mg[:, kc, 0:n], rhs=wo[:, kc, half * 512:(half + 1) * 512],
                           start=(kc == 0), stop=(kc == 7))
                    yield
                    op(DVE, "tensor_tensor", out=xt_[0:n, half * 512:(half + 1) * 512], in0=ob[0:n, :],
                       in1=xt_[0:n, half * 512:(half + 1) * 512], op=ALU.add)
                dma(SP, out=x1_d[x1row0:x1row0 + n, :], in_=xt_[0:n, :])
                yield

            def stageF(i):
                yield from front_b(128, x_p[i * 128:(i + 1) * 128, :], True, i % 2)

            def stageM(i):
                par = i % 2
                xext = xexts[par]

                def conv_fn():
                    for c in range(4):
                        op(DVE, "tensor_scalar", out=xc[:, c, :], in0=xext[:, c, 0:128], scalar1=lcw[:, 0, c:c + 1],
                           scalar2=lcb[:, c:c + 1], op0=ALU.mult, op1=ALU.add)
                        for j in range(1, 4):
                            op(DVE, "scalar_tensor_tensor", out=xc[:, c, :], in0=xext[:, c, j:j + 128], scalar=lcw[:, j, c:c + 1],
                               in1=xc[:, c, :], op0=ALU.mult, op1=ALU.add)

                def scan_fn():
                    for c in range(4):
                        op(DVE, "tensor_tensor_scan", out=hl[:, c, :], data0=ra[:, c, :], data1=mu[:, c, :],
                           initial=hst[:, c:c + 1], op0=ALU.mult, op1=ALU.add)
                    op(DVE, "tensor_copy", out=hst[:, :], in_=hl[:, :, 127])

                yield from mixer_back(128, lambda c: attA[:, c, i * 128:(i + 1) * 128], conv_fn, scan_fn, i * 128, par)
                if i == NT - 1:
                    for c in range(4):
                        dma_cols(lc_p, c, xext[:, c, 128:131], False)
                    dma(SP, out=lh_p[:].rearrange("(c p) -> p c", p=128), in_=hst[:, :], allow_slow_non_contiguous=True)

            stages = [stageF, stageM]
            for step in range(NT + len(stages) - 1):
                alive = [stages[s](step - s) for s in range(len(stages)) if 0 <= step - s < NT]
                while alive:
                    nxt = []
                    for gnr in alive:
                        try:
                            next(gnr)
                            nxt.append(gnr)
                        except StopIteration:
                            pass
                    alive = nxt

            if do_sample:
                for c in range(4):
                    for b in range(NSB):
                        dma_cols(s_lc[b], c, exts[:, c, b, 0:3], True)
                    dma_cols(s_lh, c, h0T[:, c, :], True)
                for _ in front_b(NS, x_s[:, :], False, 0):
                    pass

                def conv_s():
                    for c in range(4):
                        xv = xc[:, c, 0:NS].rearrange("p (b t) -> p b t", b=NSB)
                        op(DVE, "tensor_scalar", out=xv, in0=exts[:, c, :, 0:NST], scalar1=lcw[:, 0, c:c + 1],
                           scalar2=lcb[:, c:c + 1], op0=ALU.mult, op1=ALU.add)
                        for j in range(1, 4):
                            op(DVE, "scalar_tensor_tensor", out=xv, in0=exts[:, c, :, j:j + NST], scalar=lcw[:, j, c:c + 1],
                               in1=xv, op0=ALU.mult, op1=ALU.add)

                def scan_s():
                    for c in range(4):
                        for b in range(NSB):
                            op(DVE, "tensor_tensor_scan", out=hl[:, c, b * NST:(b + 1) * NST], data0=ra[:, c, b * NST:(b + 1) * NST],
                               data1=mu[:, c, b * NST:(b + 1) * NST], initial=h0T[:, c, b:b + 1], op0=ALU.mult, op1=ALU.add)
                    op(DVE, "tensor_copy", out=hls[:, :, :], in_=hl[:, :, 0:NS].rearrange("p c (b t) -> p c b t", b=NSB)[:, :, :, NST - 1])

                for _ in mixer_back(NS, lambda c: s_attT[:, c, 0:NS], conv_s, scan_s, SEQ, 0):
                    pass
                for c in range(4):
                    for b in range(NSB):
                        dma_cols(lc_s[b], c, exts[:, c, b, 4:7], False)
                    dma_cols(lh_s, c, hls[:, c, :], False)
        K.barrier()

    if do_ffn:
        with contextlib.ExitStack() as s2:
            wup = K.scoped(s2, "wup", [128, 8, 2 * DFF], BF16)
            wdn = K.scoped(s2, "wdn", [128, 24, D], BF16)
            for kc in range(8):
                for c0 in range(0, 2 * DFF, 2048):
                    dma(POOL, out=wup[:, kc, c0:c0 + 2048], in_=w_up[kc * 128:(kc + 1) * 128, c0:c0 + 2048])
            for c in range(24):
                dma(POOL, out=wdn[:, c, :], in_=w_dn[c * 128:(c + 1) * 128, :])
            dma(SP, out=gbc[:], in_=g_ffn[:].broadcast_to([128, D]))
            gbc2 = K.scoped(s2, "gbc2", [128, D], F32)
            dma(SP, out=gbc2[:], in_=g_fin[:].broadcast_to([128, D]))
            x1ts = [K.scoped(s2, f"x1t{i}", [128, D], F32) for i in range(2)]
            h2 = K.scoped(s2, "h2", [128, D], BF16)
            h2Ts = [K.scoped(s2, f"h2T{i}", [128, 8, 128], BF16) for i in range(2)]
            uexl = [K.scoped(s2, f"uex{c}", [128, 130], F32) for c in range(24)]
            uexsl = [K.scoped(s2, f"uexs{c}", [128, NSB, 6], F32) for c in range(24)]
            accs = [K.scoped(s2, f"facc{i}", [128, 128], F32) for i in range(4)]
            gls = [K.scoped(s2, f"fgl{i}", [128, 128], F32) for i in range(4)]
            gTs = [K.scoped(s2, f"gT{i}", [128, 24, 128], BF16) for i in range(2)]
            x2 = K.scoped(s2, "x2", [128, D], F32)
            for c in range(24):
                op(POOL, "memset", ap=uexl[c][:, 0:2], constant=0.0)
            rf = {"a": 0}

            def prep(n, row0, par):
                x1t, h2T = x1ts[par], h2Ts[par]
                dma(SP, out=x1t[0:n, :], in_=x1_d[row0:row0 + n, :])
                rmsnorm_to_bf(x1t, n, gbc, h2)
                transpose_cols(h2, n, D, lambda c: h2T[:, c, 0:n])

            ubs = [K.scoped(s2, f"ubs{i}", [128, 128], BF16) for i in range(6)]

            def up_chunks(n, par, prompt):
                h2T, gT = h2Ts[par], gTs[par]
                banks = {}

                def s1(c):
                    g = gbank()
                    for part in range(2):
                        for kc in range(8):
                            op(PE, "matmul", out=g[:, part * n:(part + 1) * n],
                               lhsT=wup[:, kc, part * DFF + c * 128: part * DFF + (c + 1) * 128],
                               rhs=h2T[:, kc, 0:n], start=(kc == 0), stop=(kc == 7))
                    banks[c] = g

                def s2(c):
                    g = banks.pop(c)
                    if prompt:
                        op(ACT, "copy", out=uexl[c][:, 2:2 + n], in_=g[:, 0:n])
                    else:
                        op(ACT, "copy", out=uexsl[c][:, :, 2:6], in_=g[:, 0:n].rearrange("p (b t) -> p b t", b=NSB))
                    op(ACT, "copy", out=ubs[c % 6][:, 0:n], in_=g[:, n:2 * n])

                def s3(c):
                    acc = accs[c % 4]
                    if prompt:
                        ue = uexl[c]
                        taps = [ue[:, j:j + n] for j in range(3)]
                        av = acc[:, 0:n]
                    else:
                        ue = uexsl[c]
                        taps = [ue[:, :, j:j + NST] for j in range(3)]
                        av = acc[:, 0:n].rearrange("p (b t) -> p b t", b=NSB)
                    op(ACT, "activation", out=av, in_=taps[0], func=AF.Identity, scale=fcw[:, 0, c:c + 1], bias=fcb[:, c:c + 1])
                    for j in (1, 2):
                        op(DVE, "scalar_tensor_tensor", out=av, in0=taps[j], scalar=fcw[:, j, c:c + 1], in1=av,
                           op0=ALU.mult, op1=ALU.add)
                    if prompt:
                        op(POOL, "tensor_copy", out=ue[:, 0:2], in_=ue[:, 128:130])

                def s4(c):
                    op(ACT, "activation", out=gls[c % 4][:, 0:n], in_=accs[c % 4][:, 0:n], func=AF.Gelu_apprx_tanh)

                def s5(c):
                    op(POOL, "tensor_tensor", out=gT[:, c, 0:n], in0=ubs[c % 6][:, 0:n], in1=gls[c % 4][:, 0:n], op=ALU.mult)

                stages_ = [s1, s2, s3, s4, s5]
                for k in range(24 + 4):
                    for si, fn in enumerate(stages_):
                        c = k - si
                        if 0 <= c < 24:
                            fn(c)

            def down_final(n, par, yout):
                x1t, gT = x1ts[par], gTs[par]
                for half in range(2):
                    ob = OB[half]
                    for c in range(24):
                        op(PE, "matmul", out=ob[0:n, :], lhsT=gT[:, c, 0:n], rhs=wdn[:, c, half * 512:(half + 1) * 512],
                           start=(c == 0), stop=(c == 23))
                    op(DVE, "tensor_tensor", out=x2[0:n, half * 512:(half + 1) * 512], in0=ob[0:n, :],
                       in1=x1t[0:n, half * 512:(half + 1) * 512], op=ALU.add)
                rmsnorm_to_bf(x2, n, gbc2, x2, junk=h2)
                dma(SP, out=yout, in_=x2[0:n, :])

            prep(128, 0, 0)
            for i in range(NT):
                par = i % 2
                up_chunks(128, par, True)
                if i == NT - 1:
                    grp = K.new_group()
                    for c in range(24):
                        dma_cols(fc_p, c, uexl[c][:, 0:2], False, group=grp)
                    K.close_group(grp)
                    if do_sample:
                        grp = K.new_group()
                        for b in range(NSB):
                            for c in range(24):
                                dma_cols(s_fc[b], c, uexsl[c][:, b, 0:2], True, group=grp)
                        K.close_group(grp)
                        prep(NS, SEQ, 1 - par)
                else:
                    prep(128, (i + 1) * 128, 1 - par)
                down_final(128, par, y_p[i * 128:(i + 1) * 128, :])
            if do_sample:
                par = NT % 2
                up_chunks(NS, par, False)
                down_final(NS, par, y_s[:, :])
                grp = K.new_group()
                for b in range(NSB):
                    for c in range(24):
                        dma_cols(fc_s[b], c, uexsl[c][:, b, 4:6], False, group=grp)
                K.close_group(grp)
    K.finish()
    return nc


def _host_consts():
    inv = np.power(np.float32(10000.0), -np.arange(32, dtype=np.float32) / np.float32(32)).astype(np.float32)

    def tab(pos):
        ang = pos.astype(np.float32)[:, None] * inv[None, :]
        return np.cos(ang).astype(np.float32), np.sin(ang).astype(np.float32)

    cp, sp_ = tab(np.arange(SEQ))
    pos_s = np.tile(PAST + np.arange(NST), NSB)
    cs, ss_ = tab(pos_s)
    q = np.arange(128)
    cmask = np.where(q[None, :] <= q[:, None], 0.0, NEG).astype(np.float32)
    pow2 = np.tile((2.0 ** -np.arange(NBIS + 2, dtype=np.float64)).astype(np.float32)[None, :], (128, 1))
    blk = (q[:, None] // 32 == q[None, :] // 32).astype(np.float32).astype(ml_dtypes.bfloat16)
    misc = np.zeros((128, 1024), np.float32)
    seg, tq = q % 32, q // 32
    misc[:, 0] = 8 * (seg // 2) + (seg % 2)
    tp = np.arange(4)
    misc[:, 1:5] = np.where((seg[:, None] == 0) & (tp[None, :] <= tq[:, None]), 0.0, NEG)
    misc[:, 16:144] = np.arange(128)[None, :]
    agg = np.zeros((128, NSB, NS), np.float32)
    selq = np.zeros((NS, NSB, 128), np.float32)
    for b in range(NSB):
        agg[q, b, NST * b + tq] = 1.0
        selq[NST * b + tq, b, q] = 1.0
    misc[:, 160:160 + NSB * NS] = agg.reshape(128, -1)
    p0 = np.zeros((8, 4, 32, 128), np.float32)
    for t in range(4):
        for s_ in range(32):
            p0[:, t, s_, t * 32 + s_] = 1.0
    p0 = p0.reshape(32, 32 * 128).astype(ml_dtypes.bfloat16)
    return {
        "c_ident_bf": np.eye(128, dtype=np.float32).astype(ml_dtypes.bfloat16),
        "c_ident_f": np.eye(128, dtype=np.float32),
        "c_cos_p": cp, "c_sin_p": sp_, "c_cos_s": cs, "c_sin_s": ss_,
        "c_cmask": cmask, "c_pow2": pow2, "c_misc": misc, "c_blk": blk,
        "c_p0": p0, "c_selq": selq.reshape(NS, NSB * 128),
    }


_OUT_NAMES = ["y_prompt", "y_sample", "k_prompt", "v_prompt", "kidx_prompt", "lru_conv_prompt", "lru_h_prompt",
              "ffn_conv_prompt", "k_sample", "v_sample", "kidx_sample", "lru_conv_sample", "lru_h_sample",
              "ffn_conv_sample"]


def make_in_map(inputs, c, consts):
    f = lambda a: np.ascontiguousarray(np.asarray(a))
    n_phys = inputs["cache_k"].shape[1]
    m = {
        "x_prompt": f(inputs["x_prompt"][c]),
        "x_sample": f(inputs["x_sample"][NSB * c:NSB * (c + 1)]).reshape(NS, D),
        "cache_k": np.asarray(inputs["cache_k"]).reshape(n_phys * 128, 512),
        "cache_v": np.asarray(inputs["cache_v"]).reshape(n_phys * 128, 512),
        "cache_kidx": np.asarray(inputs["cache_kidx"]).reshape(n_phys * 128, 64),
        "page_table": f(inputs["page_table"][NSB * c:NSB * (c + 1)]).astype(np.int32),
        "state_lru_conv": f(inputs["state_lru_conv"][0, NSB * c:NSB * (c + 1)]),
        "state_lru_h": f(inputs["state_lru_h"][0, NSB * c:NSB * (c + 1)]),
        "state_ffn_conv": f(inputs["state_ffn_conv"][0, NSB * c:NSB * (c + 1)]),
        "norm_mix_g": f(inputs["norm_mix_g"]).reshape(1, D),
        "w_in": f(inputs["w_in"][0]),
        "lru_conv_w": f(inputs["lru_conv_w"][0]),
        "lru_conv_b": f(inputs["lru_conv_b"][0]),
        "lru_wa": f(inputs["lru_wa"][0]),
        "lru_ba": f(inputs["lru_ba"][0]),
        "lru_wx": f(inputs["lru_wx"][0]),
        "lru_bx": f(inputs["lru_bx"][0]),
        "lru_lambda": f(inputs["lru_lambda"][0]),
        "w_proj_a": f(inputs["w_proj_a"][0]),
        "w_proj_b": f(inputs["w_proj_b"][0]),
        "w_out": f(inputs["w_out"][0]),
        "norm_ffn_g": f(inputs["norm_ffn_g"]).reshape(1, D),
        "w_up": f(inputs["w_up"][0]),
        "ffn_conv_w": f(inputs["ffn_conv_w"][0]),
        "ffn_conv_b": f(inputs["ffn_conv_b"][0]),
        "w_down": f(inputs["w_down"][0]),
        "norm_final_g": f(inputs["norm_final_g"]).reshape(1, D),
    }
    m.update(consts)
    return m


def assemble(results):
    n = len(results)
    g = lambda name: [np.asarray(r[name]) for r in results]
    y_p = np.stack(g("y_prompt"))
    y_s = np.concatenate([a.reshape(NSB, NST, D) for a in g("y_sample")])
    k_p = np.stack([a.reshape(SEQ, 8, 64) for a in g("k_prompt")])[None]
    v_p = np.stack([a.reshape(SEQ, 8, 64) for a in g("v_prompt")])[None]
    ki_p = np.stack(g("kidx_prompt"))[None]
    lc_p = np.stack(g("lru_conv_prompt"))[None]
    lh_p = np.stack([a.reshape(512) for a in g("lru_h_prompt")])[None]
    fc_p = np.stack(g("ffn_conv_prompt"))[None]
    k_s = np.concatenate([a.reshape(NSB, NST, 8, 64) for a in g("k_sample")])[None]
    v_s = np.concatenate([a.reshape(NSB, NST, 8, 64) for a in g("v_sample")])[None]
    ki_s = np.concatenate([a.reshape(NSB, NST, 64) for a in g("kidx_sample")])[None]
    lc_s = np.concatenate(g("lru_conv_sample"))[None]
    lh_s = np.concatenate(g("lru_h_sample"))[None]
    fc_s = np.concatenate(g("ffn_conv_sample"))[None]
    outs = (y_p, y_s, k_p, v_p, ki_p, lc_p, lh_p, fc_p, k_s, v_s, ki_s, lc_s, lh_s, fc_s)
    return tuple(np.ascontiguousarray(o, dtype=np.float32) for o in outs)


def kernel(**inputs):
    n_cores = 8
    n_phys = int(np.asarray(inputs["cache_k"]).shape[1])
    consts = _host_consts()
    nc = build(n_phys)
    in_maps = [make_in_map(inputs, c, consts) for c in range(n_cores)]
    res = run_bass_kernel_spmd(nc, in_maps, core_ids=list(range(n_cores)))
    return assemble(res.results)
```

```python
import numpy as np
import concourse.bass as bass
import concourse.mybir as mybir
from concourse.bass_utils import run_bass_kernel_spmd

F32 = mybir.dt.float32
BF16 = mybir.dt.bfloat16
I32 = mybir.dt.int32
U32 = mybir.dt.uint32
AF = mybir.ActivationFunctionType
ALU = mybir.AluOpType
AX = mybir.AxisListType

_OUTKEYS = ("out", "accum_out", "out_max", "out_indices", "ap")
SEM_LIMIT = 30000


class Buf:
    def __init__(self, t, kind):
        self.t = t
        self.kind = kind
        self.w = {}
        self.r = {}
        self.ld = None
        self.st = None

    def __getitem__(self, idx):
        return self.t[idx]


def _upd(d, sem, val):
    k = sem.name if hasattr(sem, "name") else id(sem)
    if k not in d or d[k][1] < val:
        d[k] = (sem, val)


class Eng:
    def __init__(self, K, name, eng):
        self.K = K
        self.name = name
        self.eng = eng
        self.sem = None
        self.cnt = 0
        self.nsem = 0
        self.waited = {}
        self.ninst = 0
        self.lazy = False
        self.lazy_self_ok = False
        self.pending = None

    def wait(self, deps):
        for k, (sem, val) in deps.items():
            if self.waited.get(k, 0) < val:
                owner = self.K.sem_owner.get(k)
                if owner is not None and owner.pending is not None and owner.sem is sem and val > owner.cnt:
                    if owner is self and self.lazy_self_ok:
                        continue
                    owner.flush()
                self.eng.wait_ge(sem, val)
                self.waited[k] = val

    def flush(self):
        if self.pending is not None:
            self.cnt += 1
            self.pending.then_inc(self.sem, 1)
            self.pending = None

    def tick(self, inst):
        if self.pending is None and (self.sem is None or self.cnt >= SEM_LIMIT):
            self.sem = self.K.nc.alloc_semaphore(f"s_{self.name}{self.nsem}")
            self.K.sem_owner[self.sem.name] = self
            self.nsem += 1
            self.cnt = 0
        self.ninst += 1
        if self.lazy:
            self.pending = inst
            return (self.sem, self.cnt + 1)
        self.cnt += 1
        inst.then_inc(self.sem, 1)
        return (self.sem, self.cnt)


class KB:
    def __init__(self, nc):
        self.nc = nc
        self.bufs = {}
        self.pe = Eng(self, "pe", nc.tensor)
        self.act = Eng(self, "act", nc.scalar)
        self.dve = Eng(self, "dve", nc.vector)
        self.pool = Eng(self, "pool", nc.gpsimd)
        self.sp = Eng(self, "sp", nc.sync)
        self.final = {}
        self.nsem_dma = 0
        self.sem_owner = {}
        self.pe.lazy = True
        self.pe.lazy_self_ok = True

    def sb(self, name, shape, dtype=F32, side=None):
        t = self.nc.alloc_sbuf_tensor(name, list(shape), dtype, side=side)
        b = Buf(t, "sb")
        self.bufs[t.name] = b
        return b

    def ps(self, name, shape, dtype=F32):
        t = self.nc.alloc_psum_tensor(name, list(shape), dtype)
        b = Buf(t, "ps")
        self.bufs[t.name] = b
        return b

    def dram(self, name, shape, dtype, kind):
        t = self.nc.dram_tensor(name, list(shape), dtype, kind=kind)
        b = Buf(t, "dram")
        self.bufs[t.name] = b
        return b

    def bufof(self, ap):
        return self.bufs[ap.tensor.name]

    def op(self, E, name, *args, **kw):
        reads, writes = [], []
        for k, v in kw.items():
            if hasattr(v, "tensor") and hasattr(v, "partition_size"):
                (writes if k in _OUTKEYS else reads).append(self.bufof(v))
        deps = {}
        for b in reads:
            for k, sv in b.w.items():
                _upd(deps, *sv)
        for b in writes:
            for k, sv in b.w.items():
                _upd(deps, *sv)
            for k, sv in b.r.items():
                _upd(deps, *sv)
        E.wait(deps)
        inst = getattr(E.eng, name)(*args, **kw)
        sv = E.tick(inst)
        for b in reads:
            _upd(b.r, *sv)
        for b in writes:
            _upd(b.w, *sv)
        return inst

    def new_group(self):
        rec = [self.nc.alloc_semaphore(f"grp{self.nsem_dma}"), 0, []]
        self.nsem_dma += 1
        return rec

    def close_group(self, rec):
        for (ob, ib) in rec[2]:
            _upd(ob.w, rec[0], rec[1])
            _upd(ib.r, rec[0], rec[1])
            if ob.kind == "dram":
                _upd(self.final, rec[0], rec[1])

    def dma(self, Q, out, in_, indirect=None, group=None, **kw):
        ob, ib = self.bufof(out), self.bufof(in_)
        extra_reads = []
        if indirect is not None:
            extra_reads.append(self.bufof(indirect))
        deps = {}
        for b in [ib] + extra_reads:
            for k, sv in b.w.items():
                _upd(deps, *sv)
        for k, sv in ob.w.items():
            _upd(deps, *sv)
        for k, sv in ob.r.items():
            _upd(deps, *sv)
        Q.wait(deps)
        if group is not None:
            rec = group
            group[2].append((ob, ib))
        elif ob.kind == "sb":
            if ob.ld is None:
                ob.ld = [self.nc.alloc_semaphore(f"ld{self.nsem_dma}"), 0]
                self.nsem_dma += 1
            rec = ob.ld
        else:
            if ib.st is None:
                ib.st = [self.nc.alloc_semaphore(f"st{self.nsem_dma}"), 0]
                self.nsem_dma += 1
            rec = ib.st
        rec[1] += 16
        if indirect is not None:
            inst = Q.eng.indirect_dma_start(out=out, out_offset=None, in_=in_, in_offset=kw.pop("in_offset"), **kw)
        else:
            inst = Q.eng.dma_start(out=out, in_=in_, **kw)
        inst.then_inc(rec[0], 16)
        sv = (rec[0], rec[1])
        _upd(ob.w, *sv)
        for b in [ib] + extra_reads:
            _upd(b.r, *sv)
        if ob.kind == "dram":
            _upd(self.final, *sv)
        return inst

    def barrier(self):
        deps = {}
        for E in (self.pe, self.act, self.dve, self.pool, self.sp):
            E.flush()
            if E.sem is not None:
                _upd(deps, E.sem, E.cnt)
        for b in self.bufs.values():
            for rec in (b.ld, b.st):
                if rec is not None:
                    _upd(deps, rec[0], rec[1])
        for E in (self.pe, self.act, self.dve, self.pool, self.sp):
            E.wait(deps)

    def scoped(self, stack, name, shape, dtype=F32):
        t = stack.enter_context(self.nc.sbuf_tensor(name, list(shape), dtype))
        b = Buf(t, "sb")
        self.bufs[t.name] = b
        return b

    def finish(self):
        for E in (self.pe, self.act, self.dve, self.pool):
            E.flush()
        self.sp.wait(self.final)
        self.sp.eng.nop() if False else None

import contextlib
import ml_dtypes

D = 1024
SEQ = 2048
NT = SEQ // 128
NSB = 4
NST = 4
NS = NSB * NST
PAST = 16384
NPAGES = 128
DIN = 5192
DFF = 3072
EPS = 1e-6
NBIS = 20
NEG = -1.0e30
CANDR = 4
NC_ = CANDR * 8


def build(n_phys, do_sample=True, do_ffn=True):
    nc = bass.Bass("TRN2", target_bir_lowering=False)
    K = KB(nc)
    op, dma = K.op, K.dma
    PE, ACT, DVE, POOL, SP = K.pe, K.act, K.dve, K.pool, K.sp

    def din(name, shape, dt=F32):
        return K.dram(name, shape, dt, "ExternalInput")

    def dout(name, shape, dt=F32):
        return K.dram(name, shape, dt, "ExternalOutput")

    x_p = din("x_prompt", [SEQ, D])
    x_s = din("x_sample", [NS, D])
    c_k = din("cache_k", [n_phys * 128, 512])
    c_v = din("cache_v", [n_phys * 128, 512])
    c_i = din("cache_kidx", [n_phys * 128, 64])
    p_t = din("page_table", [NSB, NPAGES], I32)
    s_lc = din("state_lru_conv", [NSB, 3, 512])
    s_lh = din("state_lru_h", [NSB, 512])
    s_fc = din("state_ffn_conv", [NSB, 2, DFF])
    g_mix = din("norm_mix_g", [1, D])
    w_in = din("w_in", [D, DIN])
    l_cw = din("lru_conv_w", [4, 512])
    l_cb = din("lru_conv_b", [512])
    l_wa = din("lru_wa", [8, 64, 64])
    l_ba = din("lru_ba", [512])
    l_wx = din("lru_wx", [8, 64, 64])
    l_bx = din("lru_bx", [512])
    l_lam = din("lru_lambda", [512])
    w_pa = din("w_proj_a", [512, D])
    w_pb = din("w_proj_b", [512, D])
    w_o = din("w_out", [D, D])
    g_ffn = din("norm_ffn_g", [1, D])
    w_up = din("w_up", [D, 2 * DFF])
    f_cw = din("ffn_conv_w", [3, DFF])
    f_cb = din("ffn_conv_b", [DFF])
    w_dn = din("w_down", [DFF, D])
    g_fin = din("norm_final_g", [1, D])
    c_idb = din("c_ident_bf", [128, 128], BF16)
    c_idf = din("c_ident_f", [128, 128])
    c_cosp = din("c_cos_p", [SEQ, 32])
    c_sinp = din("c_sin_p", [SEQ, 32])
    c_coss = din("c_cos_s", [NS, 32])
    c_sins = din("c_sin_s", [NS, 32])
    c_cmask = din("c_cmask", [128, 128])
    c_pow2 = din("c_pow2", [128, NBIS + 2])
    c_misc = din("c_misc", [128, 1024])
    c_blk = din("c_blk", [128, 128], BF16)
    c_p0 = din("c_p0", [32, 32 * 128], BF16)
    c_selq = din("c_selq", [NS, NSB * 128])
    y_p = dout("y_prompt", [SEQ, D])
    y_s = dout("y_sample", [NS, D])
    k_p = dout("k_prompt", [SEQ, 512])
    v_p = dout("v_prompt", [SEQ, 512])
    ki_p = dout("kidx_prompt", [SEQ, 64])
    lc_p = dout("lru_conv_prompt", [3, 512])
    lh_p = dout("lru_h_prompt", [512])
    fc_p = dout("ffn_conv_prompt", [2, DFF])
    k_s = dout("k_sample", [NS, 512])
    v_s = dout("v_sample", [NS, 512])
    ki_s = dout("kidx_sample", [NS, 64])
    lc_s = dout("lru_conv_sample", [NSB, 3, 512])
    lh_s = dout("lru_h_sample", [NSB, 512])
    fc_s = dout("ffn_conv_sample", [NSB, 2, DFF])
    x1_d = K.dram("x1_scratch", [SEQ + NS, D], F32, "Internal")
    wi_d = K.dram("wi_scratch", [NS, 8], F32, "Internal")

    pT = [K.ps(f"pT{i}", [128, 1024], BF16) for i in range(2)]
    OB = [K.ps(f"OB{i}", [128, 512], F32) for i in range(2)]
    GB = [K.ps(f"GB{i}", [128, 512], F32) for i in range(4)]
    rr = {"g": 0, "t": 0}

    def gbank():
        rr["g"] = (rr["g"] + 1) % 4
        return GB[rr["g"]]

    tb_mode = {"single": False}

    def tbank():
        if tb_mode["single"]:
            return pT[0]
        rr["t"] = (rr["t"] + 1) % 2
        return pT[rr["t"]]

    idb = K.sb("idb", [128, 128], BF16)
    idf = K.sb("idf", [128, 128], F32)
    cmask = K.sb("cmask", [128, 128], F32)
    pow2 = K.sb("pow2", [128, NBIS + 2], F32)
    gbc = K.sb("gbc", [128, D], F32)
    gbc2 = None
    dma(SP, out=idb[:], in_=c_idb[:])
    dma(SP, out=idf[:], in_=c_idf[:])
    dma(SP, out=cmask[:], in_=c_cmask[:])
    dma(SP, out=pow2[:], in_=c_pow2[:])
    dma(SP, out=gbc[:], in_=g_mix[:].broadcast_to([128, D]))


    lcw = K.sb("lcw", [128, 4, 4], F32)
    lcb = K.sb("lcb", [128, 4], F32)
    lba = K.sb("lba", [128, 4], F32)
    lbx = K.sb("lbx", [128, 4], F32)
    lcl = K.sb("lcl", [128, 4], F32)
    fcw = K.sb("fcw", [128, 3, 24], F32)
    fcb = K.sb("fcb", [128, 24], F32)
    def dma_cols(dram2d, c, sb_ap, load, group=None):
        d = dram2d[:, c * 128:(c + 1) * 128].rearrange("j p -> p j")
        if load:
            dma(SP, out=sb_ap, in_=d, allow_slow_non_contiguous=True, group=group)
        else:
            dma(SP, out=d, in_=sb_ap, allow_slow_non_contiguous=True, group=group)

    for c in range(4):
        dma_cols(l_cw, c, lcw[:, :, c], True)
    dma(SP, out=lcb[:], in_=l_cb[:].rearrange("(c p) -> p c", p=128), allow_slow_non_contiguous=True)
    dma(SP, out=lba[:], in_=l_ba[:].rearrange("(c p) -> p c", p=128), allow_slow_non_contiguous=True)
    dma(SP, out=lbx[:], in_=l_bx[:].rearrange("(c p) -> p c", p=128), allow_slow_non_contiguous=True)
    dma(SP, out=lcl[:], in_=l_lam[:].rearrange("(c p) -> p c", p=128), allow_slow_non_contiguous=True)
    for c in range(24):
        dma_cols(f_cw, c, fcw[:, :, c], True)
    dma(SP, out=fcb[:], in_=f_cb[:].rearrange("(c p) -> p c", p=128), allow_slow_non_contiguous=True)
    op(ACT, "activation", out=lcl[:], in_=lcl[:], func=AF.Exp, scale=-1.0)
    op(ACT, "activation", out=lcl[:], in_=lcl[:], func=AF.Ln, bias=1.0)
    op(DVE, "tensor_scalar", out=lcl[:], in0=lcl[:], scalar1=-8.0, scalar2=None, op0=ALU.mult)

    ss = K.sb("ss", [128, 1], F32)
    rstd = K.sb("rstd", [128, 1], F32)


    def rmsnorm_to_bf(xt, n, g, hb, junk=None):
        junk = hb if junk is None else junk
        op(ACT, "activation", out=junk[0:n, :], in_=xt[0:n, :], func=AF.Square, accum_out=ss[0:n, :])
        op(DVE, "tensor_scalar", out=rstd[0:n, :], in0=ss[0:n, :], scalar1=1.0 / D, scalar2=EPS,
           op0=ALU.mult, op1=ALU.add)
        op(ACT, "activation", out=rstd[0:n, :], in_=rstd[0:n, :], func=AF.Sqrt)
        op(DVE, "reciprocal", out=rstd[0:n, :], in_=rstd[0:n, :])
        op(DVE, "scalar_tensor_tensor", out=hb[0:n, :], in0=xt[0:n, :], scalar=rstd[0:n, 0:1], in1=g[0:n, :],
           op0=ALU.mult, op1=ALU.mult)

    def transpose_cols(src, n, ncols, dst_fn):
        nb = ncols // 128
        for c0 in range(0, nb, 8):
            tb = tbank()
            m = min(8, nb - c0)
            for c in range(m):
                op(PE, "transpose", out=tb[:, c * 128:c * 128 + n], in_=src[0:n, (c0 + c) * 128:(c0 + c + 1) * 128],
                   identity=idb[0:n, 0:n])
            for c in range(m):
                op(ACT, "copy", out=dst_fn(c0 + c), in_=tb[:, c * 128:c * 128 + n])

    def rope(src_ps, n, nh, cosb, sinb, dst, tmp):
        v = src_ps.rearrange("p (h two d) -> p h two d", h=nh, two=2)
        x1, x2 = v[:, :, 0, :], v[:, :, 1, :]
        cb = cosb[0:n, :].unsqueeze(1).broadcast_to([n, nh, 32])
        sb_ = sinb[0:n, :].unsqueeze(1).broadcast_to([n, nh, 32])
        t1, t2 = tmp[0][0:n, 0:nh, :], tmp[1][0:n, 0:nh, :]
        op(DVE, "tensor_tensor", out=t1, in0=x1, in1=cb, op=ALU.mult)
        op(DVE, "tensor_tensor", out=t2, in0=x2, in1=sb_, op=ALU.mult)
        op(DVE, "tensor_tensor", out=dst[:, :, 0:32], in0=t1, in1=t2, op=ALU.subtract)
        op(DVE, "tensor_tensor", out=t1, in0=x2, in1=cb, op=ALU.mult)
        op(DVE, "tensor_tensor", out=t2, in0=x1, in1=sb_, op=ALU.mult)
        op(DVE, "tensor_tensor", out=dst[:, :, 32:64], in0=t1, in1=t2, op=ALU.add)

    with contextlib.ExitStack() as so:
        attA = K.scoped(so, "attA", [128, 4, SEQ], BF16)
        s_q = K.scoped(so, "s_q", [NS, 512], F32)
        s_qi = K.scoped(so, "s_qi", [NS, 8, 128], BF16)
        s_kib = K.scoped(so, "s_kib", [NS, 64], BF16)
        s_wi = K.scoped(so, "s_wi", [NS, 8], F32)
        s_attT = K.scoped(so, "s_attT", [128, 4, NS], BF16)
        xt = K.scoped(so, "xt", [128, D], F32)
        hb = K.scoped(so, "hb", [128, D], BF16)
        hT = K.scoped(so, "hT", [128, 8, 128], BF16)

        def load_norm_T(n, xsrc):
            dma(SP, out=xt[0:n, :], in_=xsrc)
            rmsnorm_to_bf(xt, n, gbc, hb)
            transpose_cols(hb, n, D, lambda c: hT[:, c, 0:n])

        with contextlib.ExitStack() as s1:
            NA = 2120
            win = K.scoped(s1, "win_a", [128, 8, NA], BF16)
            for kc in range(8):
                for c0 in range(0, NA, 1060):
                    dma(POOL, out=win[:, kc, c0:c0 + 1060], in_=w_in[kc * 128:(kc + 1) * 128, c0:c0 + 1060])
            KTb = [K.scoped(s1, f"KT{j}", [128, 4, 128], BF16) for j in range(NT)]
            VAb = [K.scoped(s1, f"VA{j}", [128, 8, 65], BF16) for j in range(NT)]
            kiT = K.scoped(s1, "kiT", [128, SEQ], BF16)
            for j in range(NT):
                op(POOL, "memset", ap=VAb[j][:], constant=1.0)
            rt = [K.scoped(s1, f"rt{i}", [128, 8, 32], F32) for i in range(2)]
            cosb = K.scoped(s1, "cosb", [128, 32], F32)
            sinb = K.scoped(s1, "sinb", [128, 32], F32)
            qb = K.scoped(s1, "qb", [128, 8, 64], BF16)
            qTs = [K.scoped(s1, f"qT{i}", [128, 4, 128], BF16) for i in range(3)]
            kf = K.scoped(s1, "kf", [128, 8, 64], F32)
            kb = K.scoped(s1, "kb", [128, 512], BF16)
            vf = K.scoped(s1, "vf", [128, 512], F32)
            qib = K.scoped(s1, "qib", [128, 8, 64], BF16)
            qiT = K.scoped(s1, "qiT", [128, 4, 128], BF16)
            kif = K.scoped(s1, "kif", [128, 1, 64], F32)
            kib2 = K.scoped(s1, "kib2", [128, 128], BF16)
            wif = K.scoped(s1, "wif", [128, 8], F32)
            dg = K.scoped(s1, "dg", [128, 8, 128], BF16)
            Rh = [K.scoped(s1, f"Rh{i}", [128, 512], BF16) for i in range(4)]
            scs = [K.scoped(s1, f"sc{i}", [128, SEQ], F32) for i in range(2)]
            bis = K.scoped(s1, "bis", [128, 8], F32)
            dl = K.scoped(s1, "dl", [128, NBIS + 2], F32)
            mk = K.scoped(s1, "mk", [128, SEQ], BF16)
            mkTs = [K.scoped(s1, f"mkT{i}", [128, NT, 128], BF16) for i in range(2)]
            PTb = [K.scoped(s1, f"PTb{i}", [128, 512], BF16) for i in range(3)]
            pT1f = pT[1][:].bitcast(F32)
            rcp = K.scoped(s1, "rcp", [128, 8, 1], F32)
            att = K.scoped(s1, "att", [128, 8, 64], BF16)
            rrR = {"r": 0, "p": 0, "ga": 0, "gc": 0}
            GA = [GB[0], GB[1]]
            SCB = GB[2]
            GC = [GB[3], GB[3]]
            tb_mode["single"] = True

            def gbankA():
                rrR["ga"] = (rrR["ga"] + 1) % 2
                return GA[rrR["ga"]]

            def front_a(n, xsrc, cos_d, sin_d, is_prompt, i):
                load_norm_T(n, xsrc)
                dma(SP, out=cosb[0:n, :], in_=cos_d)
                dma(SP, out=sinb[0:n, :], in_=sin_d)
                yield

                def tm_group(c0, c1):
                    g = gbankA()
                    for kc in range(8):
                        op(PE, "matmul", out=g[0:n, 0:c1 - c0], lhsT=hT[:, kc, 0:n], rhs=win[:, kc, c0:c1],
                           start=(kc == 0), stop=(kc == 7))
                    return g

                g = tm_group(0, 512)
                if is_prompt:
                    rope(g[0:n, :], n, 8, cosb, sinb, qb[0:n], rt)
                else:
                    rope(g[0:n, :], n, 8, cosb, sinb, s_q[0:n, :].rearrange("p (h d) -> p h d", h=8), rt)
                yield
                g = tm_group(512, 1024)
                rope(g[0:n, :], n, 8, cosb, sinb, kf[0:n], rt)
                if is_prompt:
                    dma(SP, out=k_p[i * 128:(i + 1) * 128, :], in_=kf[:].rearrange("p h d -> p (h d)"))
                    op(POOL, "tensor_copy", out=kb[0:n, :], in_=kf[:].rearrange("p h d -> p (h d)"))
                else:
                    dma(SP, out=k_s[:, :], in_=kf[0:n].rearrange("p h d -> p (h d)"))
                yield
                g = tm_group(1024, 1536)
                op(ACT, "copy", out=vf[0:n, :], in_=g[0:n, :])
                if is_prompt:
                    dma(SP, out=v_p[i * 128:(i + 1) * 128, :], in_=vf[:, :])
                    op(POOL, "tensor_copy", out=VAb[i][:, :, 0:64], in_=vf[:].rearrange("p (h d) -> p h d", h=8))
                else:
                    dma(SP, out=v_s[:, :], in_=vf[0:n, :])
                yield
                g = tm_group(1536, 2048)
                rope(g[0:n, :], n, 8, cosb, sinb, (qib if is_prompt else s_qi)[0:n], rt)
                yield
                g = tm_group(2048, 2120)
                rope(g[0:n, 0:64], n, 1, cosb, sinb, kif[0:n], rt)
                if is_prompt:
                    dma(SP, out=ki_p[i * 128:(i + 1) * 128, :], in_=kif[:, 0, :])
                    op(POOL, "tensor_copy", out=kib2[0:n, 0:64], in_=kif[0:n, 0, :])
                    op(POOL, "tensor_copy", out=kib2[0:n, 64:128], in_=kif[0:n, 0, :])
                    op(ACT, "copy", out=wif[0:n, :], in_=g[0:n, 64:72])
                else:
                    dma(SP, out=ki_s[:, :], in_=kif[0:n, 0, :])
                    op(POOL, "tensor_copy", out=s_kib[0:n, :], in_=kif[0:n, 0, :])
                    op(POOL, "tensor_copy", out=s_qi[0:n, :, 64:128], in_=s_qi[0:n, :, 0:64])
                    op(ACT, "copy", out=s_wi[0:n, :], in_=g[0:n, 64:72])
                yield

            if do_sample:
                for _ in front_a(NS, x_s[:, :], c_coss[:, :], c_sins[:, :], False, 0):
                    pass

            def stageA(i):
                n = 128
                L = 128 * (i + 1)
                sc = scs[i % 2]
                qT = qTs[i % 3]
                yield from front_a(n, x_p[i * 128:(i + 1) * 128, :], c_cosp[i * 128:(i + 1) * 128, :],
                                   c_sinp[i * 128:(i + 1) * 128, :], True, i)
                transpose_cols(qb[:].rearrange("p h d -> p (h d)"), n, 512, lambda c: qT[:, c, :])
                yield
                transpose_cols(kb, n, 512, lambda c: KTb[i][:, c, :])
                yield
                transpose_cols(qib[:].rearrange("p h d -> p (h d)"), n, 512, lambda c: qiT[:, c, :])
                transpose_cols(kib2, n, 128, lambda c: kiT[:, i * 128:(i + 1) * 128])
                for h in range(8):
                    op(POOL, "tensor_scalar", out=dg[:, h, :], in0=idf[:, :], scalar1=wif[:, h:h + 1], scalar2=1.0,
                       op0=ALU.mult, op1=ALU.mult)
                yield
                for c0 in range(0, L, 512):
                    c1 = min(L, c0 + 512)
                    wd = c1 - c0
                    acc = SCB
                    gs = {}

                    def issueS(h):
                        pr, hf = h // 2, h % 2
                        g = gbankA()
                        op(PE, "matmul", out=g[:, 0:wd], lhsT=qiT[hf * 64:(hf + 1) * 64, pr, :],
                           rhs=kiT[hf * 64:(hf + 1) * 64, c0:c1], start=True, stop=True)
                        gs[h] = g

                    issueS(0)
                    Rs_ = {}
                    for h in range(8):
                        if h + 1 < 8:
                            issueS(h + 1)
                        Rs_[h] = Rh[h % 4]
                        op(ACT, "activation", out=Rs_[h][:, 0:wd], in_=gs[h][:, 0:wd], func=AF.Relu)
                        if h >= 1:
                            op(PE, "matmul", out=acc[:, 0:wd], lhsT=dg[:, h - 1, :], rhs=Rs_[h - 1][:, 0:wd],
                               start=(h - 1 == 0), stop=False)
                        if h % 2 == 1:
                            yield
                    op(PE, "matmul", out=acc[:, 0:wd], lhsT=dg[:, 7, :], rhs=Rs_[7][:, 0:wd], start=False, stop=True)
                    if c1 == L:
                        if wd > 128:
                            op(ACT, "copy", out=sc[:, c0:c1 - 128], in_=acc[:, 0:wd - 128])
                        op(DVE, "tensor_tensor", out=sc[:, c1 - 128:c1], in0=acc[:, wd - 128:wd], in1=cmask[:, :], op=ALU.add)
                    else:
                        op(ACT, "copy", out=sc[:, c0:c1], in_=acc[:, 0:wd])
                    yield

            def stageB(i):
                n = 128
                L = 128 * (i + 1)
                sc = scs[i % 2]
                mkT = mkTs[i % 2]
                tcol = bis[:, 3:4]
                if i < 2:
                    op(DVE, "memset", ap=tcol, constant=-1.0e29)
                else:
                    op(DVE, "tensor_reduce", out=bis[:, 0:1], in_=sc[:, 0:L], axis=AX.X, op=ALU.max)
                    op(DVE, "tensor_reduce", out=bis[:, 1:2], in_=sc[:, 0:L - 128], axis=AX.X, op=ALU.min)
                    yield
                    op(DVE, "tensor_tensor", out=bis[:, 2:3], in0=bis[:, 0:1], in1=bis[:, 1:2], op=ALU.subtract)
                    op(DVE, "tensor_scalar", out=dl[:, :], in0=pow2[:, :], scalar1=bis[:, 2:3], scalar2=None, op0=ALU.mult)
                    op(DVE, "tensor_tensor", out=tcol, in0=bis[:, 1:2], in1=dl[:, 1:2], op=ALU.add)
                    yield
                    for k in range(2, NBIS + 2):
                        op(DVE, "tensor_scalar", out=mk[:, 0:L], in0=sc[:, 0:L], scalar1=tcol, scalar2=None,
                           op0=ALU.is_ge, op1=ALU.add, accum_out=bis[:, 4:5])
                        op(DVE, "tensor_scalar", out=bis[:, 5:6], in0=bis[:, 4:5], scalar1=255.5, scalar2=dl[:, k - 1:k],
                           op0=ALU.is_ge, op1=ALU.mult)
                        op(DVE, "scalar_tensor_tensor", out=tcol, in0=tcol, scalar=dl[:, k:k + 1], in1=bis[:, 5:6],
                           op0=ALU.subtract, op1=ALU.add)
                        yield
                    op(DVE, "tensor_tensor", out=tcol, in0=tcol, in1=dl[:, NBIS + 1:NBIS + 2], op=ALU.subtract)
                    yield
                op(DVE, "tensor_scalar", out=mk[:, 0:L], in0=sc[:, 0:L], scalar1=tcol, scalar2=None, op0=ALU.is_ge)
                yield
                for c0 in range(0, i + 1, 8):
                    transpose_cols(mk[:, c0 * 128:min(L, (c0 + 8) * 128)], n, min(L, (c0 + 8) * 128) - c0 * 128,
                                   lambda c: mkT[:, c0 + c, :])
                    yield

            def stageC(i):
                n = 128
                qT = qTs[i % 3]
                mkT = mkTs[i % 2]
                units = [(h, j0, min(i + 1, j0 + 4)) for h in range(8) for j0 in range(0, i + 1, 4)]
                GC = [GB[3], pT1f]
                sg, Pu = {}, {}

                def issueST(u):
                    h, j0, j1 = units[u]
                    pr, hf = h // 2, h % 2
                    g = GC[u % 2]
                    for j in range(j0, j1):
                        op(PE, "matmul", out=g[:, (j - j0) * 128:(j - j0 + 1) * 128],
                           lhsT=KTb[j][hf * 64:(hf + 1) * 64, pr, :],
                           rhs=qT[hf * 64:(hf + 1) * 64, pr, :], start=True, stop=True)
                    sg[u] = g

                def issuePV(u):
                    h, j0, j1 = units[u]
                    ob = OB[h // 4]
                    for j in range(j0, j1):
                        op(PE, "matmul", out=ob[:, (h % 4) * 65:(h % 4) * 65 + 65], lhsT=Pu[u][:, (j - j0) * 128:(j - j0 + 1) * 128],
                           rhs=VAb[j][:, h, :], start=(j == 0), stop=(j == i))

                issueST(0)
                yield
                for u, (h, j0, j1) in enumerate(units):
                    if u + 1 < len(units):
                        issueST(u + 1)
                    g = sg.pop(u)
                    Pu[u] = PTb[u % 3]
                    wd = (j1 - j0) * 128
                    op(ACT, "activation", out=Pu[u][:, 0:wd], in_=g[:, 0:wd], func=AF.Exp, scale=0.125)
                    op(POOL, "tensor_tensor", out=Pu[u][:, 0:wd], in0=Pu[u][:, 0:wd],
                       in1=mkT[:, j0:j1, :].rearrange("p j q -> p (j q)"), op=ALU.mult)
                    if u >= 1:
                        issuePV(u - 1)
                    yield
                issuePV(len(units) - 1)
                yield
                for hh in range(2):
                    ov = OB[hh][:, 0:260].rearrange("p (h d) -> p h d", h=4)
                    op(DVE, "reciprocal", out=rcp[:, hh * 4:hh * 4 + 4, :], in_=ov[:, :, 64:65])
                    op(DVE, "tensor_tensor", out=att[:, hh * 4:hh * 4 + 4, :], in0=ov[:, :, 0:64],
                       in1=rcp[:, hh * 4:hh * 4 + 4, :].broadcast_to([128, 4, 64]), op=ALU.mult)
                yield
                transpose_cols(att[:].rearrange("p h d -> p (h d)"), n, 512, lambda c: attA[:, c, i * 128:(i + 1) * 128])
                yield

            stages = [stageA, stageB, stageC]
            for step in range(NT + len(stages) - 1):
                alive = [stages[s](step - s) for s in range(len(stages)) if 0 <= step - s < NT]
                while alive:
                    nxt = []
                    for gnr in alive:
                        try:
                            next(gnr)
                            nxt.append(gnr)
                        except StopIteration:
                            pass
                    alive = nxt
            tb_mode["single"] = False
        K.barrier()

        wdn = K.sb("wdn", [128, 24, D], BF16, side="right")
        for c in range(24):
            dma(POOL, out=wdn[:, c, :], in_=w_dn[c * 128:(c + 1) * 128, :])
        if do_sample:
            with contextlib.ExitStack() as s3:
                NBS = 34
                BR = 16384.0
                misc = K.scoped(s3, "misc", [128, 1024], F32)
                blk = K.scoped(s3, "blk", [128, 128], BF16)
                selq = K.scoped(s3, "selq", [NS, NSB, 128], F32)
                dma(SP, out=misc[:], in_=c_misc[:])
                dma(SP, out=blk[:], in_=c_blk[:])
                dma(SP, out=selq[:], in_=c_selq[:].rearrange("t (b m) -> t b m", b=NSB))
                slotbase = misc[:, 0:1]
                negnew = misc[:, 1:5]
                iota_pg = misc[:, 16:144]
                agg = misc[:, 160:224].rearrange("p (b t) -> p b t", b=NSB)
                qiTs = K.scoped(s3, "qiTs", [128, NSB, 8 * NST], BF16)
                tb = tbank()
                for h in range(8):
                    op(PE, "transpose", out=tb[:, h * NS:(h + 1) * NS], in_=s_qi[0:NS, h, :], identity=idb[0:NS, 0:NS])
                op(ACT, "copy", out=qiTs[:].rearrange("p b (h t) -> p h b t", h=8), in_=tb[:, 0:8 * NS].rearrange("p (h b t) -> p h b t", h=8, b=NSB))
                kinT = K.scoped(s3, "kinT", [64, NS], BF16)
                tb = tbank()
                op(PE, "transpose", out=tb[0:64, 0:NS], in_=s_kib[0:NS, :], identity=idb[0:NS, 0:NS])
                op(ACT, "copy", out=kinT[:, :], in_=tb[0:64, 0:NS])
                wcol = K.scoped(s3, "wcol", [32, NSB], F32)
                dma(SP, out=wi_d[:, :], in_=s_wi[:, :])
                for h in range(8):
                    dma(SP, out=wcol[h * 4:(h + 1) * 4, :], in_=wi_d[:, h].rearrange("(b t) -> t b", b=NSB),
                        allow_slow_non_contiguous=True)
                ptc = K.scoped(s3, "ptc", [128, NSB], I32)
                for b in range(NSB):
                    dma(SP, out=ptc[:, b:b + 1], in_=p_t[b, :].rearrange("(j o) -> j o", o=1), allow_slow_non_contiguous=True)
                ptri = K.scoped(s3, "ptri", [128, NSB * 128], I32)
                dma(SP, out=ptri[:, :], in_=p_t[:, :].rearrange("(o b) j -> o (b j)", o=1).broadcast_to([128, NSB * 128]))
                ptrf = K.scoped(s3, "ptrf", [128, NSB, 128], F32)
                op(DVE, "tensor_copy", out=ptrf[:].rearrange("p b j -> p (b j)"), in_=ptri[:, :])
                Rs = [K.scoped(s3, f"Rs{i}", [32, 512], BF16) for i in range(2)]
                scw = K.scoped(s3, "scw", [128, 512], F32)
                candv = K.scoped(s3, "candv", [128, NSB, 36], F32)
                candi = K.scoped(s3, "candi", [128, NSB, 32], U32)
                s3a = contextlib.ExitStack()
                s3a.__enter__()
                p0 = K.scoped(s3a, "p0", [32, 32 * 128], BF16)
                dma(SP, out=p0[:], in_=c_p0[:])
                Wp = K.scoped(s3a, "Wp", [32, 32, 128], BF16)
                pg = K.scoped(s3a, "pg", [128, 8192], F32)
                pgb = K.scoped(s3a, "pgb", [128, 8192], BF16)
                kTs = K.scoped(s3a, "kTs", [128, 64, 128], BF16)
                rsx = 0
                for b in range(NSB):
                    op(DVE, "tensor_scalar", out=Wp[:].rearrange("p s m -> p (s m)"), in0=p0[:, :], scalar1=wcol[:, b:b + 1],
                       scalar2=None, op0=ALU.mult)
                    dma(POOL, out=pg[:, :], in_=c_i[:, :].rearrange("(n s) d -> n (s d)", s=128), indirect=ptc[:, b:b + 1],
                        in_offset=bass.IndirectOffsetOnAxis(ap=ptc[:, b:b + 1], axis=0))
                    op(DVE, "tensor_copy", out=pgb[:, :], in_=pg[:, :])
                    for s0 in range(0, 64, 8):
                        tb = tbank()
                        for s_ in range(8):
                            op(PE, "transpose", out=tb[:, s_ * 128:(s_ + 1) * 128],
                               in_=pgb[:, (s0 + s_) * 128:(s0 + s_ + 1) * 128], identity=idb[:, :])
                        op(ACT, "copy", out=kTs[:, s0:s0 + 8, :], in_=tb[:, :].rearrange("p (s j) -> p s j", s=8))
                    accb = OB[0]
                    for seg in range(32):
                        gq, e = seg // 2, seg % 2
                        g = gbank()
                        op(PE, "matmul", out=g[0:32, :], lhsT=qiTs[e * 64:(e + 1) * 64, b, :],
                           rhs=kTs[e * 64:(e + 1) * 64, gq * 4:(gq + 1) * 4, :], start=True, stop=True)
                        rsx = (rsx + 1) % 2
                        op(ACT, "activation", out=Rs[rsx][:, :], in_=g[0:32, :], func=AF.Relu)
                        op(PE, "matmul", out=accb[:, :], lhsT=Wp[:, seg, :], rhs=Rs[rsx][:, :], start=(seg == 0), stop=(seg == 31))
                    op(ACT, "copy", out=scw[:, :], in_=accb[:, :])
                    g = gbank()
                    op(PE, "matmul", out=g[0:32, 0:NST], lhsT=qiTs[0:64, b, :],
                       rhs=kinT[:, b * NST:(b + 1) * NST], start=True, stop=True)
                    rsx = (rsx + 1) % 2
                    op(ACT, "activation", out=Rs[rsx][:, 0:NST], in_=g[0:32, 0:NST], func=AF.Relu)
                    g2 = gbank()
                    op(PE, "matmul", out=g2[:, 0:NST], lhsT=Wp[:, 0, :], rhs=Rs[rsx][:, 0:NST], start=True, stop=True)
                    op(DVE, "tensor_tensor", out=candv[:, b, 32:36], in0=g2[:, 0:NST], in1=negnew, op=ALU.add)
                    for r in range(CANDR):
                        op(DVE, "max", out=candv[:, b, r * 8:(r + 1) * 8], in_=scw[:, :])
                        op(DVE, "max_index", out=candi[:, b, r * 8:(r + 1) * 8], in_max=candv[:, b, r * 8:(r + 1) * 8], in_values=scw[:, :])
                        if r + 1 < CANDR:
                            op(DVE, "match_replace", out=scw[:, :], in_to_replace=candv[:, b, r * 8:(r + 1) * 8], in_values=scw[:, :],
                               imm_value=NEG)
                s3a.__exit__(None, None, None)
                K.barrier()
                thr = K.scoped(s3, "thr", [128, NSB, 1], F32)
                cmpb = K.scoped(s3, "cmpb", [128, NSB, 36], F32)
                cnt = K.scoped(s3, "cnt", [128, NSB], F32)
                cntb = K.scoped(s3, "cntb", [128, NSB], BF16)
                stp = K.scoped(s3, "stp", [128, NSB], F32)
                op(DVE, "memset", ap=thr[:], constant=0.0)
                for k in range(1, NBS + 1):
                    dlt = BR * (2.0 ** -k)
                    op(DVE, "tensor_tensor", out=cmpb[:], in0=candv[:], in1=thr[:].broadcast_to([128, NSB, 36]), op=ALU.is_ge)
                    op(DVE, "tensor_reduce", out=cnt[:, :], in_=cmpb[:], axis=AX.X, op=ALU.add)
                    op(DVE, "tensor_copy", out=cntb[:, :], in_=cnt[:, :])
                    g = gbank()
                    op(PE, "matmul", out=g[:, 0:NSB], lhsT=blk[:, :], rhs=cntb[:, :], start=True, stop=True)
                    op(DVE, "tensor_scalar", out=stp[:, :], in0=g[:, 0:NSB], scalar1=255.5, scalar2=2.0 * dlt, op0=ALU.is_ge, op1=ALU.mult)
                    op(DVE, "scalar_tensor_tensor", out=thr[:, :, 0], in0=thr[:, :, 0], scalar=-dlt, in1=stp[:, :], op0=ALU.add, op1=ALU.add)
                op(DVE, "tensor_scalar", out=thr[:, :, 0], in0=thr[:, :, 0], scalar1=-BR * (2.0 ** -NBS), scalar2=None, op0=ALU.add)
                sel = K.scoped(s3, "sel", [128, NSB, 36], F32)
                op(DVE, "tensor_tensor", out=sel[:], in0=candv[:], in1=thr[:].broadcast_to([128, NSB, 36]), op=ALU.is_ge)
                idxf = K.scoped(s3, "idxf", [128, NSB, 32], F32)
                spl = K.scoped(s3, "spl", [128, NSB, 32], F32)
                tmp3 = K.scoped(s3, "tmp3", [128, NSB, 32], F32)
                pgf = K.scoped(s3, "pgf", [128, NSB, 32], F32)
                op(DVE, "tensor_copy", out=idxf[:], in_=candi[:])
                op(DVE, "tensor_scalar", out=spl[:], in0=idxf[:], scalar1=127.5, scalar2=None, op0=ALU.is_ge)
                for thv in (255.5, 383.5):
                    op(DVE, "tensor_scalar", out=tmp3[:], in0=idxf[:], scalar1=thv, scalar2=None, op0=ALU.is_ge)
                    op(DVE, "tensor_tensor", out=spl[:], in0=spl[:], in1=tmp3[:], op=ALU.add)
                op(DVE, "scalar_tensor_tensor", out=pgf[:], in0=spl[:], scalar=-128.0, in1=idxf[:], op0=ALU.mult, op1=ALU.add)
                oh = K.scoped(s3, "oh", [128, 32, 128], F32)
                phys = K.scoped(s3, "phys", [128, NSB, 32], F32)
                for b in range(NSB):
                    op(DVE, "tensor_tensor", out=oh[:], in0=pgf[:, b, :].unsqueeze(2).broadcast_to([128, 32, 128]),
                       in1=iota_pg.unsqueeze(1).broadcast_to([128, 32, 128]), op=ALU.is_equal)
                    op(DVE, "tensor_tensor", out=oh[:], in0=oh[:], in1=ptrf[:, b, :].unsqueeze(1).broadcast_to([128, 32, 128]), op=ALU.mult)
                    op(DVE, "tensor_reduce", out=phys[:, b, :], in_=oh[:], axis=AX.X, op=ALU.add)
                rowf = K.scoped(s3, "rowf", [128, NSB, 32], F32)
                op(DVE, "tensor_scalar", out=rowf[:], in0=phys[:], scalar1=128.0, scalar2=slotbase, op0=ALU.mult, op1=ALU.add)
                op(DVE, "scalar_tensor_tensor", out=rowf[:], in0=spl[:], scalar=2.0, in1=rowf[:], op0=ALU.mult, op1=ALU.add)
                BIG = 1.0e9
                op(DVE, "tensor_scalar", out=tmp3[:], in0=sel[:, :, 0:32], scalar1=-BIG, scalar2=BIG, op0=ALU.mult, op1=ALU.add)
                op(DVE, "tensor_tensor", out=rowf[:], in0=rowf[:], in1=sel[:, :, 0:32], op=ALU.mult)
                op(DVE, "tensor_tensor", out=rowf[:], in0=rowf[:], in1=tmp3[:], op=ALU.add)
                rowi = K.scoped(s3, "rowi", [128, NSB, 32], I32)
                op(DVE, "tensor_copy", out=rowi[:], in_=rowf[:])
                CH = 4
                Kgs = [K.scoped(s3, f"Kg{i}", [128, CH, 512], F32) for i in range(2)]
                Vgs = [K.scoped(s3, f"Vg{i}", [128, CH, 512], F32) for i in range(2)]
                prod = K.scoped(s3, "prod", [128, CH, 512], F32)
                qsb = K.scoped(s3, "qsb", [128, 512], F32)
                S_ = K.scoped(s3, "S_", [128, CH, 8], F32)
                Pm = K.scoped(s3, "Pm", [128, CH, 8], F32)
                Oacc = K.scoped(s3, "Oacc", [128, 512], F32)
                Opart = K.scoped(s3, "Opart", [128, 512], F32)
                racc = K.scoped(s3, "racc", [128, 8], F32)
                rpart = K.scoped(s3, "rpart", [128, 8], F32)
                for i_ in range(2):
                    op(DVE, "memset", ap=Kgs[i_][:], constant=0.0)
                    op(DVE, "memset", ap=Vgs[i_][:], constant=0.0)
                cix = 0
                nrows = n_phys * 128
                bc_reg = nc.gpsimd.to_reg(nrows - 1)
                for b in range(NSB):
                    g = gbank()
                    op(PE, "matmul", out=g[:, :], lhsT=selq[:, b, :], rhs=s_q[:, :], start=True, stop=True)
                    op(ACT, "copy", out=qsb[:, :], in_=g[:, :])
                    op(DVE, "memset", ap=Oacc[:], constant=0.0)
                    op(DVE, "memset", ap=racc[:], constant=0.0)
                    chunks = [(c0, CH, False) for c0 in range(0, 32, CH)] + [(32, NST, True)]
                    for (c0, w_, isnew) in chunks:
                        cix += 1
                        Kg, Vg = Kgs[cix % 2], Vgs[cix % 2]
                        if isnew:
                            dma(SP, out=Kg[:, 0:NST, :].rearrange("p t f -> p (t f)"),
                                in_=k_s[b * NST:(b + 1) * NST, :].rearrange("(o t) f -> o (t f)", o=1).broadcast_to([128, NST * 512]))
                            dma(SP, out=Vg[:, 0:NST, :].rearrange("p t f -> p (t f)"),
                                in_=v_s[b * NST:(b + 1) * NST, :].rearrange("(o t) f -> o (t f)", o=1).broadcast_to([128, NST * 512]))
                        else:
                            for kk in range(w_):
                                for (dst, src) in ((Kg, c_k), (Vg, c_v)):
                                    dma(POOL, out=dst[:, kk, :], in_=src[:, :], indirect=rowi[:, b, c0 + kk:c0 + kk + 1],
                                        in_offset=bass.IndirectOffsetOnAxis(ap=rowi[:, b, c0 + kk:c0 + kk + 1], axis=0),
                                        bounds_check=bc_reg, oob_is_err=False)
                        selc = sel[:, b, c0:c0 + w_]
                        op(DVE, "tensor_tensor", out=prod[:, 0:w_, :], in0=Kg[:, 0:w_, :],
                           in1=qsb[:, :].unsqueeze(1).broadcast_to([128, w_, 512]), op=ALU.mult)
                        op(DVE, "tensor_reduce", out=S_[:, 0:w_, :], in_=prod[:, 0:w_, :].rearrange("p k (h d) -> p k h d", h=8),
                           axis=AX.X, op=ALU.add)
                        op(DVE, "tensor_tensor", out=S_[:, 0:w_, :], in0=S_[:, 0:w_, :],
                           in1=selc.unsqueeze(2).broadcast_to([128, w_, 8]), op=ALU.mult)
                        op(ACT, "activation", out=Pm[:, 0:w_, :], in_=S_[:, 0:w_, :], func=AF.Exp, scale=0.125)
                        op(DVE, "tensor_tensor", out=Pm[:, 0:w_, :], in0=Pm[:, 0:w_, :],
                           in1=selc.unsqueeze(2).broadcast_to([128, w_, 8]), op=ALU.mult)
                        op(DVE, "tensor_tensor", out=prod[:, 0:w_, :].rearrange("p k (h d) -> p k h d", h=8),
                           in0=Vg[:, 0:w_, :].rearrange("p k (h d) -> p k h d", h=8),
                           in1=Pm[:, 0:w_, :].unsqueeze(3).broadcast_to([128, w_, 8, 64]), op=ALU.mult)
                        op(DVE, "tensor_reduce", out=Opart[:, :], in_=prod[:, 0:w_, :].rearrange("p k f -> p f k"), axis=AX.X, op=ALU.add)
                        op(DVE, "tensor_tensor", out=Oacc[:, :], in0=Oacc[:, :], in1=Opart[:, :], op=ALU.add)
                        op(DVE, "tensor_reduce", out=rpart[:, :], in_=Pm[:, 0:w_, :].rearrange("p k h -> p h k"), axis=AX.X, op=ALU.add)
                        op(DVE, "tensor_tensor", out=racc[:, :], in0=racc[:, :], in1=rpart[:, :], op=ALU.add)
                    op(PE, "matmul", out=OB[0][0:NS, :], lhsT=agg[:, b, :], rhs=Oacc[:, :], start=(b == 0), stop=(b == NSB - 1))
                    op(PE, "matmul", out=OB[1][0:NS, 0:8], lhsT=agg[:, b, :], rhs=racc[:, :], start=(b == 0), stop=(b == NSB - 1))
                rcs = K.scoped(s3, "rcs", [NS, 8, 1], F32)
                atts = K.scoped(s3, "atts", [NS, 8, 64], BF16)
                op(DVE, "reciprocal", out=rcs[:, :, 0], in_=OB[1][0:NS, 0:8])
                op(DVE, "tensor_tensor", out=atts[:], in0=OB[0][0:NS, :].rearrange("p (h d) -> p h d", h=8),
                   in1=rcs[:].broadcast_to([NS, 8, 64]), op=ALU.mult)
                transpose_cols(atts[:].rearrange("p h d -> p (h d)"), NS, 512, lambda c: s_attT[:, c, 0:NS])
            K.barrier()

        with contextlib.ExitStack() as s4:
            NB0 = 2120
            winb = K.scoped(s4, "win_b", [128, 8, 3072], BF16)
            for kc in range(8):
                for c0 in range(0, 3072, 1536):
                    dma(POOL, out=winb[:, kc, c0:c0 + 1536], in_=w_in[kc * 128:(kc + 1) * 128, NB0 + c0:NB0 + c0 + 1536])
            wpa = K.scoped(s4, "wpa", [128, 4, D], BF16)
            wpb = K.scoped(s4, "wpb", [128, 4, D], BF16)
            wo = K.scoped(s4, "wo", [128, 8, D], BF16)
            wabd = K.scoped(s4, "wabd", [128, 4, 128], BF16)
            wxbd = K.scoped(s4, "wxbd", [128, 4, 128], BF16)
            dma(POOL, out=wpa[:], in_=w_pa[:].rearrange("(c p) n -> p c n", p=128))
            dma(POOL, out=wpb[:], in_=w_pb[:].rearrange("(c p) n -> p c n", p=128))
            for kc in range(8):
                dma(POOL, out=wo[:, kc, :], in_=w_o[kc * 128:(kc + 1) * 128, :])
            op(DVE, "memset", ap=wabd[:], constant=0.0)
            op(DVE, "memset", ap=wxbd[:], constant=0.0)
            for blk_ in range(8):
                c, j = blk_ // 2, blk_ % 2
                dma(POOL, out=wabd[j * 64:(j + 1) * 64, c, j * 64:(j + 1) * 64], in_=l_wa[blk_])
                dma(POOL, out=wxbd[j * 64:(j + 1) * 64, c, j * 64:(j + 1) * 64], in_=l_wx[blk_])
            hst = K.scoped(s4, "hst", [128, 4], F32)
            xexts = [K.scoped(s4, f"xext{i}", [128, 4, 131], F32) for i in range(2)]
            exts = K.scoped(s4, "exts", [128, 4, NSB, 7], F32)
            h0T = K.scoped(s4, "h0T", [128, 4, NSB], F32)
            hls = K.scoped(s4, "hls", [128, 4, NSB], F32)
            xtbs = [K.scoped(s4, f"xtb{i}", [128, D], F32) for i in range(2)]
            ggs = [K.scoped(s4, f"gg{i}", [128, 4, 128], BF16) for i in range(2)]
            sgas = [K.scoped(s4, f"sga{i}", [128, 8, 128], BF16) for i in range(2)]
            sgbs = [K.scoped(s4, f"sgb{i}", [128, 8, 128], BF16) for i in range(2)]
            xc = K.scoped(s4, "xc", [128, 4, 128], F32)
            xcb = K.scoped(s4, "xcb", [128, 4, 128], BF16)
            ra = K.scoped(s4, "ra", [128, 4, 128], F32)
            ih = K.scoped(s4, "ih", [128, 4, 128], F32)
            mu = K.scoped(s4, "mu", [128, 4, 128], F32)
            hl = K.scoped(s4, "hl", [128, 4, 128], F32)
            ybi = K.scoped(s4, "ybi", [128, 4, 128], BF16)
            mg = K.scoped(s4, "mg", [128, 8, 128], BF16)
            t1 = K.scoped(s4, "t1", [128, 4, 128], F32)
            t2 = K.scoped(s4, "t2", [128, 4, 128], F32)
            op(DVE, "memset", ap=hst[:], constant=0.0)
            op(DVE, "memset", ap=xexts[0][:], constant=0.0)
            op(DVE, "memset", ap=xexts[1][:], constant=0.0)

            def front_b(n, xsrc, is_prompt, par):
                xt_, gg, sga, sgb, xext = xtbs[par], ggs[par], sgas[par], sgbs[par], xexts[par]
                dma(SP, out=xt_[0:n, :], in_=xsrc)
                rmsnorm_to_bf(xt_, n, gbc, hb)
                transpose_cols(hb, n, D, lambda c: hT[:, c, 0:n])
                yield

                def fm_bank(col0, nchunk):
                    g = gbank()
                    for cc in range(nchunk):
                        for kc in range(8):
                            op(PE, "matmul", out=g[:, cc * n:(cc + 1) * n],
                               lhsT=winb[:, kc, col0 + cc * 128:col0 + (cc + 1) * 128],
                               rhs=hT[:, kc, 0:n], start=(kc == 0), stop=(kc == 7))
                    return g[:, 0:nchunk * n].rearrange("p (c n) -> p c n", c=nchunk)

                gv = fm_bank(0, 4)
                if is_prompt:
                    op(POOL, "tensor_copy", out=xext[:, :, 0:3], in_=xexts[1 - par][:, :, 128:131])
                    op(ACT, "copy", out=xext[:, :, 3:3 + n], in_=gv)
                else:
                    for c in range(4):
                        op(ACT, "copy", out=exts[:, c, :, 3:7], in_=gv[:, c, :].rearrange("p (b t) -> p b t", b=NSB))
                yield
                gv = fm_bank(512, 4)
                op(ACT, "activation", out=gg[:, :, 0:n], in_=gv, func=AF.Gelu_apprx_tanh)
                yield
                for hh in range(2):
                    gv = fm_bank(1024 + hh * 512, 4)
                    op(ACT, "activation", out=sga[:, hh * 4:hh * 4 + 4, 0:n], in_=gv, func=AF.Sigmoid)
                    yield
                for hh in range(2):
                    gv = fm_bank(2048 + hh * 512, 4)
                    op(ACT, "activation", out=sgb[:, hh * 4:hh * 4 + 4, 0:n], in_=gv, func=AF.Sigmoid)
                    yield

            def mixer_back(n, attT_fn, conv_fn, scan_fn, x1row0, par):
                xt_, gg, sga, sgb = xtbs[par], ggs[par], sgas[par], sgbs[par]
                conv_fn()
                op(POOL, "tensor_copy", out=xcb[:, :, 0:n], in_=xc[:, :, 0:n])
                yield
                gb_ = gbank()
                for c in range(4):
                    op(PE, "matmul", out=gb_[:, c * n:(c + 1) * n], lhsT=wabd[:, c, :], rhs=xcb[:, c, 0:n], start=True, stop=True)
                for c in range(4):
                    op(ACT, "activation", out=ra[:, c, 0:n], in_=gb_[:, c * n:(c + 1) * n], func=AF.Sigmoid, bias=lba[:, c:c + 1])
                gb2 = gbank()
                for c in range(4):
                    op(PE, "matmul", out=gb2[:, c * n:(c + 1) * n], lhsT=wxbd[:, c, :], rhs=xcb[:, c, 0:n], start=True, stop=True)
                for c in range(4):
                    op(ACT, "activation", out=ih[:, c, 0:n], in_=gb2[:, c * n:(c + 1) * n], func=AF.Sigmoid, bias=lbx[:, c:c + 1])
                yield
                for c in range(4):
                    op(ACT, "activation", out=ra[:, c, 0:n], in_=ra[:, c, 0:n], func=AF.Exp, scale=lcl[:, c:c + 1])
                yield
                op(DVE, "tensor_tensor", out=mu[:, :, 0:n], in0=ra[:, :, 0:n], in1=ra[:, :, 0:n], op=ALU.mult)
                op(DVE, "tensor_scalar", out=mu[:, :, 0:n], in0=mu[:, :, 0:n], scalar1=-1.0, scalar2=1.0, op0=ALU.mult, op1=ALU.add)
                op(DVE, "tensor_scalar", out=mu[:, :, 0:n], in0=mu[:, :, 0:n], scalar1=0.0, scalar2=None, op0=ALU.max)
                yield
                op(ACT, "activation", out=mu[:, :, 0:n], in_=mu[:, :, 0:n], func=AF.Sqrt)
                yield
                op(DVE, "tensor_tensor", out=mu[:, :, 0:n], in0=mu[:, :, 0:n], in1=ih[:, :, 0:n], op=ALU.mult)
                op(DVE, "tensor_tensor", out=mu[:, :, 0:n], in0=mu[:, :, 0:n], in1=xc[:, :, 0:n], op=ALU.mult)
                yield
                scan_fn()
                yield
                op(DVE, "tensor_tensor", out=ybi[:, :, 0:n], in0=hl[:, :, 0:n], in1=gg[:, :, 0:n], op=ALU.mult)
                yield
                for half in range(2):
                    ga_ = gbank()
                    for cc in range(4):
                        col = half * 4 + cc
                        for c in range(4):
                            op(PE, "matmul", out=ga_[:, cc * n:(cc + 1) * n], lhsT=wpa[:, c, col * 128:(col + 1) * 128],
                               rhs=attT_fn(c), start=(c == 0), stop=(c == 3))
                    op(DVE, "tensor_tensor", out=t1[:, :, 0:n], in0=ga_[:, 0:4 * n].rearrange("p (c n) -> p c n", c=4),
                       in1=sga[:, half * 4:half * 4 + 4, 0:n], op=ALU.mult)
                    gb3 = gbank()
                    for cc in range(4):
                        col = half * 4 + cc
                        for c in range(4):
                            op(PE, "matmul", out=gb3[:, cc * n:(cc + 1) * n], lhsT=wpb[:, c, col * 128:(col + 1) * 128],
                               rhs=ybi[:, c, 0:n], start=(c == 0), stop=(c == 3))
                    op(DVE, "tensor_tensor", out=t2[:, :, 0:n], in0=gb3[:, 0:4 * n].rearrange("p (c n) -> p c n", c=4),
                       in1=sgb[:, half * 4:half * 4 + 4, 0:n], op=ALU.mult)
                    yield
                    op(DVE, "tensor_tensor", out=mg[:, half * 4:half * 4 + 4, 0:n], in0=t1[:, :, 0:n], in1=t2[:, :, 0:n], op=ALU.add)
                    yield
                for half in range(2):
                    ob = OB[half]
                    for kc in range(8):
                        op(PE, "matmul", out=ob[0:n, :], lhsT=mg[:, kc, 0:n], rhs=wo[:, kc, half * 512:(half + 1) * 512],
                           start=(kc == 0), stop=(kc == 7))
                    yield
                    op(DVE, "tensor_tensor", out=xt_[0:n, half * 512:(half + 1) * 512], in0=ob[0:n, :],
                       in1=xt_[0:n, half * 512:(half + 1) * 512], op=ALU.add)
                dma(SP, out=x1_d[x1row0:x1row0 + n, :], in_=xt_[0:n, :])
                yield

            def stageF(i):
                yield from front_b(128, x_p[i * 128:(i + 1) * 128, :], True, i % 2)

            def stageM(i):
                par = i % 2
                xext = xexts[par]

                def conv_fn():
                    for c in range(4):
                        op(DVE, "tensor_scalar", out=xc[:, c, :], in0=xext[:, c, 0:128], scalar1=lcw[:, 0, c:c + 1],
                           scalar2=lcb[:, c:c + 1], op0=ALU.mult, op1=ALU.add)
                        for j in range(1, 4):
                            op(DVE, "scalar_tensor_tensor", out=xc[:, c, :], in0=xext[:, c, j:j + 128], scalar=lcw[:, j, c:c + 1],
                               in1=xc[:, c, :], op0=ALU.mult, op1=ALU.add)

                def scan_fn():
                    for c in range(4):
                        op(DVE, "tensor_tensor_scan", out=hl[:, c, :], data0=ra[:, c, :], data1=mu[:, c, :],
                           initial=hst[:, c:c + 1], op0=ALU.mult, op1=ALU.add)
                    op(DVE, "tensor_copy", out=hst[:, :], in_=hl[:, :, 127])

                yield from mixer_back(128, lambda c: attA[:, c, i * 128:(i + 1) * 128], conv_fn, scan_fn, i * 128, par)
                if i == NT - 1:
                    for c in range(4):
                        dma_cols(lc_p, c, xext[:, c, 128:131], False)
                    dma(SP, out=lh_p[:].rearrange("(c p) -> p c", p=128), in_=hst[:, :], allow_slow_non_contiguous=True)

            stages = [stageF, stageM]
            for step in range(NT + len(stages) - 1):
                alive = [stages[s](step - s) for s in range(len(stages)) if 0 <= step - s < NT]
                while alive:
                    nxt = []
                    for gnr in alive:
                        try:
                            next(gnr)
                            nxt.append(gnr)
                        except StopIteration:
                            pass
                    alive = nxt

            if do_sample:
                for c in range(4):
                    for b in range(NSB):
                        dma_cols(s_lc[b], c, exts[:, c, b, 0:3], True)
                    dma_cols(s_lh, c, h0T[:, c, :], True)
                for _ in front_b(NS, x_s[:, :], False, 0):
                    pass

                def conv_s():
                    for c in range(4):
                        xv = xc[:, c, 0:NS].rearrange("p (b t) -> p b t", b=NSB)
                        op(DVE, "tensor_scalar", out=xv, in0=exts[:, c, :, 0:NST], scalar1=lcw[:, 0, c:c + 1],
                           scalar2=lcb[:, c:c + 1], op0=ALU.mult, op1=ALU.add)
                        for j in range(1, 4):
                            op(DVE, "scalar_tensor_tensor", out=xv, in0=exts[:, c, :, j:j + NST], scalar=lcw[:, j, c:c + 1],
                               in1=xv, op0=ALU.mult, op1=ALU.add)

                def scan_s():
                    for c in range(4):
                        for b in range(NSB):
                            op(DVE, "tensor_tensor_scan", out=hl[:, c, b * NST:(b + 1) * NST], data0=ra[:, c, b * NST:(b + 1) * NST],
                               data1=mu[:, c, b * NST:(b + 1) * NST], initial=h0T[:, c, b:b + 1], op0=ALU.mult, op1=ALU.add)
                    op(DVE, "tensor_copy", out=hls[:, :, :], in_=hl[:, :, 0:NS].rearrange("p c (b t) -> p c b t", b=NSB)[:, :, :, NST - 1])

                for _ in mixer_back(NS, lambda c: s_attT[:, c, 0:NS], conv_s, scan_s, SEQ, 0):
                    pass
                for c in range(4):
                    for b in range(NSB):
                        dma_cols(lc_s[b], c, exts[:, c, b, 4:7], False)
                    dma_cols(lh_s, c, hls[:, c, :], False)
        K.barrier()

    if do_ffn:
        with contextlib.ExitStack() as s2:
            wup = K.scoped(s2, "wup", [128, 8, 2 * DFF], BF16)
            for kc in range(8):
                for c0 in range(0, 2 * DFF, 2048):
                    dma(POOL, out=wup[:, kc, c0:c0 + 2048], in_=w_up[kc * 128:(kc + 1) * 128, c0:c0 + 2048])
            dma(SP, out=gbc[:], in_=g_ffn[:].broadcast_to([128, D]))
            gbc2 = K.scoped(s2, "gbc2", [128, D], F32)
            dma(SP, out=gbc2[:], in_=g_fin[:].broadcast_to([128, D]))
            x1ts = [K.scoped(s2, f"x1t{i}", [128, D], F32) for i in range(2)]
            h2 = K.scoped(s2, "h2", [128, D], BF16)
            h2Ts = [K.scoped(s2, f"h2T{i}", [128, 8, 128], BF16) for i in range(2)]
            uexl = [K.scoped(s2, f"uex{c}", [128, 130], F32) for c in range(24)]
            uexsl = [K.scoped(s2, f"uexs{c}", [128, NSB, 6], F32) for c in range(24)]
            accs = [K.scoped(s2, f"facc{i}", [128, 128], F32) for i in range(4)]
            gls = [K.scoped(s2, f"fgl{i}", [128, 128], F32) for i in range(4)]
            gTs = [K.scoped(s2, f"gT{i}", [128, 24, 128], BF16) for i in range(2)]
            x2 = K.scoped(s2, "x2", [128, D], F32)
            for c in range(24):
                op(POOL, "memset", ap=uexl[c][:, 0:2], constant=0.0)
            rf = {"a": 0}

            def prep(n, row0, par):
                x1t, h2T = x1ts[par], h2Ts[par]
                dma(SP, out=x1t[0:n, :], in_=x1_d[row0:row0 + n, :])
                rmsnorm_to_bf(x1t, n, gbc, h2)
                transpose_cols(h2, n, D, lambda c: h2T[:, c, 0:n])

            ubs = [K.scoped(s2, f"ubs{i}", [128, 128], BF16) for i in range(6)]

            def up_chunks(n, par, prompt):
                h2T, gT = h2Ts[par], gTs[par]
                banks = {}

                def s1(c):
                    g = gbank()
                    for part in range(2):
                        for kc in range(8):
                            op(PE, "matmul", out=g[:, part * n:(part + 1) * n],
                               lhsT=wup[:, kc, part * DFF + c * 128: part * DFF + (c + 1) * 128],
                               rhs=h2T[:, kc, 0:n], start=(kc == 0), stop=(kc == 7))
                    banks[c] = g

                def s2(c):
                    g = banks.pop(c)
                    if prompt:
                        op(ACT, "copy", out=uexl[c][:, 2:2 + n], in_=g[:, 0:n])
                    else:
                        op(ACT, "copy", out=uexsl[c][:, :, 2:6], in_=g[:, 0:n].rearrange("p (b t) -> p b t", b=NSB))
                    op(ACT, "copy", out=ubs[c % 6][:, 0:n], in_=g[:, n:2 * n])

                def s3(c):
                    acc = accs[c % 4]
                    if prompt:
                        ue = uexl[c]
                        taps = [ue[:, j:j + n] for j in range(3)]
                        av = acc[:, 0:n]
                    else:
                        ue = uexsl[c]
                        taps = [ue[:, :, j:j + NST] for j in range(3)]
                        av = acc[:, 0:n].rearrange("p (b t) -> p b t", b=NSB)
                    op(ACT, "activation", out=av, in_=taps[0], func=AF.Identity, scale=fcw[:, 0, c:c + 1], bias=fcb[:, c:c + 1])
                    for j in (1, 2):
                        op(DVE, "scalar_tensor_tensor", out=av, in0=taps[j], scalar=fcw[:, j, c:c + 1], in1=av,
                           op0=ALU.mult, op1=ALU.add)
                    if prompt:
                        op(POOL, "tensor_copy", out=ue[:, 0:2], in_=ue[:, 128:130])

                def s4(c):
                    op(ACT, "activation", out=gls[c % 4][:, 0:n], in_=accs[c % 4][:, 0:n], func=AF.Gelu_apprx_tanh)

                def s5(c):
                    op(POOL, "tensor_tensor", out=gT[:, c, 0:n], in0=ubs[c % 6][:, 0:n], in1=gls[c % 4][:, 0:n], op=ALU.mult)

                stages_ = [s1, s2, s3, s4, s5]
                for k in range(24 + 4):
                    for si, fn in enumerate(stages_):
                        c = k - si
                        if 0 <= c < 24:
                            fn(c)

            def down_final(n, par, yout):
                x1t, gT = x1ts[par], gTs[par]
                for half in range(2):
                    ob = OB[half]
                    for c in range(24):
                        op(PE, "matmul", out=ob[0:n, :], lhsT=gT[:, c, 0:n], rhs=wdn[:, c, half * 512:(half + 1) * 512],
                           start=(c == 0), stop=(c == 23))
                    op(DVE, "tensor_tensor", out=x2[0:n, half * 512:(half + 1) * 512], in0=ob[0:n, :],
                       in1=x1t[0:n, half * 512:(half + 1) * 512], op=ALU.add)
                rmsnorm_to_bf(x2, n, gbc2, x2, junk=h2)
                dma(SP, out=yout, in_=x2[0:n, :])

            prep(128, 0, 0)
            for i in range(NT):
                par = i % 2
                up_chunks(128, par, True)
                if i == NT - 1:
                    grp = K.new_group()
                    for c in range(24):
                        dma_cols(fc_p, c, uexl[c][:, 0:2], False, group=grp)
                    K.close_group(grp)
                    if do_sample:
                        grp = K.new_group()
                        for b in range(NSB):
                            for c in range(24):
                                dma_cols(s_fc[b], c, uexsl[c][:, b, 0:2], True, group=grp)
                        K.close_group(grp)
                        prep(NS, SEQ, 1 - par)
                else:
                    prep(128, (i + 1) * 128, 1 - par)
                down_final(128, par, y_p[i * 128:(i + 1) * 128, :])
            if do_sample:
                par = NT % 2
                up_chunks(NS, par, False)
                down_final(NS, par, y_s[:, :])
                grp = K.new_group()
                for b in range(NSB):
                    for c in range(24):
                        dma_cols(fc_s[b], c, uexsl[c][:, b, 4:6], False, group=grp)
                K.close_group(grp)
    K.finish()
    return nc


def _host_consts():
    inv = np.power(np.float32(10000.0), -np.arange(32, dtype=np.float32) / np.float32(32)).astype(np.float32)

    def tab(pos):
        ang = pos.astype(np.float32)[:, None] * inv[None, :]
        return np.cos(ang).astype(np.float32), np.sin(ang).astype(np.float32)

    cp, sp_ = tab(np.arange(SEQ))
    pos_s = np.tile(PAST + np.arange(NST), NSB)
    cs, ss_ = tab(pos_s)
    q = np.arange(128)
    cmask = np.where(q[None, :] <= q[:, None], 0.0, NEG).astype(np.float32)
    pow2 = np.tile((2.0 ** -np.arange(NBIS + 2, dtype=np.float64)).astype(np.float32)[None, :], (128, 1))
    blk = (q[:, None] // 32 == q[None, :] // 32).astype(np.float32).astype(ml_dtypes.bfloat16)
    misc = np.zeros((128, 1024), np.float32)
    seg, tq = q % 32, q // 32
    misc[:, 0] = 8 * (seg // 2) + (seg % 2)
    tp = np.arange(4)
    misc[:, 1:5] = np.where((seg[:, None] == 0) & (tp[None, :] <= tq[:, None]), 0.0, NEG)
    misc[:, 16:144] = np.arange(128)[None, :]
    agg = np.zeros((128, NSB, NS), np.float32)
    selq = np.zeros((NS, NSB, 128), np.float32)
    for b in range(NSB):
        agg[q, b, NST * b + tq] = 1.0
        selq[NST * b + tq, b, q] = 1.0
    misc[:, 160:160 + NSB * NS] = agg.reshape(128, -1)
    p0 = np.zeros((8, 4, 32, 128), np.float32)
    for t in range(4):
        for s_ in range(32):
            p0[:, t, s_, t * 32 + s_] = 1.0
    p0 = p0.reshape(32, 32 * 128).astype(ml_dtypes.bfloat16)
    return {
        "c_ident_bf": np.eye(128, dtype=np.float32).astype(ml_dtypes.bfloat16),
        "c_ident_f": np.eye(128, dtype=np.float32),
        "c_cos_p": cp, "c_sin_p": sp_, "c_cos_s": cs, "c_sin_s": ss_,
        "c_cmask": cmask, "c_pow2": pow2, "c_misc": misc, "c_blk": blk,
        "c_p0": p0, "c_selq": selq.reshape(NS, NSB * 128),
    }


_OUT_NAMES = ["y_prompt", "y_sample", "k_prompt", "v_prompt", "kidx_prompt", "lru_conv_prompt", "lru_h_prompt",
              "ffn_conv_prompt", "k_sample", "v_sample", "kidx_sample", "lru_conv_sample", "lru_h_sample",
              "ffn_conv_sample"]


def make_in_map(inputs, c, consts):
    f = lambda a: np.ascontiguousarray(np.asarray(a))
    n_phys = inputs["cache_k"].shape[1]
    m = {
        "x_prompt": f(inputs["x_prompt"][c]),
        "x_sample": f(inputs["x_sample"][NSB * c:NSB * (c + 1)]).reshape(NS, D),
        "cache_k": np.asarray(inputs["cache_k"]).reshape(n_phys * 128, 512),
        "cache_v": np.asarray(inputs["cache_v"]).reshape(n_phys * 128, 512),
        "cache_kidx": np.asarray(inputs["cache_kidx"]).reshape(n_phys * 128, 64),
        "page_table": f(inputs["page_table"][NSB * c:NSB * (c + 1)]).astype(np.int32),
        "state_lru_conv": f(inputs["state_lru_conv"][0, NSB * c:NSB * (c + 1)]),
        "state_lru_h": f(inputs["state_lru_h"][0, NSB * c:NSB * (c + 1)]),
        "state_ffn_conv": f(inputs["state_ffn_conv"][0, NSB * c:NSB * (c + 1)]),
        "norm_mix_g": f(inputs["norm_mix_g"]).reshape(1, D),
        "w_in": f(inputs["w_in"][0]),
        "lru_conv_w": f(inputs["lru_conv_w"][0]),
        "lru_conv_b": f(inputs["lru_conv_b"][0]),
        "lru_wa": f(inputs["lru_wa"][0]),
        "lru_ba": f(inputs["lru_ba"][0]),
        "lru_wx": f(inputs["lru_wx"][0]),
        "lru_bx": f(inputs["lru_bx"][0]),
        "lru_lambda": f(inputs["lru_lambda"][0]),
        "w_proj_a": f(inputs["w_proj_a"][0]),
        "w_proj_b": f(inputs["w_proj_b"][0]),
        "w_out": f(inputs["w_out"][0]),
        "norm_ffn_g": f(inputs["norm_ffn_g"]).reshape(1, D),
        "w_up": f(inputs["w_up"][0]),
        "ffn_conv_w": f(inputs["ffn_conv_w"][0]),
        "ffn_conv_b": f(inputs["ffn_conv_b"][0]),
        "w_down": f(inputs["w_down"][0]),
        "norm_final_g": f(inputs["norm_final_g"]).reshape(1, D),
    }
    m.update(consts)
    return m


def assemble(results):
    n = len(results)
    g = lambda name: [np.asarray(r[name]) for r in results]
    y_p = np.stack(g("y_prompt"))
    y_s = np.concatenate([a.reshape(NSB, NST, D) for a in g("y_sample")])
    k_p = np.stack([a.reshape(SEQ, 8, 64) for a in g("k_prompt")])[None]
    v_p = np.stack([a.reshape(SEQ, 8, 64) for a in g("v_prompt")])[None]
    ki_p = np.stack(g("kidx_prompt"))[None]
    lc_p = np.stack(g("lru_conv_prompt"))[None]
    lh_p = np.stack([a.reshape(512) for a in g("lru_h_prompt")])[None]
    fc_p = np.stack(g("ffn_conv_prompt"))[None]
    k_s = np.concatenate([a.reshape(NSB, NST, 8, 64) for a in g("k_sample")])[None]
    v_s = np.concatenate([a.reshape(NSB, NST, 8, 64) for a in g("v_sample")])[None]
    ki_s = np.concatenate([a.reshape(NSB, NST, 64) for a in g("kidx_sample")])[None]
    lc_s = np.concatenate(g("lru_conv_sample"))[None]
    lh_s = np.concatenate(g("lru_h_sample"))[None]
    fc_s = np.concatenate(g("ffn_conv_sample"))[None]
    outs = (y_p, y_s, k_p, v_p, ki_p, lc_p, lh_p, fc_p, k_s, v_s, ki_s, lc_s, lh_s, fc_s)
    return tuple(np.ascontiguousarray(o, dtype=np.float32) for o in outs)


def kernel(**inputs):
    n_cores = 8
    n_phys = int(np.asarray(inputs["cache_k"]).shape[1])
    consts = _host_consts()
    nc = build(n_phys)
    in_maps = [make_in_map(inputs, c, consts) for c in range(n_cores)]
    res = run_bass_kernel_spmd(nc, in_maps, core_ids=list(range(n_cores)))
    return assemble(res.results)
```

```python
import numpy as np
import concourse.bass as bass
import concourse.mybir as mybir
from concourse.bass_utils import run_bass_kernel_spmd

F32 = mybir.dt.float32
BF16 = mybir.dt.bfloat16
I32 = mybir.dt.int32
U32 = mybir.dt.uint32
AF = mybir.ActivationFunctionType
ALU = mybir.AluOpType
AX = mybir.AxisListType

_OUTKEYS = ("out", "accum_out", "out_max", "out_indices", "ap")
SEM_LIMIT = 30000


class Buf:
    def __init__(self, t, kind):
        self.t = t
        self.kind = kind
        self.w = {}
        self.r = {}
        self.ld = None
        self.st = None

    def __getitem__(self, idx):
        return self.t[idx]


def _upd(d, sem, val):
    k = sem.name if hasattr(sem, "name") else id(sem)
    if k not in d or d[k][1] < val:
        d[k] = (sem, val)


class Eng:
    def __init__(self, K, name, eng):
        self.K = K
        self.name = name
        self.eng = eng
        self.sem = None
        self.cnt = 0
        self.nsem = 0
        self.waited = {}
        self.ninst = 0
        self.lazy = False
        self.lazy_self_ok = False
        self.pending = None

    def wait(self, deps):
        for k, (sem, val) in deps.items():
            if self.waited.get(k, 0) < val:
                owner = self.K.sem_owner.get(k)
                if owner is not None and owner.pending is not None and owner.sem is sem and val > owner.cnt:
                    if owner is self and self.lazy_self_ok:
                        continue
                    owner.flush()
                self.eng.wait_ge(sem, val)
                self.waited[k] = val

    def flush(self):
        if self.pending is not None:
            self.cnt += 1
            self.pending.then_inc(self.sem, 1)
            self.pending = None

    def tick(self, inst):
        if self.pending is None and (self.sem is None or self.cnt >= SEM_LIMIT):
            self.sem = self.K.nc.alloc_semaphore(f"s_{self.name}{self.nsem}")
            self.K.sem_owner[self.sem.name] = self
            self.nsem += 1
            self.cnt = 0
        self.ninst += 1
        if self.lazy:
            self.pending = inst
            return (self.sem, self.cnt + 1)
        self.cnt += 1
        inst.then_inc(self.sem, 1)
        return (self.sem, self.cnt)


class KB:
    def __init__(self, nc):
        self.nc = nc
        self.bufs = {}
        self.pe = Eng(self, "pe", nc.tensor)
        self.act = Eng(self, "act", nc.scalar)
        self.dve = Eng(self, "dve", nc.vector)
        self.pool = Eng(self, "pool", nc.gpsimd)
        self.sp = Eng(self, "sp", nc.sync)
        self.final = {}
        self.nsem_dma = 0
        self.sem_owner = {}
        self.pe.lazy = True
        self.pe.lazy_self_ok = True

    def sb(self, name, shape, dtype=F32, side=None):
        t = self.nc.alloc_sbuf_tensor(name, list(shape), dtype, side=side)
        b = Buf(t, "sb")
        self.bufs[t.name] = b
        return b

    def ps(self, name, shape, dtype=F32):
        t = self.nc.alloc_psum_tensor(name, list(shape), dtype)
        b = Buf(t, "ps")
        self.bufs[t.name] = b
        return b

    def dram(self, name, shape, dtype, kind):
        t = self.nc.dram_tensor(name, list(shape), dtype, kind=kind)
        b = Buf(t, "dram")
        self.bufs[t.name] = b
        return b

    def bufof(self, ap):
        return self.bufs[ap.tensor.name]

    def op(self, E, name, *args, **kw):
        reads, writes = [], []
        for k, v in kw.items():
            if hasattr(v, "tensor") and hasattr(v, "partition_size"):
                (writes if k in _OUTKEYS else reads).append(self.bufof(v))
        deps = {}
        for b in reads:
            for k, sv in b.w.items():
                _upd(deps, *sv)
        for b in writes:
            for k, sv in b.w.items():
                _upd(deps, *sv)
            for k, sv in b.r.items():
                _upd(deps, *sv)
        E.wait(deps)
        inst = getattr(E.eng, name)(*args, **kw)
        sv = E.tick(inst)
        for b in reads:
            _upd(b.r, *sv)
        for b in writes:
            _upd(b.w, *sv)
        return inst

    def new_group(self):
        rec = [self.nc.alloc_semaphore(f"grp{self.nsem_dma}"), 0, []]
        self.nsem_dma += 1
        return rec

    def close_group(self, rec):
        for (ob, ib) in rec[2]:
            _upd(ob.w, rec[0], rec[1])
            _upd(ib.r, rec[0], rec[1])
            if ob.kind == "dram":
                _upd(self.final, rec[0], rec[1])

    def dma(self, Q, out, in_, indirect=None, group=None, **kw):
        ob, ib = self.bufof(out), self.bufof(in_)
        extra_reads = []
        if indirect is not None:
            extra_reads.append(self.bufof(indirect))
        deps = {}
        for b in [ib] + extra_reads:
            for k, sv in b.w.items():
                _upd(deps, *sv)
        for k, sv in ob.w.items():
            _upd(deps, *sv)
        for k, sv in ob.r.items():
            _upd(deps, *sv)
        Q.wait(deps)
        if group is not None:
            rec = group
            group[2].append((ob, ib))
        elif ob.kind == "sb":
            if ob.ld is None:
                ob.ld = [self.nc.alloc_semaphore(f"ld{self.nsem_dma}"), 0]
                self.nsem_dma += 1
            rec = ob.ld
        else:
            if ib.st is None:
                ib.st = [self.nc.alloc_semaphore(f"st{self.nsem_dma}"), 0]
                self.nsem_dma += 1
            rec = ib.st
        rec[1] += 16
        if indirect is not None:
            inst = Q.eng.indirect_dma_start(out=out, out_offset=None, in_=in_, in_offset=kw.pop("in_offset"), **kw)
        else:
            inst = Q.eng.dma_start(out=out, in_=in_, **kw)
        inst.then_inc(rec[0], 16)
        sv = (rec[0], rec[1])
        _upd(ob.w, *sv)
        for b in [ib] + extra_reads:
            _upd(b.r, *sv)
        if ob.kind == "dram":
            _upd(self.final, *sv)
        return inst

    def barrier(self):
        deps = {}
        for E in (self.pe, self.act, self.dve, self.pool, self.sp):
            E.flush()
            if E.sem is not None:
                _upd(deps, E.sem, E.cnt)
        for b in self.bufs.values():
            for rec in (b.ld, b.st):
                if rec is not None:
                    _upd(deps, rec[0], rec[1])
        for E in (self.pe, self.act, self.dve, self.pool, self.sp):
            E.wait(deps)

    def scoped(self, stack, name, shape, dtype=F32):
        t = stack.enter_context(self.nc.sbuf_tensor(name, list(shape), dtype))
        b = Buf(t, "sb")
        self.bufs[t.name] = b
        return b

    def finish(self):
        for E in (self.pe, self.act, self.dve, self.pool):
            E.flush()
        self.sp.wait(self.final)
        self.sp.eng.nop() if False else None

import contextlib
import ml_dtypes

D = 1024
SEQ = 2048
NT = SEQ // 128
NSB = 4
NST = 4
NS = NSB * NST
PAST = 16384
NPAGES = 128
DIN = 5192
DFF = 3072
EPS = 1e-6
NBIS = 20
NEG = -1.0e30
CANDR = 4
NC_ = CANDR * 8


def build(n_phys, do_sample=True, do_ffn=True):
    nc = bass.Bass("TRN2", target_bir_lowering=False)
    K = KB(nc)
    op, dma = K.op, K.dma
    PE, ACT, DVE, POOL, SP = K.pe, K.act, K.dve, K.pool, K.sp

    def din(name, shape, dt=F32):
        return K.dram(name, shape, dt, "ExternalInput")

    def dout(name, shape, dt=F32):
        return K.dram(name, shape, dt, "ExternalOutput")

    x_p = din("x_prompt", [SEQ, D])
    x_s = din("x_sample", [NS, D])
    c_k = din("cache_k", [n_phys * 128, 512])
    c_v = din("cache_v", [n_phys * 128, 512])
    c_i = din("cache_kidx", [n_phys * 128, 64])
    p_t = din("page_table", [NSB, NPAGES], I32)
    s_lc = din("state_lru_conv", [NSB, 3, 512])
    s_lh = din("state_lru_h", [NSB, 512])
    s_fc = din("state_ffn_conv", [NSB, 2, DFF])
    g_mix = din("norm_mix_g", [1, D])
    w_in = din("w_in", [D, DIN])
    l_cw = din("lru_conv_w", [4, 512])
    l_cb = din("lru_conv_b", [512])
    l_wa = din("lru_wa", [8, 64, 64])
    l_ba = din("lru_ba", [512])
    l_wx = din("lru_wx", [8, 64, 64])
    l_bx = din("lru_bx", [512])
    l_lam = din("lru_lambda", [512])
    w_pa = din("w_proj_a", [512, D])
    w_pb = din("w_proj_b", [512, D])
    w_o = din("w_out", [D, D])
    g_ffn = din("norm_ffn_g", [1, D])
    w_up = din("w_up", [D, 2 * DFF])
    f_cw = din("ffn_conv_w", [3, DFF])
    f_cb = din("ffn_conv_b", [DFF])
    w_dn = din("w_down", [DFF, D])
    g_fin = din("norm_final_g", [1, D])
    c_idb = din("c_ident_bf", [128, 128], BF16)
    c_idf = din("c_ident_f", [128, 128])
    c_cosp = din("c_cos_p", [SEQ, 32])
    c_sinp = din("c_sin_p", [SEQ, 32])
    c_coss = din("c_cos_s", [NS, 32])
    c_sins = din("c_sin_s", [NS, 32])
    c_cmask = din("c_cmask", [128, 128])
    c_pow2 = din("c_pow2", [128, NBIS + 2])
    c_misc = din("c_misc", [128, 1024])
    c_blk = din("c_blk", [128, 128], BF16)
    c_p0 = din("c_p0", [32, 32 * 128], BF16)
    c_selq = din("c_selq", [NS, NSB * 128])
    y_p = dout("y_prompt", [SEQ, D])
    y_s = dout("y_sample", [NS, D])
    k_p = dout("k_prompt", [SEQ, 512])
    v_p = dout("v_prompt", [SEQ, 512])
    ki_p = dout("kidx_prompt", [SEQ, 64])
    lc_p = dout("lru_conv_prompt", [3, 512])
    lh_p = dout("lru_h_prompt", [512])
    fc_p = dout("ffn_conv_prompt", [2, DFF])
    k_s = dout("k_sample", [NS, 512])
    v_s = dout("v_sample", [NS, 512])
    ki_s = dout("kidx_sample", [NS, 64])
    lc_s = dout("lru_conv_sample", [NSB, 3, 512])
    lh_s = dout("lru_h_sample", [NSB, 512])
    fc_s = dout("ffn_conv_sample", [NSB, 2, DFF])
    x1_d = K.dram("x1_scratch", [SEQ + NS, D], F32, "Internal")
    wi_d = K.dram("wi_scratch", [NS, 8], F32, "Internal")

    pT = [K.ps(f"pT{i}", [128, 1024], BF16) for i in range(2)]
    OB = [K.ps(f"OB{i}", [128, 512], F32) for i in range(2)]
    GB = [K.ps(f"GB{i}", [128, 512], F32) for i in range(4)]
    rr = {"g": 0, "t": 0}

    def gbank():
        rr["g"] = (rr["g"] + 1) % 4
        return GB[rr["g"]]

    tb_mode = {"single": False}

    def tbank():
        if tb_mode["single"]:
            return pT[0]
        rr["t"] = (rr["t"] + 1) % 2
        return pT[rr["t"]]

    idb = K.sb("idb", [128, 128], BF16)
    idf = K.sb("idf", [128, 128], F32)
    cmask = K.sb("cmask", [128, 128], F32)
    pow2 = K.sb("pow2", [128, NBIS + 2], F32)
    gbc = K.sb("gbc", [128, D], F32)
    gbc2 = None
    dma(SP, out=idb[:], in_=c_idb[:])
    dma(SP, out=idf[:], in_=c_idf[:])
    dma(SP, out=cmask[:], in_=c_cmask[:])
    dma(SP, out=pow2[:], in_=c_pow2[:])
    dma(SP, out=gbc[:], in_=g_mix[:].broadcast_to([128, D]))


    lcw = K.sb("lcw", [128, 4, 4], F32)
    lcb = K.sb("lcb", [128, 4], F32)
    lba = K.sb("lba", [128, 4], F32)
    lbx = K.sb("lbx", [128, 4], F32)
    lcl = K.sb("lcl", [128, 4], F32)
    fcw = K.sb("fcw", [128, 3, 24], F32)
    fcb = K.sb("fcb", [128, 24], F32)
    def dma_cols(dram2d, c, sb_ap, load, group=None):
        d = dram2d[:, c * 128:(c + 1) * 128].rearrange("j p -> p j")
        if load:
            dma(SP, out=sb_ap, in_=d, allow_slow_non_contiguous=True, group=group)
        else:
            dma(SP, out=d, in_=sb_ap, allow_slow_non_contiguous=True, group=group)

    for c in range(4):
        dma_cols(l_cw, c, lcw[:, :, c], True)
    dma(SP, out=lcb[:], in_=l_cb[:].rearrange("(c p) -> p c", p=128), allow_slow_non_contiguous=True)
    dma(SP, out=lba[:], in_=l_ba[:].rearrange("(c p) -> p c", p=128), allow_slow_non_contiguous=True)
    dma(SP, out=lbx[:], in_=l_bx[:].rearrange("(c p) -> p c", p=128), allow_slow_non_contiguous=True)
    dma(SP, out=lcl[:], in_=l_lam[:].rearrange("(c p) -> p c", p=128), allow_slow_non_contiguous=True)
    for c in range(24):
        dma_cols(f_cw, c, fcw[:, :, c], True)
    dma(SP, out=fcb[:], in_=f_cb[:].rearrange("(c p) -> p c", p=128), allow_slow_non_contiguous=True)
    op(ACT, "activation", out=lcl[:], in_=lcl[:], func=AF.Exp, scale=-1.0)
    op(ACT, "activation", out=lcl[:], in_=lcl[:], func=AF.Ln, bias=1.0)
    op(DVE, "tensor_scalar", out=lcl[:], in0=lcl[:], scalar1=-8.0, scalar2=None, op0=ALU.mult)

    ss = K.sb("ss", [128, 1], F32)
    rstd = K.sb("rstd", [128, 1], F32)


    def rmsnorm_to_bf(xt, n, g, hb, junk=None):
        junk = hb if junk is None else junk
        op(ACT, "activation", out=junk[0:n, :], in_=xt[0:n, :], func=AF.Square, accum_out=ss[0:n, :])
        op(DVE, "tensor_scalar", out=rstd[0:n, :], in0=ss[0:n, :], scalar1=1.0 / D, scalar2=EPS,
           op0=ALU.mult, op1=ALU.add)
        op(ACT, "activation", out=rstd[0:n, :], in_=rstd[0:n, :], func=AF.Sqrt)
        op(DVE, "reciprocal", out=rstd[0:n, :], in_=rstd[0:n, :])
        op(DVE, "scalar_tensor_tensor", out=hb[0:n, :], in0=xt[0:n, :], scalar=rstd[0:n, 0:1], in1=g[0:n, :],
           op0=ALU.mult, op1=ALU.mult)

    def load_cast_w(dst_fn, src_fn, nk, ncols, stagings):
        idx = 0
        for kc in range(nk):
            for c0 in range(0, ncols, 1024):
                w = min(1024, ncols - c0)
                if idx % 2 == 0:
                    dma(POOL, out=dst_fn(kc, c0, w), in_=src_fn(kc, c0, w))
                else:
                    st = stagings[(idx // 2) % len(stagings)]
                    dma(SP, out=st[:, 0:w], in_=src_fn(kc, c0, w))
                    op(ACT if (idx // 2) % 2 == 0 else DVE, "copy" if (idx // 2) % 2 == 0 else "tensor_copy",
                       out=dst_fn(kc, c0, w), in_=st[:, 0:w])
                idx += 1

    def transpose_cols(src, n, ncols, dst_fn):
        nb = ncols // 128
        for c0 in range(0, nb, 8):
            tb = tbank()
            m = min(8, nb - c0)
            for c in range(m):
                op(PE, "transpose", out=tb[:, c * 128:c * 128 + n], in_=src[0:n, (c0 + c) * 128:(c0 + c + 1) * 128],
                   identity=idb[0:n, 0:n])
            for c in range(m):
                op(ACT, "copy", out=dst_fn(c0 + c), in_=tb[:, c * 128:c * 128 + n])

    def rope(src_ps, n, nh, cosb, sinb, dst, tmp):
        v = src_ps.rearrange("p (h two d) -> p h two d", h=nh, two=2)
        x1, x2 = v[:, :, 0, :], v[:, :, 1, :]
        cb = cosb[0:n, :].unsqueeze(1).broadcast_to([n, nh, 32])
        sb_ = sinb[0:n, :].unsqueeze(1).broadcast_to([n, nh, 32])
        t1, t2 = tmp[0][0:n, 0:nh, :], tmp[1][0:n, 0:nh, :]
        op(DVE, "tensor_tensor", out=t1, in0=x1, in1=cb, op=ALU.mult)
        op(DVE, "tensor_tensor", out=t2, in0=x2, in1=sb_, op=ALU.mult)
        op(DVE, "tensor_tensor", out=dst[:, :, 0:32], in0=t1, in1=t2, op=ALU.subtract)
        op(DVE, "tensor_tensor", out=t1, in0=x2, in1=cb, op=ALU.mult)
        op(DVE, "tensor_tensor", out=t2, in0=x1, in1=sb_, op=ALU.mult)
        op(DVE, "tensor_tensor", out=dst[:, :, 32:64], in0=t1, in1=t2, op=ALU.add)

    with contextlib.ExitStack() as so:
        attA = K.scoped(so, "attA", [128, 4, SEQ], BF16)
        s_q = K.scoped(so, "s_q", [NS, 512], F32)
        s_qi = K.scoped(so, "s_qi", [NS, 8, 128], BF16)
        s_kib = K.scoped(so, "s_kib", [NS, 64], BF16)
        s_wi = K.scoped(so, "s_wi", [NS, 8], F32)
        s_attT = K.scoped(so, "s_attT", [128, 4, NS], BF16)
        xt = K.scoped(so, "xt", [128, D], F32)
        hb = K.scoped(so, "hb", [128, D], BF16)
        hT = K.scoped(so, "hT", [128, 8, 128], BF16)

        def load_norm_T(n, xsrc):
            dma(SP, out=xt[0:n, :], in_=xsrc)
            rmsnorm_to_bf(xt, n, gbc, hb)
            transpose_cols(hb, n, D, lambda c: hT[:, c, 0:n])

        with contextlib.ExitStack() as s1:
            NA = 2120
            win = K.scoped(s1, "win_a", [128, 8, NA], BF16)
            for kc in range(8):
                for c0 in range(0, NA, 1060):
                    dma(POOL, out=win[:, kc, c0:c0 + 1060], in_=w_in[kc * 128:(kc + 1) * 128, c0:c0 + 1060])
            KTb = [K.scoped(s1, f"KT{j}", [128, 4, 128], BF16) for j in range(NT)]
            VAb = [K.scoped(s1, f"VA{j}", [128, 8, 65], BF16) for j in range(NT)]
            kiT = K.scoped(s1, "kiT", [128, SEQ], BF16)
            for j in range(NT):
                op(POOL, "memset", ap=VAb[j][:], constant=1.0)
            rt = [K.scoped(s1, f"rt{i}", [128, 8, 32], F32) for i in range(2)]
            cosb = K.scoped(s1, "cosb", [128, 32], F32)
            sinb = K.scoped(s1, "sinb", [128, 32], F32)
            qb = K.scoped(s1, "qb", [128, 8, 64], BF16)
            qTs = [K.scoped(s1, f"qT{i}", [128, 4, 128], BF16) for i in range(3)]
            kf = K.scoped(s1, "kf", [128, 8, 64], F32)
            kb = K.scoped(s1, "kb", [128, 512], BF16)
            vf = K.scoped(s1, "vf", [128, 512], F32)
            qib = K.scoped(s1, "qib", [128, 8, 64], BF16)
            qiT = K.scoped(s1, "qiT", [128, 4, 128], BF16)
            kif = K.scoped(s1, "kif", [128, 1, 64], F32)
            kib2 = K.scoped(s1, "kib2", [128, 128], BF16)
            wif = K.scoped(s1, "wif", [128, 8], F32)
            dg = K.scoped(s1, "dg", [128, 8, 128], BF16)
            Rh = [K.scoped(s1, f"Rh{i}", [128, 512], BF16) for i in range(4)]
            scs = [K.scoped(s1, f"sc{i}", [128, SEQ], F32) for i in range(2)]
            bis = K.scoped(s1, "bis", [128, 8], F32)
            dl = K.scoped(s1, "dl", [128, NBIS + 2], F32)
            mk = K.scoped(s1, "mk", [128, SEQ], BF16)
            mkTs = [K.scoped(s1, f"mkT{i}", [128, NT, 128], BF16) for i in range(2)]
            PTb = [K.scoped(s1, f"PTb{i}", [128, 512], BF16) for i in range(3)]
            pT1f = pT[1][:].bitcast(F32)
            rcp = K.scoped(s1, "rcp", [128, 8, 1], F32)
            att = K.scoped(s1, "att", [128, 8, 64], BF16)
            rrR = {"r": 0, "p": 0, "ga": 0, "gc": 0}
            GA = [GB[0], GB[1]]
            SCB = GB[2]
            GC = [GB[3], GB[3]]
            tb_mode["single"] = True

            def gbankA():
                rrR["ga"] = (rrR["ga"] + 1) % 2
                return GA[rrR["ga"]]

            def front_a(n, xsrc, cos_d, sin_d, is_prompt, i):
                load_norm_T(n, xsrc)
                dma(SP, out=cosb[0:n, :], in_=cos_d)
                dma(SP, out=sinb[0:n, :], in_=sin_d)
                yield

                def tm_group(c0, c1):
                    g = gbankA()
                    for kc in range(8):
                        op(PE, "matmul", out=g[0:n, 0:c1 - c0], lhsT=hT[:, kc, 0:n], rhs=win[:, kc, c0:c1],
                           start=(kc == 0), stop=(kc == 7))
                    return g

                g = tm_group(0, 512)
                if is_prompt:
                    rope(g[0:n, :], n, 8, cosb, sinb, qb[0:n], rt)
                else:
                    rope(g[0:n, :], n, 8, cosb, sinb, s_q[0:n, :].rearrange("p (h d) -> p h d", h=8), rt)
                yield
                g = tm_group(512, 1024)
                rope(g[0:n, :], n, 8, cosb, sinb, kf[0:n], rt)
                if is_prompt:
                    dma(SP, out=k_p[i * 128:(i + 1) * 128, :], in_=kf[:].rearrange("p h d -> p (h d)"))
                    op(POOL, "tensor_copy", out=kb[0:n, :], in_=kf[:].rearrange("p h d -> p (h d)"))
                else:
                    dma(SP, out=k_s[:, :], in_=kf[0:n].rearrange("p h d -> p (h d)"))
                yield
                g = tm_group(1024, 1536)
                op(ACT, "copy", out=vf[0:n, :], in_=g[0:n, :])
                if is_prompt:
                    dma(SP, out=v_p[i * 128:(i + 1) * 128, :], in_=vf[:, :])
                    op(POOL, "tensor_copy", out=VAb[i][:, :, 0:64], in_=vf[:].rearrange("p (h d) -> p h d", h=8))
                else:
                    dma(SP, out=v_s[:, :], in_=vf[0:n, :])
                yield
                g = tm_group(1536, 2048)
                rope(g[0:n, :], n, 8, cosb, sinb, (qib if is_prompt else s_qi)[0:n], rt)
                yield
                g = tm_group(2048, 2120)
                rope(g[0:n, 0:64], n, 1, cosb, sinb, kif[0:n], rt)
                if is_prompt:
                    dma(SP, out=ki_p[i * 128:(i + 1) * 128, :], in_=kif[:, 0, :])
                    op(POOL, "tensor_copy", out=kib2[0:n, 0:64], in_=kif[0:n, 0, :])
                    op(POOL, "tensor_copy", out=kib2[0:n, 64:128], in_=kif[0:n, 0, :])
                    op(ACT, "copy", out=wif[0:n, :], in_=g[0:n, 64:72])
                else:
                    dma(SP, out=ki_s[:, :], in_=kif[0:n, 0, :])
                    op(POOL, "tensor_copy", out=s_kib[0:n, :], in_=kif[0:n, 0, :])
                    op(POOL, "tensor_copy", out=s_qi[0:n, :, 64:128], in_=s_qi[0:n, :, 0:64])
                    op(ACT, "copy", out=s_wi[0:n, :], in_=g[0:n, 64:72])
                yield

            if do_sample:
                for _ in front_a(NS, x_s[:, :], c_coss[:, :], c_sins[:, :], False, 0):
                    pass

            def stageA(i):
                n = 128
                L = 128 * (i + 1)
                sc = scs[i % 2]
                qT = qTs[i % 3]
                yield from front_a(n, x_p[i * 128:(i + 1) * 128, :], c_cosp[i * 128:(i + 1) * 128, :],
                                   c_sinp[i * 128:(i + 1) * 128, :], True, i)
                transpose_cols(qb[:].rearrange("p h d -> p (h d)"), n, 512, lambda c: qT[:, c, :])
                yield
                transpose_cols(kb, n, 512, lambda c: KTb[i][:, c, :])
                yield
                transpose_cols(qib[:].rearrange("p h d -> p (h d)"), n, 512, lambda c: qiT[:, c, :])
                transpose_cols(kib2, n, 128, lambda c: kiT[:, i * 128:(i + 1) * 128])
                for h in range(8):
                    op(POOL, "tensor_scalar", out=dg[:, h, :], in0=idf[:, :], scalar1=wif[:, h:h + 1], scalar2=1.0,
                       op0=ALU.mult, op1=ALU.mult)
                yield
                for c0 in range(0, L, 512):
                    c1 = min(L, c0 + 512)
                    wd = c1 - c0
                    acc = SCB
                    gs = {}

                    def issueS(h):
                        pr, hf = h // 2, h % 2
                        g = gbankA()
                        op(PE, "matmul", out=g[:, 0:wd], lhsT=qiT[hf * 64:(hf + 1) * 64, pr, :],
                           rhs=kiT[hf * 64:(hf + 1) * 64, c0:c1], start=True, stop=True)
                        gs[h] = g

                    issueS(0)
                    Rs_ = {}
                    for h in range(8):
                        if h + 1 < 8:
                            issueS(h + 1)
                        Rs_[h] = Rh[h % 4]
                        op(ACT, "activation", out=Rs_[h][:, 0:wd], in_=gs[h][:, 0:wd], func=AF.Relu)
                        if h >= 1:
                            op(PE, "matmul", out=acc[:, 0:wd], lhsT=dg[:, h - 1, :], rhs=Rs_[h - 1][:, 0:wd],
                               start=(h - 1 == 0), stop=False)
                        if h % 2 == 1:
                            yield
                    op(PE, "matmul", out=acc[:, 0:wd], lhsT=dg[:, 7, :], rhs=Rs_[7][:, 0:wd], start=False, stop=True)
                    if c1 == L:
                        if wd > 128:
                            op(ACT, "copy", out=sc[:, c0:c1 - 128], in_=acc[:, 0:wd - 128])
                        op(DVE, "tensor_tensor", out=sc[:, c1 - 128:c1], in0=acc[:, wd - 128:wd], in1=cmask[:, :], op=ALU.add)
                    else:
                        op(ACT, "copy", out=sc[:, c0:c1], in_=acc[:, 0:wd])
                    yield

            def stageB(i):
                n = 128
                L = 128 * (i + 1)
                sc = scs[i % 2]
                mkT = mkTs[i % 2]
                tcol = bis[:, 3:4]
                if i < 2:
                    op(DVE, "memset", ap=tcol, constant=-1.0e29)
                else:
                    op(DVE, "tensor_reduce", out=bis[:, 0:1], in_=sc[:, 0:L], axis=AX.X, op=ALU.max)
                    op(DVE, "tensor_reduce", out=bis[:, 1:2], in_=sc[:, 0:L - 128], axis=AX.X, op=ALU.min)
                    yield
                    op(DVE, "tensor_tensor", out=bis[:, 2:3], in0=bis[:, 0:1], in1=bis[:, 1:2], op=ALU.subtract)
                    op(DVE, "tensor_scalar", out=dl[:, :], in0=pow2[:, :], scalar1=bis[:, 2:3], scalar2=None, op0=ALU.mult)
                    op(DVE, "tensor_tensor", out=tcol, in0=bis[:, 1:2], in1=dl[:, 1:2], op=ALU.add)
                    yield
                    for k in range(2, NBIS + 2):
                        op(DVE, "tensor_scalar", out=mk[:, 0:L], in0=sc[:, 0:L], scalar1=tcol, scalar2=None,
                           op0=ALU.is_ge, op1=ALU.add, accum_out=bis[:, 4:5])
                        op(DVE, "tensor_scalar", out=bis[:, 5:6], in0=bis[:, 4:5], scalar1=255.5, scalar2=dl[:, k - 1:k],
                           op0=ALU.is_ge, op1=ALU.mult)
                        op(DVE, "scalar_tensor_tensor", out=tcol, in0=tcol, scalar=dl[:, k:k + 1], in1=bis[:, 5:6],
                           op0=ALU.subtract, op1=ALU.add)
                        yield
                    op(DVE, "tensor_tensor", out=tcol, in0=tcol, in1=dl[:, NBIS + 1:NBIS + 2], op=ALU.subtract)
                    yield
                op(DVE, "tensor_scalar", out=mk[:, 0:L], in0=sc[:, 0:L], scalar1=tcol, scalar2=None, op0=ALU.is_ge)
                yield
                for c0 in range(0, i + 1, 8):
                    transpose_cols(mk[:, c0 * 128:min(L, (c0 + 8) * 128)], n, min(L, (c0 + 8) * 128) - c0 * 128,
                                   lambda c: mkT[:, c0 + c, :])
                    yield

            def stageC(i):
                n = 128
                qT = qTs[i % 3]
                mkT = mkTs[i % 2]
                units = [(h, j0, min(i + 1, j0 + 4)) for h in range(8) for j0 in range(0, i + 1, 4)]
                GC = [GB[3], pT1f]
                sg, Pu = {}, {}

                def issueST(u):
                    h, j0, j1 = units[u]
                    pr, hf = h // 2, h % 2
                    g = GC[u % 2]
                    for j in range(j0, j1):
                        op(PE, "matmul", out=g[:, (j - j0) * 128:(j - j0 + 1) * 128],
                           lhsT=KTb[j][hf * 64:(hf + 1) * 64, pr, :],
                           rhs=qT[hf * 64:(hf + 1) * 64, pr, :], start=True, stop=True)
                    sg[u] = g

                def issuePV(u):
                    h, j0, j1 = units[u]
                    ob = OB[h // 4]
                    for j in range(j0, j1):
                        op(PE, "matmul", out=ob[:, (h % 4) * 65:(h % 4) * 65 + 65], lhsT=Pu[u][:, (j - j0) * 128:(j - j0 + 1) * 128],
                           rhs=VAb[j][:, h, :], start=(j == 0), stop=(j == i))

                issueST(0)
                yield
                for u, (h, j0, j1) in enumerate(units):
                    if u + 1 < len(units):
                        issueST(u + 1)
                    g = sg.pop(u)
                    Pu[u] = PTb[u % 3]
                    wd = (j1 - j0) * 128
                    op(ACT, "activation", out=Pu[u][:, 0:wd], in_=g[:, 0:wd], func=AF.Exp, scale=0.125)
                    op(POOL, "tensor_tensor", out=Pu[u][:, 0:wd], in0=Pu[u][:, 0:wd],
                       in1=mkT[:, j0:j1, :].rearrange("p j q -> p (j q)"), op=ALU.mult)
                    if u >= 1:
                        issuePV(u - 1)
                    yield
                issuePV(len(units) - 1)
                yield
                for hh in range(2):
                    ov = OB[hh][:, 0:260].rearrange("p (h d) -> p h d", h=4)
                    op(DVE, "reciprocal", out=rcp[:, hh * 4:hh * 4 + 4, :], in_=ov[:, :, 64:65])
                    op(DVE, "tensor_tensor", out=att[:, hh * 4:hh * 4 + 4, :], in0=ov[:, :, 0:64],
                       in1=rcp[:, hh * 4:hh * 4 + 4, :].broadcast_to([128, 4, 64]), op=ALU.mult)
                yield
                transpose_cols(att[:].rearrange("p h d -> p (h d)"), n, 512, lambda c: attA[:, c, i * 128:(i + 1) * 128])
                yield

            stages = [stageA, stageB, stageC]
            for step in range(NT + len(stages) - 1):
                alive = [stages[s](step - s) for s in range(len(stages)) if 0 <= step - s < NT]
                while alive:
                    nxt = []
                    for gnr in alive:
                        try:
                            next(gnr)
                            nxt.append(gnr)
                        except StopIteration:
                            pass
                    alive = nxt
            tb_mode["single"] = False
        K.barrier()

        wdn = K.sb("wdn", [128, 24, D], BF16, side="right")
        for c in range(24):
            dma(POOL, out=wdn[:, c, :], in_=w_dn[c * 128:(c + 1) * 128, :])
        if do_sample:
            with contextlib.ExitStack() as s3:
                NBS = 34
                BR = 16384.0
                misc = K.scoped(s3, "misc", [128, 1024], F32)
                blk = K.scoped(s3, "blk", [128, 128], BF16)
                selq = K.scoped(s3, "selq", [NS, NSB, 128], F32)
                dma(SP, out=misc[:], in_=c_misc[:])
                dma(SP, out=blk[:], in_=c_blk[:])
                dma(SP, out=selq[:], in_=c_selq[:].rearrange("t (b m) -> t b m", b=NSB))
                slotbase = misc[:, 0:1]
                negnew = misc[:, 1:5]
                iota_pg = misc[:, 16:144]
                agg = misc[:, 160:224].rearrange("p (b t) -> p b t", b=NSB)
                qiTs = K.scoped(s3, "qiTs", [128, NSB, 8 * NST], BF16)
                tb = tbank()
                for h in range(8):
                    op(PE, "transpose", out=tb[:, h * NS:(h + 1) * NS], in_=s_qi[0:NS, h, :], identity=idb[0:NS, 0:NS])
                op(ACT, "copy", out=qiTs[:].rearrange("p b (h t) -> p h b t", h=8), in_=tb[:, 0:8 * NS].rearrange("p (h b t) -> p h b t", h=8, b=NSB))
                kinT = K.scoped(s3, "kinT", [64, NS], BF16)
                tb = tbank()
                op(PE, "transpose", out=tb[0:64, 0:NS], in_=s_kib[0:NS, :], identity=idb[0:NS, 0:NS])
                op(ACT, "copy", out=kinT[:, :], in_=tb[0:64, 0:NS])
                wcol = K.scoped(s3, "wcol", [32, NSB], F32)
                dma(SP, out=wi_d[:, :], in_=s_wi[:, :])
                for h in range(8):
                    dma(SP, out=wcol[h * 4:(h + 1) * 4, :], in_=wi_d[:, h].rearrange("(b t) -> t b", b=NSB),
                        allow_slow_non_contiguous=True)
                ptc = K.scoped(s3, "ptc", [128, NSB], I32)
                for b in range(NSB):
                    dma(SP, out=ptc[:, b:b + 1], in_=p_t[b, :].rearrange("(j o) -> j o", o=1), allow_slow_non_contiguous=True)
                ptri = K.scoped(s3, "ptri", [128, NSB * 128], I32)
                dma(SP, out=ptri[:, :], in_=p_t[:, :].rearrange("(o b) j -> o (b j)", o=1).broadcast_to([128, NSB * 128]))
                ptrf = K.scoped(s3, "ptrf", [128, NSB, 128], F32)
                op(DVE, "tensor_copy", out=ptrf[:].rearrange("p b j -> p (b j)"), in_=ptri[:, :])
                Rs = [K.scoped(s3, f"Rs{i}", [32, 512], BF16) for i in range(4)]
                scw = K.scoped(s3, "scw", [128, 512], F32)
                candv = K.scoped(s3, "candv", [128, NSB, 36], F32)
                candi = K.scoped(s3, "candi", [128, NSB, 32], U32)
                s3a = contextlib.ExitStack()
                s3a.__enter__()
                p0 = K.scoped(s3a, "p0", [32, 32 * 128], BF16)
                dma(SP, out=p0[:], in_=c_p0[:])
                Wp = K.scoped(s3a, "Wp", [32, 32, 128], BF16)
                pg = K.scoped(s3a, "pg", [128, 8192], F32)
                pgb = K.scoped(s3a, "pgb", [128, 8192], BF16)
                kTs = K.scoped(s3a, "kTs", [128, 64, 128], BF16)
                rsx = 0
                for b in range(NSB):
                    op(DVE, "tensor_scalar", out=Wp[:].rearrange("p s m -> p (s m)"), in0=p0[:, :], scalar1=wcol[:, b:b + 1],
                       scalar2=None, op0=ALU.mult)
                    dma(POOL, out=pg[:, :], in_=c_i[:, :].rearrange("(n s) d -> n (s d)", s=128), indirect=ptc[:, b:b + 1],
                        in_offset=bass.IndirectOffsetOnAxis(ap=ptc[:, b:b + 1], axis=0))
                    op(DVE, "tensor_copy", out=pgb[:, :], in_=pg[:, :])
                    for s0 in range(0, 64, 8):
                        tb = tbank()
                        for s_ in range(8):
                            op(PE, "transpose", out=tb[:, s_ * 128:(s_ + 1) * 128],
                               in_=pgb[:, (s0 + s_) * 128:(s0 + s_ + 1) * 128], identity=idb[:, :])
                        op(ACT, "copy", out=kTs[:, s0:s0 + 8, :], in_=tb[:, :].rearrange("p (s j) -> p s j", s=8))
                    accb = OB[0]
                    for seg in range(32):
                        gq, e = seg // 2, seg % 2
                        g = gbank()
                        op(PE, "matmul", out=g[0:32, :], lhsT=qiTs[e * 64:(e + 1) * 64, b, :],
                           rhs=kTs[e * 64:(e + 1) * 64, gq * 4:(gq + 1) * 4, :], start=True, stop=True)
                        rsx = (rsx + 1) % 2
                        op(ACT, "activation", out=Rs[rsx][:, :], in_=g[0:32, :], func=AF.Relu)
                        op(PE, "matmul", out=accb[:, :], lhsT=Wp[:, seg, :], rhs=Rs[rsx][:, :], start=(seg == 0), stop=(seg == 31))
                    op(ACT, "copy", out=scw[:, :], in_=accb[:, :])
                    g = gbank()
                    op(PE, "matmul", out=g[0:32, 0:NST], lhsT=qiTs[0:64, b, :],
                       rhs=kinT[:, b * NST:(b + 1) * NST], start=True, stop=True)
                    rsx = (rsx + 1) % 2
                    op(ACT, "activation", out=Rs[rsx][:, 0:NST], in_=g[0:32, 0:NST], func=AF.Relu)
                    g2 = gbank()
                    op(PE, "matmul", out=g2[:, 0:NST], lhsT=Wp[:, 0, :], rhs=Rs[rsx][:, 0:NST], start=True, stop=True)
                    op(DVE, "tensor_tensor", out=candv[:, b, 32:36], in0=g2[:, 0:NST], in1=negnew, op=ALU.add)
                    for r in range(CANDR):
                        op(DVE, "max", out=candv[:, b, r * 8:(r + 1) * 8], in_=scw[:, :])
                        op(DVE, "max_index", out=candi[:, b, r * 8:(r + 1) * 8], in_max=candv[:, b, r * 8:(r + 1) * 8], in_values=scw[:, :])
                        if r + 1 < CANDR:
                            op(DVE, "match_replace", out=scw[:, :], in_to_replace=candv[:, b, r * 8:(r + 1) * 8], in_values=scw[:, :],
                               imm_value=NEG)
                s3a.__exit__(None, None, None)
                K.barrier()
                thr = K.scoped(s3, "thr", [128, NSB, 1], F32)
                cmpb = K.scoped(s3, "cmpb", [128, NSB, 36], F32)
                cnt = K.scoped(s3, "cnt", [128, NSB], F32)
                cntb = K.scoped(s3, "cntb", [128, NSB], BF16)
                stp = K.scoped(s3, "stp", [128, NSB], F32)
                op(DVE, "memset", ap=thr[:], constant=0.0)
                for k in range(1, NBS + 1):
                    dlt = BR * (2.0 ** -k)
                    op(DVE, "tensor_tensor", out=cmpb[:], in0=candv[:], in1=thr[:].broadcast_to([128, NSB, 36]), op=ALU.is_ge)
                    op(DVE, "tensor_reduce", out=cnt[:, :], in_=cmpb[:], axis=AX.X, op=ALU.add)
                    op(DVE, "tensor_copy", out=cntb[:, :], in_=cnt[:, :])
                    g = gbank()
                    op(PE, "matmul", out=g[:, 0:NSB], lhsT=blk[:, :], rhs=cntb[:, :], start=True, stop=True)
                    op(DVE, "tensor_scalar", out=stp[:, :], in0=g[:, 0:NSB], scalar1=255.5, scalar2=2.0 * dlt, op0=ALU.is_ge, op1=ALU.mult)
                    op(DVE, "scalar_tensor_tensor", out=thr[:, :, 0], in0=thr[:, :, 0], scalar=-dlt, in1=stp[:, :], op0=ALU.add, op1=ALU.add)
                op(DVE, "tensor_scalar", out=thr[:, :, 0], in0=thr[:, :, 0], scalar1=-BR * (2.0 ** -NBS), scalar2=None, op0=ALU.add)
                sel = K.scoped(s3, "sel", [128, NSB, 36], F32)
                op(DVE, "tensor_tensor", out=sel[:], in0=candv[:], in1=thr[:].broadcast_to([128, NSB, 36]), op=ALU.is_ge)
                idxf = K.scoped(s3, "idxf", [128, NSB, 32], F32)
                spl = K.scoped(s3, "spl", [128, NSB, 32], F32)
                tmp3 = K.scoped(s3, "tmp3", [128, NSB, 32], F32)
                pgf = K.scoped(s3, "pgf", [128, NSB, 32], F32)
                op(DVE, "tensor_copy", out=idxf[:], in_=candi[:])
                op(DVE, "tensor_scalar", out=spl[:], in0=idxf[:], scalar1=127.5, scalar2=None, op0=ALU.is_ge)
                for thv in (255.5, 383.5):
                    op(DVE, "tensor_scalar", out=tmp3[:], in0=idxf[:], scalar1=thv, scalar2=None, op0=ALU.is_ge)
                    op(DVE, "tensor_tensor", out=spl[:], in0=spl[:], in1=tmp3[:], op=ALU.add)
                op(DVE, "scalar_tensor_tensor", out=pgf[:], in0=spl[:], scalar=-128.0, in1=idxf[:], op0=ALU.mult, op1=ALU.add)
                oh = K.scoped(s3, "oh", [128, 32, 128], F32)
                phys = K.scoped(s3, "phys", [128, NSB, 32], F32)
                for b in range(NSB):
                    op(DVE, "tensor_tensor", out=oh[:], in0=pgf[:, b, :].unsqueeze(2).broadcast_to([128, 32, 128]),
                       in1=iota_pg.unsqueeze(1).broadcast_to([128, 32, 128]), op=ALU.is_equal)
                    op(DVE, "tensor_tensor", out=oh[:], in0=oh[:], in1=ptrf[:, b, :].unsqueeze(1).broadcast_to([128, 32, 128]), op=ALU.mult)
                    op(DVE, "tensor_reduce", out=phys[:, b, :], in_=oh[:], axis=AX.X, op=ALU.add)
                rowf = K.scoped(s3, "rowf", [128, NSB, 32], F32)
                op(DVE, "tensor_scalar", out=rowf[:], in0=phys[:], scalar1=128.0, scalar2=slotbase, op0=ALU.mult, op1=ALU.add)
                op(DVE, "scalar_tensor_tensor", out=rowf[:], in0=spl[:], scalar=2.0, in1=rowf[:], op0=ALU.mult, op1=ALU.add)
                BIG = 1.0e9
                op(DVE, "tensor_scalar", out=tmp3[:], in0=sel[:, :, 0:32], scalar1=-BIG, scalar2=BIG, op0=ALU.mult, op1=ALU.add)
                op(DVE, "tensor_tensor", out=rowf[:], in0=rowf[:], in1=sel[:, :, 0:32], op=ALU.mult)
                op(DVE, "tensor_tensor", out=rowf[:], in0=rowf[:], in1=tmp3[:], op=ALU.add)
                rowi = K.scoped(s3, "rowi", [128, NSB, 32], I32)
                op(DVE, "tensor_copy", out=rowi[:], in_=rowf[:])
                CH = 4
                Kgs = [K.scoped(s3, f"Kg{i}", [128, CH, 512], F32) for i in range(2)]
                Vgs = [K.scoped(s3, f"Vg{i}", [128, CH, 512], F32) for i in range(2)]
                prod = K.scoped(s3, "prod", [128, CH, 512], F32)
                qsb = K.scoped(s3, "qsb", [128, 512], F32)
                S_ = K.scoped(s3, "S_", [128, CH, 8], F32)
                Pm = K.scoped(s3, "Pm", [128, CH, 8], F32)
                Oacc = K.scoped(s3, "Oacc", [128, 512], F32)
                Opart = K.scoped(s3, "Opart", [128, 512], F32)
                racc = K.scoped(s3, "racc", [128, 8], F32)
                rpart = K.scoped(s3, "rpart", [128, 8], F32)
                for i_ in range(2):
                    op(DVE, "memset", ap=Kgs[i_][:], constant=0.0)
                    op(DVE, "memset", ap=Vgs[i_][:], constant=0.0)
                cix = 0
                nrows = n_phys * 128
                bc_reg = nc.gpsimd.to_reg(nrows - 1)
                for b in range(NSB):
                    g = gbank()
                    op(PE, "matmul", out=g[:, :], lhsT=selq[:, b, :], rhs=s_q[:, :], start=True, stop=True)
                    op(ACT, "copy", out=qsb[:, :], in_=g[:, :])
                    op(DVE, "memset", ap=Oacc[:], constant=0.0)
                    op(DVE, "memset", ap=racc[:], constant=0.0)
                    chunks = [(c0, CH, False) for c0 in range(0, 32, CH)] + [(32, NST, True)]
                    for (c0, w_, isnew) in chunks:
                        cix += 1
                        Kg, Vg = Kgs[cix % 2], Vgs[cix % 2]
                        if isnew:
                            dma(SP, out=Kg[:, 0:NST, :].rearrange("p t f -> p (t f)"),
                                in_=k_s[b * NST:(b + 1) * NST, :].rearrange("(o t) f -> o (t f)", o=1).broadcast_to([128, NST * 512]))
                            dma(SP, out=Vg[:, 0:NST, :].rearrange("p t f -> p (t f)"),
                                in_=v_s[b * NST:(b + 1) * NST, :].rearrange("(o t) f -> o (t f)", o=1).broadcast_to([128, NST * 512]))
                        else:
                            for kk in range(w_):
                                for (dst, src) in ((Kg, c_k), (Vg, c_v)):
                                    dma(POOL, out=dst[:, kk, :], in_=src[:, :], indirect=rowi[:, b, c0 + kk:c0 + kk + 1],
                                        in_offset=bass.IndirectOffsetOnAxis(ap=rowi[:, b, c0 + kk:c0 + kk + 1], axis=0),
                                        bounds_check=bc_reg, oob_is_err=False)
                        selc = sel[:, b, c0:c0 + w_]
                        op(DVE, "tensor_tensor", out=prod[:, 0:w_, :], in0=Kg[:, 0:w_, :],
                           in1=qsb[:, :].unsqueeze(1).broadcast_to([128, w_, 512]), op=ALU.mult)
                        op(DVE, "tensor_reduce", out=S_[:, 0:w_, :], in_=prod[:, 0:w_, :].rearrange("p k (h d) -> p k h d", h=8),
                           axis=AX.X, op=ALU.add)
                        op(DVE, "tensor_tensor", out=S_[:, 0:w_, :], in0=S_[:, 0:w_, :],
                           in1=selc.unsqueeze(2).broadcast_to([128, w_, 8]), op=ALU.mult)
                        op(ACT, "activation", out=Pm[:, 0:w_, :], in_=S_[:, 0:w_, :], func=AF.Exp, scale=0.125)
                        op(DVE, "tensor_tensor", out=Pm[:, 0:w_, :], in0=Pm[:, 0:w_, :],
                           in1=selc.unsqueeze(2).broadcast_to([128, w_, 8]), op=ALU.mult)
                        op(DVE, "tensor_tensor", out=prod[:, 0:w_, :].rearrange("p k (h d) -> p k h d", h=8),
                           in0=Vg[:, 0:w_, :].rearrange("p k (h d) -> p k h d", h=8),
                           in1=Pm[:, 0:w_, :].unsqueeze(3).broadcast_to([128, w_, 8, 64]), op=ALU.mult)
                        op(DVE, "tensor_reduce", out=Opart[:, :], in_=prod[:, 0:w_, :].rearrange("p k f -> p f k"), axis=AX.X, op=ALU.add)
                        op(DVE, "tensor_tensor", out=Oacc[:, :], in0=Oacc[:, :], in1=Opart[:, :], op=ALU.add)
                        op(DVE, "tensor_reduce", out=rpart[:, :], in_=Pm[:, 0:w_, :].rearrange("p k h -> p h k"), axis=AX.X, op=ALU.add)
                        op(DVE, "tensor_tensor", out=racc[:, :], in0=racc[:, :], in1=rpart[:, :], op=ALU.add)
                    op(PE, "matmul", out=OB[0][0:NS, :], lhsT=agg[:, b, :], rhs=Oacc[:, :], start=(b == 0), stop=(b == NSB - 1))
                    op(PE, "matmul", out=OB[1][0:NS, 0:8], lhsT=agg[:, b, :], rhs=racc[:, :], start=(b == 0), stop=(b == NSB - 1))
                rcs = K.scoped(s3, "rcs", [NS, 8, 1], F32)
                atts = K.scoped(s3, "atts", [NS, 8, 64], BF16)
                op(DVE, "reciprocal", out=rcs[:, :, 0], in_=OB[1][0:NS, 0:8])
                op(DVE, "tensor_tensor", out=atts[:], in0=OB[0][0:NS, :].rearrange("p (h d) -> p h d", h=8),
                   in1=rcs[:].broadcast_to([NS, 8, 64]), op=ALU.mult)
                transpose_cols(atts[:].rearrange("p h d -> p (h d)"), NS, 512, lambda c: s_attT[:, c, 0:NS])
            K.barrier()

        with contextlib.ExitStack() as s4:
            NB0 = 2120
            winb = K.scoped(s4, "win_b", [128, 8, 3072], BF16)
            wpa = K.scoped(s4, "wpa", [128, 4, D], BF16)
            wpb = K.scoped(s4, "wpb", [128, 4, D], BF16)
            wo = K.scoped(s4, "wo", [128, 8, D], BF16)
            wabd = K.scoped(s4, "wabd", [128, 4, 128], BF16)
            wxbd = K.scoped(s4, "wxbd", [128, 4, 128], BF16)
            dma(POOL, out=wpa[:], in_=w_pa[:].rearrange("(c p) n -> p c n", p=128))
            dma(POOL, out=wpb[:], in_=w_pb[:].rearrange("(c p) n -> p c n", p=128))
            for kc in range(8):
                dma(POOL, out=wo[:, kc, :], in_=w_o[kc * 128:(kc + 1) * 128, :])
            op(DVE, "memset", ap=wabd[:], constant=0.0)
            op(DVE, "memset", ap=wxbd[:], constant=0.0)
            for blk_ in range(8):
                c, j = blk_ // 2, blk_ % 2
                dma(POOL, out=wabd[j * 64:(j + 1) * 64, c, j * 64:(j + 1) * 64], in_=l_wa[blk_])
                dma(POOL, out=wxbd[j * 64:(j + 1) * 64, c, j * 64:(j + 1) * 64], in_=l_wx[blk_])
            hst = K.scoped(s4, "hst", [128, 4], F32)
            xexts = [K.scoped(s4, f"xext{i}", [128, 4, 131], F32) for i in range(2)]
            exts = K.scoped(s4, "exts", [128, 4, NSB, 7], F32)
            h0T = K.scoped(s4, "h0T", [128, 4, NSB], F32)
            hls = K.scoped(s4, "hls", [128, 4, NSB], F32)
            xtbs = [K.scoped(s4, f"xtb{i}", [128, D], F32) for i in range(2)]
            ggs = [K.scoped(s4, f"gg{i}", [128, 4, 128], BF16) for i in range(2)]
            sgas = [K.scoped(s4, f"sga{i}", [128, 8, 128], BF16) for i in range(2)]
            sgbs = [K.scoped(s4, f"sgb{i}", [128, 8, 128], BF16) for i in range(2)]
            xc = K.scoped(s4, "xc", [128, 4, 128], F32)
            xcb = K.scoped(s4, "xcb", [128, 4, 128], BF16)
            ra = K.scoped(s4, "ra", [128, 4, 128], F32)
            ih = K.scoped(s4, "ih", [128, 4, 128], F32)
            mu = K.scoped(s4, "mu", [128, 4, 128], F32)
            hl = K.scoped(s4, "hl", [128, 4, 128], F32)
            ybi = K.scoped(s4, "ybi", [128, 4, 128], BF16)
            mg = K.scoped(s4, "mg", [128, 8, 128], BF16)
            t1 = K.scoped(s4, "t1", [128, 4, 128], F32)
            t2 = K.scoped(s4, "t2", [128, 4, 128], F32)
            op(DVE, "memset", ap=hst[:], constant=0.0)
            op(DVE, "memset", ap=xexts[0][:], constant=0.0)
            op(DVE, "memset", ap=xexts[1][:], constant=0.0)
            for kc in range(8):
                for c0 in range(0, 3072, 1536):
                    dma(POOL, out=winb[:, kc, c0:c0 + 1536], in_=w_in[kc * 128:(kc + 1) * 128, NB0 + c0:NB0 + c0 + 1536])

            def front_b(n, xsrc, is_prompt, par):
                xt_, gg, sga, sgb, xext = xtbs[par], ggs[par], sgas[par], sgbs[par], xexts[par]
                dma(SP, out=xt_[0:n, :], in_=xsrc)
                rmsnorm_to_bf(xt_, n, gbc, hb)
                transpose_cols(hb, n, D, lambda c: hT[:, c, 0:n])
                yield

                def fm_bank(col0, nchunk):
                    g = gbank()
                    for cc in range(nchunk):
                        for kc in range(8):
                            op(PE, "matmul", out=g[:, cc * n:(cc + 1) * n],
                               lhsT=winb[:, kc, col0 + cc * 128:col0 + (cc + 1) * 128],
                               rhs=hT[:, kc, 0:n], start=(kc == 0), stop=(kc == 7))
                    return g[:, 0:nchunk * n].rearrange("p (c n) -> p c n", c=nchunk)

                gv = fm_bank(0, 4)
                if is_prompt:
                    op(POOL, "tensor_copy", out=xext[:, :, 0:3], in_=xexts[1 - par][:, :, 128:131])
                    op(ACT, "copy", out=xext[:, :, 3:3 + n], in_=gv)
                else:
                    for c in range(4):
                        op(ACT, "copy", out=exts[:, c, :, 3:7], in_=gv[:, c, :].rearrange("p (b t) -> p b t", b=NSB))
                yield
                gv = fm_bank(512, 4)
                op(ACT, "activation", out=gg[:, :, 0:n], in_=gv, func=AF.Gelu_apprx_tanh)
                yield
                for hh in range(2):
                    gv = fm_bank(1024 + hh * 512, 4)
                    op(ACT, "activation", out=sga[:, hh * 4:hh * 4 + 4, 0:n], in_=gv, func=AF.Sigmoid)
                    yield
                for hh in range(2):
                    gv = fm_bank(2048 + hh * 512, 4)
                    op(ACT, "activation", out=sgb[:, hh * 4:hh * 4 + 4, 0:n], in_=gv, func=AF.Sigmoid)
                    yield

            def mixer_back(n, attT_fn, conv_fn, scan_fn, x1row0, par):
                xt_, gg, sga, sgb = xtbs[par], ggs[par], sgas[par], sgbs[par]
                conv_fn()
                op(POOL, "tensor_copy", out=xcb[:, :, 0:n], in_=xc[:, :, 0:n])
                yield
                gb_ = gbank()
                for c in range(4):
                    op(PE, "matmul", out=gb_[:, c * n:(c + 1) * n], lhsT=wabd[:, c, :], rhs=xcb[:, c, 0:n], start=True, stop=True)
                for c in range(4):
                    op(ACT, "activation", out=ra[:, c, 0:n], in_=gb_[:, c * n:(c + 1) * n], func=AF.Sigmoid, bias=lba[:, c:c + 1])
                gb2 = gbank()
                for c in range(4):
                    op(PE, "matmul", out=gb2[:, c * n:(c + 1) * n], lhsT=wxbd[:, c, :], rhs=xcb[:, c, 0:n], start=True, stop=True)
                for c in range(4):
                    op(ACT, "activation", out=ih[:, c, 0:n], in_=gb2[:, c * n:(c + 1) * n], func=AF.Sigmoid, bias=lbx[:, c:c + 1])
                yield
                for c in range(4):
                    op(ACT, "activation", out=ra[:, c, 0:n], in_=ra[:, c, 0:n], func=AF.Exp, scale=lcl[:, c:c + 1])
                yield
                op(DVE, "tensor_tensor", out=mu[:, :, 0:n], in0=ra[:, :, 0:n], in1=ra[:, :, 0:n], op=ALU.mult)
                op(DVE, "tensor_scalar", out=mu[:, :, 0:n], in0=mu[:, :, 0:n], scalar1=-1.0, scalar2=1.0, op0=ALU.mult, op1=ALU.add)
                op(DVE, "tensor_scalar", out=mu[:, :, 0:n], in0=mu[:, :, 0:n], scalar1=0.0, scalar2=None, op0=ALU.max)
                yield
                op(ACT, "activation", out=mu[:, :, 0:n], in_=mu[:, :, 0:n], func=AF.Sqrt)
                yield
                op(DVE, "tensor_tensor", out=mu[:, :, 0:n], in0=mu[:, :, 0:n], in1=ih[:, :, 0:n], op=ALU.mult)
                op(DVE, "tensor_tensor", out=mu[:, :, 0:n], in0=mu[:, :, 0:n], in1=xc[:, :, 0:n], op=ALU.mult)
                yield
                scan_fn()
                yield
                op(DVE, "tensor_tensor", out=ybi[:, :, 0:n], in0=hl[:, :, 0:n], in1=gg[:, :, 0:n], op=ALU.mult)
                yield
                for half in range(2):
                    ga_ = gbank()
                    for cc in range(4):
                        col = half * 4 + cc
                        for c in range(4):
                            op(PE, "matmul", out=ga_[:, cc * n:(cc + 1) * n], lhsT=wpa[:, c, col * 128:(col + 1) * 128],
                               rhs=attT_fn(c), start=(c == 0), stop=(c == 3))
                    op(DVE, "tensor_tensor", out=t1[:, :, 0:n], in0=ga_[:, 0:4 * n].rearrange("p (c n) -> p c n", c=4),
                       in1=sga[:, half * 4:half * 4 + 4, 0:n], op=ALU.mult)
                    gb3 = gbank()
                    for cc in range(4):
                        col = half * 4 + cc
                        for c in range(4):
                            op(PE, "matmul", out=gb3[:, cc * n:(cc + 1) * n], lhsT=wpb[:, c, col * 128:(col + 1) * 128],
                               rhs=ybi[:, c, 0:n], start=(c == 0), stop=(c == 3))
                    op(DVE, "tensor_tensor", out=t2[:, :, 0:n], in0=gb3[:, 0:4 * n].rearrange("p (c n) -> p c n", c=4),
                       in1=sgb[:, half * 4:half * 4 + 4, 0:n], op=ALU.mult)
                    yield
                    op(DVE, "tensor_tensor", out=mg[:, half * 4:half * 4 + 4, 0:n], in0=t1[:, :, 0:n], in1=t2[:, :, 0:n], op=ALU.add)
                    yield
                for half in range(2):
                    ob = OB[half]
                    for kc in range(8):
                        op(PE, "matmul", out=ob[0:n, :], lhsT=mg[:, kc, 0:n], rhs=wo[:, kc, half * 512:(half + 1) * 512],
                           start=(kc == 0), stop=(kc == 7))
                    yield
                    op(DVE, "tensor_tensor", out=xt_[0:n, half * 512:(half + 1) * 512], in0=ob[0:n, :],
                       in1=xt_[0:n, half * 512:(half + 1) * 512], op=ALU.add)
                dma(SP, out=x1_d[x1row0:x1row0 + n, :], in_=xt_[0:n, :])
                yield

            def stageF(i):
                yield from front_b(128, x_p[i * 128:(i + 1) * 128, :], True, i % 2)

            def stageM(i):
                par = i % 2
                xext = xexts[par]

                def conv_fn():
                    for c in range(4):
                        op(DVE, "tensor_scalar", out=xc[:, c, :], in0=xext[:, c, 0:128], scalar1=lcw[:, 0, c:c + 1],
                           scalar2=lcb[:, c:c + 1], op0=ALU.mult, op1=ALU.add)
                        for j in range(1, 4):
                            op(DVE, "scalar_tensor_tensor", out=xc[:, c, :], in0=xext[:, c, j:j + 128], scalar=lcw[:, j, c:c + 1],
                               in1=xc[:, c, :], op0=ALU.mult, op1=ALU.add)

                def scan_fn():
                    for c in range(4):
                        op(DVE, "tensor_tensor_scan", out=hl[:, c, :], data0=ra[:, c, :], data1=mu[:, c, :],
                           initial=hst[:, c:c + 1], op0=ALU.mult, op1=ALU.add)
                    op(DVE, "tensor_copy", out=hst[:, :], in_=hl[:, :, 127])

                yield from mixer_back(128, lambda c: attA[:, c, i * 128:(i + 1) * 128], conv_fn, scan_fn, i * 128, par)
                if i == NT - 1:
                    for c in range(4):
                        dma_cols(lc_p, c, xext[:, c, 128:131], False)
                    dma(SP, out=lh_p[:].rearrange("(c p) -> p c", p=128), in_=hst[:, :], allow_slow_non_contiguous=True)

            stages = [stageF, stageM]
            for step in range(NT + len(stages) - 1):
                alive = [stages[s](step - s) for s in range(len(stages)) if 0 <= step - s < NT]
                while alive:
                    nxt = []
                    for gnr in alive:
                        try:
                            next(gnr)
                            nxt.append(gnr)
                        except StopIteration:
                            pass
                    alive = nxt

            if do_sample:
                for c in range(4):
                    for b in range(NSB):
                        dma_cols(s_lc[b], c, exts[:, c, b, 0:3], True)
                    dma_cols(s_lh, c, h0T[:, c, :], True)
                for _ in front_b(NS, x_s[:, :], False, 0):
                    pass

                def conv_s():
                    for c in range(4):
                        xv = xc[:, c, 0:NS].rearrange("p (b t) -> p b t", b=NSB)
                        op(DVE, "tensor_scalar", out=xv, in0=exts[:, c, :, 0:NST], scalar1=lcw[:, 0, c:c + 1],
                           scalar2=lcb[:, c:c + 1], op0=ALU.mult, op1=ALU.add)
                        for j in range(1, 4):
                            op(DVE, "scalar_tensor_tensor", out=xv, in0=exts[:, c, :, j:j + NST], scalar=lcw[:, j, c:c + 1],
                               in1=xv, op0=ALU.mult, op1=ALU.add)

                def scan_s():
                    for c in range(4):
                        for b in range(NSB):
                            op(DVE, "tensor_tensor_scan", out=hl[:, c, b * NST:(b + 1) * NST], data0=ra[:, c, b * NST:(b + 1) * NST],
                               data1=mu[:, c, b * NST:(b + 1) * NST], initial=h0T[:, c, b:b + 1], op0=ALU.mult, op1=ALU.add)
                    op(DVE, "tensor_copy", out=hls[:, :, :], in_=hl[:, :, 0:NS].rearrange("p c (b t) -> p c b t", b=NSB)[:, :, :, NST - 1])

                for _ in mixer_back(NS, lambda c: s_attT[:, c, 0:NS], conv_s, scan_s, SEQ, 0):
                    pass
                for c in range(4):
                    for b in range(NSB):
                        dma_cols(lc_s[b], c, exts[:, c, b, 4:7], False)
                    dma_cols(lh_s, c, hls[:, c, :], False)
        K.barrier()

    if do_ffn:
        with contextlib.ExitStack() as s2:
            wup = K.scoped(s2, "wup", [128, 8, 2 * DFF], BF16)
            dma(SP, out=gbc[:], in_=g_ffn[:].broadcast_to([128, D]))
            gbc2 = K.scoped(s2, "gbc2", [128, D], F32)
            dma(SP, out=gbc2[:], in_=g_fin[:].broadcast_to([128, D]))
            x1ts = [K.scoped(s2, f"x1t{i}", [128, D], F32) for i in range(2)]
            h2 = K.scoped(s2, "h2", [128, D], BF16)
            h2Ts = [K.scoped(s2, f"h2T{i}", [128, 8, 128], BF16) for i in range(2)]
            uexl = [K.scoped(s2, f"uex{c}", [128, 130], F32) for c in range(24)]
            uexsl = [K.scoped(s2, f"uexs{c}", [128, NSB, 6], F32) for c in range(24)]
            accs = [K.scoped(s2, f"facc{i}", [128, 128], F32) for i in range(4)]
            gls = [K.scoped(s2, f"fgl{i}", [128, 128], F32) for i in range(4)]
            gTs = [K.scoped(s2, f"gT{i}", [128, 24, 128], BF16) for i in range(2)]
            x2 = K.scoped(s2, "x2", [128, D], F32)
            for c in range(24):
                op(POOL, "memset", ap=uexl[c][:, 0:2], constant=0.0)
            stg = K.scoped(s2, "stg", [8, 512], F32)
            for kc in range(8):
                for c0 in range(0, 2 * DFF, 2048):
                    dma(POOL, out=wup[:, kc, c0:c0 + 2048], in_=w_up[kc * 128:(kc + 1) * 128, c0:c0 + 2048])
            rf = {"a": 0}

            def prep(n, row0, par):
                x1t, h2T = x1ts[par], h2Ts[par]
                dma(SP, out=x1t[0:n, :], in_=x1_d[row0:row0 + n, :])
                rmsnorm_to_bf(x1t, n, gbc, h2)
                transpose_cols(h2, n, D, lambda c: h2T[:, c, 0:n])

            ubs = [K.scoped(s2, f"ubs{i}", [128, 128], BF16) for i in range(6)]

            def up_chunks(n, par, prompt):
                h2T, gT = h2Ts[par], gTs[par]
                banks = {}

                def s1(c):
                    g = gbank()
                    for part in range(2):
                        for kc in range(8):
                            op(PE, "matmul", out=g[:, part * n:(part + 1) * n],
                               lhsT=wup[:, kc, part * DFF + c * 128: part * DFF + (c + 1) * 128],
                               rhs=h2T[:, kc, 0:n], start=(kc == 0), stop=(kc == 7))
                    banks[c] = g

                def s2(c):
                    g = banks.pop(c)
                    if prompt:
                        op(ACT, "copy", out=uexl[c][:, 2:2 + n], in_=g[:, 0:n])
                    else:
                        op(ACT, "copy", out=uexsl[c][:, :, 2:6], in_=g[:, 0:n].rearrange("p (b t) -> p b t", b=NSB))
                    op(ACT, "copy", out=ubs[c % 6][:, 0:n], in_=g[:, n:2 * n])

                def s3(c):
                    acc = accs[c % 4]
                    if prompt:
                        ue = uexl[c]
                        taps = [ue[:, j:j + n] for j in range(3)]
                        av = acc[:, 0:n]
                    else:
                        ue = uexsl[c]
                        taps = [ue[:, :, j:j + NST] for j in range(3)]
                        av = acc[:, 0:n].rearrange("p (b t) -> p b t", b=NSB)
                    op(ACT, "activation", out=av, in_=taps[0], func=AF.Identity, scale=fcw[:, 0, c:c + 1], bias=fcb[:, c:c + 1])
                    for j in (1, 2):
                        op(DVE, "scalar_tensor_tensor", out=av, in0=taps[j], scalar=fcw[:, j, c:c + 1], in1=av,
                           op0=ALU.mult, op1=ALU.add)
                    if prompt:
                        op(POOL, "tensor_copy", out=ue[:, 0:2], in_=ue[:, 128:130])

                def s4(c):
                    op(ACT, "activation", out=gls[c % 4][:, 0:n], in_=accs[c % 4][:, 0:n], func=AF.Gelu_apprx_tanh)

                def s5(c):
                    op(POOL, "tensor_tensor", out=gT[:, c, 0:n], in0=ubs[c % 6][:, 0:n], in1=gls[c % 4][:, 0:n], op=ALU.mult)

                stages_ = [s1, s2, s3, s4, s5]
                for k in range(24 + 4):
                    for si, fn in enumerate(stages_):
                        c = k - si
                        if 0 <= c < 24:
                            fn(c)

            def down_final(n, par, yout):
                x1t, gT = x1ts[par], gTs[par]
                for half in range(2):
                    ob = OB[half]
                    for c in range(24):
                        op(PE, "matmul", out=ob[0:n, :], lhsT=gT[:, c, 0:n], rhs=wdn[:, c, half * 512:(half + 1) * 512],
                           start=(c == 0), stop=(c == 23))
                    op(DVE, "tensor_tensor", out=x2[0:n, half * 512:(half + 1) * 512], in0=ob[0:n, :],
                       in1=x1t[0:n, half * 512:(half + 1) * 512], op=ALU.add)
                rmsnorm_to_bf(x2, n, gbc2, x2, junk=h2)
                dma(SP, out=yout, in_=x2[0:n, :])

            prep(128, 0, 0)
            for i in range(NT):
                par = i % 2
                up_chunks(128, par, True)
                if i == NT - 1:
                    if do_sample:
                        for pc in range(6):
                            dma(SP, out=stg[:, :], in_=s_fc[:, :, pc * 512:(pc + 1) * 512].rearrange("b j f -> (b j) f"))
                            g = gbank()
                            for cc in range(4):
                                op(PE, "transpose", out=g[:, cc * 8:(cc + 1) * 8], in_=stg[0:8, cc * 128:(cc + 1) * 128],
                                   identity=idf[0:8, 0:8])
                            for cc in range(4):
                                op(ACT, "copy", out=uexsl[pc * 4 + cc][:, :, 0:2],
                                   in_=g[:, cc * 8:(cc + 1) * 8].rearrange("p (b j) -> p b j", b=NSB))
                        prep(NS, SEQ, 1 - par)
                else:
                    prep(128, (i + 1) * 128, 1 - par)
                down_final(128, par, y_p[i * 128:(i + 1) * 128, :])
            if do_sample:
                par = NT % 2
                up_chunks(NS, par, False)
                down_final(NS, par, y_s[:, :])
            def out_rows(src_fn, nrows, dram_fn):
                stgs = [x1ts[0], x1ts[1], x2]
                for piece in range(3):
                    st = stgs[piece]
                    for half in range(2):
                        g = gbank()
                        for cc in range(4):
                            c = piece * 8 + half * 4 + cc
                            op(PE, "transpose", out=g[0:nrows, cc * 128:(cc + 1) * 128], in_=src_fn(c), identity=idf[:, :])
                        op(ACT, "copy", out=st[0:nrows, half * 512:(half + 1) * 512], in_=g[0:nrows, 0:512])
                    dma(SP, out=dram_fn(piece), in_=st[0:nrows, :])

            out_rows(lambda c: uexl[c][:, 0:2], 2, lambda pc: fc_p[:, pc * 1024:(pc + 1) * 1024])
            if do_sample:
                cmps = [K.scoped(s2, f"cmp{i}", [128, 8], F32) for i in range(4)]

                def src_s(c):
                    t_ = cmps[c % 4]
                    op(POOL, "tensor_copy", out=t_[:, :].rearrange("p (b j) -> p b j", b=NSB), in_=uexsl[c][:, :, 4:6])
                    return t_[:, :]

                out_rows(src_s, 8, lambda pc: fc_s[:, :, pc * 1024:(pc + 1) * 1024].rearrange("b j f -> (b j) f"))
    K.finish()
    return nc


def _host_consts():
    inv = np.power(np.float32(10000.0), -np.arange(32, dtype=np.float32) / np.float32(32)).astype(np.float32)

    def tab(pos):
        ang = pos.astype(np.float32)[:, None] * inv[None, :]
        return np.cos(ang).astype(np.float32), np.sin(ang).astype(np.float32)

    cp, sp_ = tab(np.arange(SEQ))
    pos_s = np.tile(PAST + np.arange(NST), NSB)
    cs, ss_ = tab(pos_s)
    q = np.arange(128)
    cmask = np.where(q[None, :] <= q[:, None], 0.0, NEG).astype(np.float32)
    pow2 = np.tile((2.0 ** -np.arange(NBIS + 2, dtype=np.float64)).astype(np.float32)[None, :], (128, 1))
    blk = (q[:, None] // 32 == q[None, :] // 32).astype(np.float32).astype(ml_dtypes.bfloat16)
    misc = np.zeros((128, 1024), np.float32)
    seg, tq = q % 32, q // 32
    misc[:, 0] = 8 * (seg // 2) + (seg % 2)
    tp = np.arange(4)
    misc[:, 1:5] = np.where((seg[:, None] == 0) & (tp[None, :] <= tq[:, None]), 0.0, NEG)
    misc[:, 16:144] = np.arange(128)[None, :]
    agg = np.zeros((128, NSB, NS), np.float32)
    selq = np.zeros((NS, NSB, 128), np.float32)
    for b in range(NSB):
        agg[q, b, NST * b + tq] = 1.0
        selq[NST * b + tq, b, q] = 1.0
    misc[:, 160:160 + NSB * NS] = agg.reshape(128, -1)
    p0 = np.zeros((8, 4, 32, 128), np.float32)
    for t in range(4):
        for s_ in range(32):
            p0[:, t, s_, t * 32 + s_] = 1.0
    p0 = p0.reshape(32, 32 * 128).astype(ml_dtypes.bfloat16)
    return {
        "c_ident_bf": np.eye(128, dtype=np.float32).astype(ml_dtypes.bfloat16),
        "c_ident_f": np.eye(128, dtype=np.float32),
        "c_cos_p": cp, "c_sin_p": sp_, "c_cos_s": cs, "c_sin_s": ss_,
        "c_cmask": cmask, "c_pow2": pow2, "c_misc": misc, "c_blk": blk,
        "c_p0": p0, "c_selq": selq.reshape(NS, NSB * 128),
    }


_OUT_NAMES = ["y_prompt", "y_sample", "k_prompt", "v_prompt", "kidx_prompt", "lru_conv_prompt", "lru_h_prompt",
              "ffn_conv_prompt", "k_sample", "v_sample", "kidx_sample", "lru_conv_sample", "lru_h_sample",
              "ffn_conv_sample"]


def make_in_map(inputs, c, consts):
    f = lambda a: np.ascontiguousarray(np.asarray(a))
    n_phys = inputs["cache_k"].shape[1]
    m = {
        "x_prompt": f(inputs["x_prompt"][c]),
        "x_sample": f(inputs["x_sample"][NSB * c:NSB * (c + 1)]).reshape(NS, D),
        "cache_k": np.asarray(inputs["cache_k"]).reshape(n_phys * 128, 512),
        "cache_v": np.asarray(inputs["cache_v"]).reshape(n_phys * 128, 512),
        "cache_kidx": np.asarray(inputs["cache_kidx"]).reshape(n_phys * 128, 64),
        "page_table": f(inputs["page_table"][NSB * c:NSB * (c + 1)]).astype(np.int32),
        "state_lru_conv": f(inputs["state_lru_conv"][0, NSB * c:NSB * (c + 1)]),
        "state_lru_h": f(inputs["state_lru_h"][0, NSB * c:NSB * (c + 1)]),
        "state_ffn_conv": f(inputs["state_ffn_conv"][0, NSB * c:NSB * (c + 1)]),
        "norm_mix_g": f(inputs["norm_mix_g"]).reshape(1, D),
        "w_in": f(inputs["w_in"][0]),
        "lru_conv_w": f(inputs["lru_conv_w"][0]),
        "lru_conv_b": f(inputs["lru_conv_b"][0]),
        "lru_wa": f(inputs["lru_wa"][0]),
        "lru_ba": f(inputs["lru_ba"][0]),
        "lru_wx": f(inputs["lru_wx"][0]),
        "lru_bx": f(inputs["lru_bx"][0]),
        "lru_lambda": f(inputs["lru_lambda"][0]),
        "w_proj_a": f(inputs["w_proj_a"][0]),
        "w_proj_b": f(inputs["w_proj_b"][0]),
        "w_out": f(inputs["w_out"][0]),
        "norm_ffn_g": f(inputs["norm_ffn_g"]).reshape(1, D),
        "w_up": f(inputs["w_up"][0]),
        "ffn_conv_w": f(inputs["ffn_conv_w"][0]),
        "ffn_conv_b": f(inputs["ffn_conv_b"][0]),
        "w_down": f(inputs["w_down"][0]),
        "norm_final_g": f(inputs["norm_final_g"]).reshape(1, D),
    }
    m.update(consts)
    return m


def assemble(results):
    n = len(results)
    g = lambda name: [np.asarray(r[name]) for r in results]
    y_p = np.stack(g("y_prompt"))
    y_s = np.concatenate([a.reshape(NSB, NST, D) for a in g("y_sample")])
    k_p = np.stack([a.reshape(SEQ, 8, 64) for a in g("k_prompt")])[None]
    v_p = np.stack([a.reshape(SEQ, 8, 64) for a in g("v_prompt")])[None]
    ki_p = np.stack(g("kidx_prompt"))[None]
    lc_p = np.stack(g("lru_conv_prompt"))[None]
    lh_p = np.stack([a.reshape(512) for a in g("lru_h_prompt")])[None]
    fc_p = np.stack(g("ffn_conv_prompt"))[None]
    k_s = np.concatenate([a.reshape(NSB, NST, 8, 64) for a in g("k_sample")])[None]
    v_s = np.concatenate([a.reshape(NSB, NST, 8, 64) for a in g("v_sample")])[None]
    ki_s = np.concatenate([a.reshape(NSB, NST, 64) for a in g("kidx_sample")])[None]
    lc_s = np.concatenate(g("lru_conv_sample"))[None]
    lh_s = np.concatenate(g("lru_h_sample"))[None]
    fc_s = np.concatenate(g("ffn_conv_sample"))[None]
    outs = (y_p, y_s, k_p, v_p, ki_p, lc_p, lh_p, fc_p, k_s, v_s, ki_s, lc_s, lh_s, fc_s)
    return tuple(np.ascontiguousarray(o, dtype=np.float32) for o in outs)


def kernel(**inputs):
    n_cores = 8
    n_phys = int(np.asarray(inputs["cache_k"]).shape[1])
    consts = _host_consts()
    nc = build(n_phys)
    in_maps = [make_in_map(inputs, c, consts) for c in range(n_cores)]
    res = run_bass_kernel_spmd(nc, in_maps, core_ids=list(range(n_cores)))
    return assemble(res.results)
```

```python
import numpy as np
import concourse.bass as bass
import concourse.mybir as mybir
from concourse.bass_utils import run_bass_kernel_spmd

F32 = mybir.dt.float32
BF16 = mybir.dt.bfloat16
I32 = mybir.dt.int32
U32 = mybir.dt.uint32
AF = mybir.ActivationFunctionType
ALU = mybir.AluOpType
AX = mybir.AxisListType

_OUTKEYS = ("out", "accum_out", "out_max", "out_indices", "ap")
SEM_LIMIT = 30000


class Buf:
    def __init__(self, t, kind):
        self.t = t
        self.kind = kind
        self.w = {}
        self.r = {}
        self.ld = None
        self.st = None

    def __getitem__(self, idx):
        return self.t[idx]


def _upd(d, sem, val):
    k = sem.name if hasattr(sem, "name") else id(sem)
    if k not in d or d[k][1] < val:
        d[k] = (sem, val)


class Eng:
    def __init__(self, K, name, eng):
        self.K = K
        self.name = name
        self.eng = eng
        self.sem = None
        self.cnt = 0
        self.nsem = 0
        self.waited = {}
        self.ninst = 0
        self.lazy = False
        self.lazy_self_ok = False
        self.pending = None

    def wait(self, deps):
        for k, (sem, val) in deps.items():
            if self.waited.get(k, 0) < val:
                owner = self.K.sem_owner.get(k)
                if owner is not None and owner.pending is not None and owner.sem is sem and val > owner.cnt:
                    if owner is self and self.lazy_self_ok:
                        continue
                    owner.flush()
                self.eng.wait_ge(sem, val)
                self.waited[k] = val

    def flush(self):
        if self.pending is not None:
            self.cnt += 1
            self.pending.then_inc(self.sem, 1)
            self.pending = None

    def tick(self, inst):
        if self.pending is None and (self.sem is None or self.cnt >= SEM_LIMIT):
            self.sem = self.K.nc.alloc_semaphore(f"s_{self.name}{self.nsem}")
            self.K.sem_owner[self.sem.name] = self
            self.nsem += 1
            self.cnt = 0
        self.ninst += 1
        if self.lazy:
            self.pending = inst
            return (self.sem, self.cnt + 1)
        self.cnt += 1
        inst.then_inc(self.sem, 1)
        return (self.sem, self.cnt)


class KB:
    def __init__(self, nc):
        self.nc = nc
        self.bufs = {}
        self.pe = Eng(self, "pe", nc.tensor)
        self.act = Eng(self, "act", nc.scalar)
        self.dve = Eng(self, "dve", nc.vector)
        self.pool = Eng(self, "pool", nc.gpsimd)
        self.sp = Eng(self, "sp", nc.sync)
        self.final = {}
        self.nsem_dma = 0
        self.sem_owner = {}
        self.pe.lazy = True
        self.pe.lazy_self_ok = True

    def sb(self, name, shape, dtype=F32, side=None):
        t = self.nc.alloc_sbuf_tensor(name, list(shape), dtype, side=side)
        b = Buf(t, "sb")
        self.bufs[t.name] = b
        return b

    def ps(self, name, shape, dtype=F32):
        t = self.nc.alloc_psum_tensor(name, list(shape), dtype)
        b = Buf(t, "ps")
        self.bufs[t.name] = b
        return b

    def dram(self, name, shape, dtype, kind):
        t = self.nc.dram_tensor(name, list(shape), dtype, kind=kind)
        b = Buf(t, "dram")
        self.bufs[t.name] = b
        return b

    def bufof(self, ap):
        return self.bufs[ap.tensor.name]

    def op(self, E, name, *args, **kw):
        reads, writes = [], []
        for k, v in kw.items():
            if hasattr(v, "tensor") and hasattr(v, "partition_size"):
                (writes if k in _OUTKEYS else reads).append(self.bufof(v))
        deps = {}
        for b in reads:
            for k, sv in b.w.items():
                _upd(deps, *sv)
        for b in writes:
            for k, sv in b.w.items():
                _upd(deps, *sv)
            for k, sv in b.r.items():
                _upd(deps, *sv)
        E.wait(deps)
        inst = getattr(E.eng, name)(*args, **kw)
        sv = E.tick(inst)
        for b in reads:
            _upd(b.r, *sv)
        for b in writes:
            _upd(b.w, *sv)
        return inst

    def new_group(self):
        rec = [self.nc.alloc_semaphore(f"grp{self.nsem_dma}"), 0, []]
        self.nsem_dma += 1
        return rec

    def close_group(self, rec):
        for (ob, ib) in rec[2]:
            _upd(ob.w, rec[0], rec[1])
            _upd(ib.r, rec[0], rec[1])
            if ob.kind == "dram":
                _upd(self.final, rec[0], rec[1])

    def dma(self, Q, out, in_, indirect=None, group=None, **kw):
        ob, ib = self.bufof(out), self.bufof(in_)
        extra_reads = []
        if indirect is not None:
            extra_reads.append(self.bufof(indirect))
        deps = {}
        for b in [ib] + extra_reads:
            for k, sv in b.w.items():
                _upd(deps, *sv)
        for k, sv in ob.w.items():
            _upd(deps, *sv)
        for k, sv in ob.r.items():
            _upd(deps, *sv)
        Q.wait(deps)
        if group is not None:
            rec = group
            group[2].append((ob, ib))
        elif ob.kind == "sb":
            if ob.ld is None:
                ob.ld = [self.nc.alloc_semaphore(f"ld{self.nsem_dma}"), 0]
                self.nsem_dma += 1
            rec = ob.ld
        else:
            if ib.st is None:
                ib.st = [self.nc.alloc_semaphore(f"st{self.nsem_dma}"), 0]
                self.nsem_dma += 1
            rec = ib.st
        rec[1] += 16
        if indirect is not None:
            inst = Q.eng.indirect_dma_start(out=out, out_offset=None, in_=in_, in_offset=kw.pop("in_offset"), **kw)
        else:
            inst = Q.eng.dma_start(out=out, in_=in_, **kw)
        inst.then_inc(rec[0], 16)
        sv = (rec[0], rec[1])
        _upd(ob.w, *sv)
        for b in [ib] + extra_reads:
            _upd(b.r, *sv)
        if ob.kind == "dram":
            _upd(self.final, *sv)
        return inst

    def barrier(self):
        deps = {}
        for E in (self.pe, self.act, self.dve, self.pool, self.sp):
            E.flush()
            if E.sem is not None:
                _upd(deps, E.sem, E.cnt)
        for b in self.bufs.values():
            for rec in (b.ld, b.st):
                if rec is not None:
                    _upd(deps, rec[0], rec[1])
        for E in (self.pe, self.act, self.dve, self.pool, self.sp):
            E.wait(deps)

    def scoped(self, stack, name, shape, dtype=F32):
        t = stack.enter_context(self.nc.sbuf_tensor(name, list(shape), dtype))
        b = Buf(t, "sb")
        self.bufs[t.name] = b
        return b

    def finish(self):
        for E in (self.pe, self.act, self.dve, self.pool):
            E.flush()
        self.sp.wait(self.final)
        self.sp.eng.nop() if False else None

import contextlib
import ml_dtypes

D = 1024
SEQ = 2048
NT = SEQ // 128
NSB = 4
NST = 4
NS = NSB * NST
PAST = 16384
NPAGES = 128
DIN = 5192
DFF = 3072
EPS = 1e-6
NBIS = 20
NEG = -1.0e30
CANDR = 4
NC_ = CANDR * 8


def build(n_phys, do_sample=True, do_ffn=True):
    nc = bass.Bass("TRN2", target_bir_lowering=False)
    K = KB(nc)
    op, dma = K.op, K.dma
    PE, ACT, DVE, POOL, SP = K.pe, K.act, K.dve, K.pool, K.sp

    def din(name, shape, dt=F32):
        return K.dram(name, shape, dt, "ExternalInput")

    def dout(name, shape, dt=F32):
        return K.dram(name, shape, dt, "ExternalOutput")

    x_p = din("x_prompt", [SEQ, D])
    x_s = din("x_sample", [NS, D])
    c_k = din("cache_k", [n_phys * 128, 512])
    c_v = din("cache_v", [n_phys * 128, 512])
    c_i = din("cache_kidx", [n_phys * 128, 64])
    p_t = din("page_table", [NSB, NPAGES], I32)
    s_lc = din("state_lru_conv", [NSB, 3, 512])
    s_lh = din("state_lru_h", [NSB, 512])
    s_fc = din("state_ffn_conv", [NSB, 2, DFF])
    g_mix = din("norm_mix_g", [1, D])
    w_in = din("w_in", [D, DIN])
    l_cw = din("lru_conv_w", [4, 512])
    l_cb = din("lru_conv_b", [512])
    l_wa = din("lru_wa", [8, 64, 64])
    l_ba = din("lru_ba", [512])
    l_wx = din("lru_wx", [8, 64, 64])
    l_bx = din("lru_bx", [512])
    l_lam = din("lru_lambda", [512])
    w_pa = din("w_proj_a", [512, D])
    w_pb = din("w_proj_b", [512, D])
    w_o = din("w_out", [D, D])
    g_ffn = din("norm_ffn_g", [1, D])
    w_up = din("w_up", [D, 2 * DFF])
    f_cw = din("ffn_conv_w", [3, DFF])
    f_cb = din("ffn_conv_b", [DFF])
    w_dn = din("w_down", [DFF, D])
    g_fin = din("norm_final_g", [1, D])
    c_idb = din("c_ident_bf", [128, 128], BF16)
    c_idf = din("c_ident_f", [128, 128])
    c_cosp = din("c_cos_p", [SEQ, 32])
    c_sinp = din("c_sin_p", [SEQ, 32])
    c_coss = din("c_cos_s", [NS, 32])
    c_sins = din("c_sin_s", [NS, 32])
    c_cmask = din("c_cmask", [128, 128])
    c_pow2 = din("c_pow2", [128, NBIS + 2])
    c_misc = din("c_misc", [128, 1024])
    c_blk = din("c_blk", [128, 128], BF16)
    c_p0 = din("c_p0", [32, 32 * 128], BF16)
    c_selq = din("c_selq", [NS, NSB * 128])
    y_p = dout("y_prompt", [SEQ, D])
    y_s = dout("y_sample", [NS, D])
    k_p = dout("k_prompt", [SEQ, 512])
    v_p = dout("v_prompt", [SEQ, 512])
    ki_p = dout("kidx_prompt", [SEQ, 64])
    lc_p = dout("lru_conv_prompt", [3, 512])
    lh_p = dout("lru_h_prompt", [512])
    fc_p = dout("ffn_conv_prompt", [2, DFF])
    k_s = dout("k_sample", [NS, 512])
    v_s = dout("v_sample", [NS, 512])
    ki_s = dout("kidx_sample", [NS, 64])
    lc_s = dout("lru_conv_sample", [NSB, 3, 512])
    lh_s = dout("lru_h_sample", [NSB, 512])
    fc_s = dout("ffn_conv_sample", [NSB, 2, DFF])
    x1_d = K.dram("x1_scratch", [SEQ + NS, D], F32, "Internal")
    wi_d = K.dram("wi_scratch", [NS, 8], F32, "Internal")

    pT = [K.ps(f"pT{i}", [128, 1024], BF16) for i in range(2)]
    OB = [K.ps(f"OB{i}", [128, 512], F32) for i in range(2)]
    GB = [K.ps(f"GB{i}", [128, 512], F32) for i in range(4)]
    rr = {"g": 0, "t": 0}

    def gbank():
        rr["g"] = (rr["g"] + 1) % 4
        return GB[rr["g"]]

    tb_mode = {"single": False}

    def tbank():
        if tb_mode["single"]:
            return pT[0]
        rr["t"] = (rr["t"] + 1) % 2
        return pT[rr["t"]]

    idb = K.sb("idb", [128, 128], BF16)
    idf = K.sb("idf", [128, 128], F32)
    cmask = K.sb("cmask", [128, 128], F32)
    pow2 = K.sb("pow2", [128, NBIS + 2], F32)
    gbc = K.sb("gbc", [128, D], F32)
    gbc2 = None
    dma(SP, out=idb[:], in_=c_idb[:])
    dma(SP, out=idf[:], in_=c_idf[:])
    dma(SP, out=cmask[:], in_=c_cmask[:])
    dma(SP, out=pow2[:], in_=c_pow2[:])
    dma(SP, out=gbc[:], in_=g_mix[:].broadcast_to([128, D]))


    lcw = K.sb("lcw", [128, 4, 4], F32)
    lcb = K.sb("lcb", [128, 4], F32)
    lba = K.sb("lba", [128, 4], F32)
    lbx = K.sb("lbx", [128, 4], F32)
    lcl = K.sb("lcl", [128, 4], F32)
    fcw = K.sb("fcw", [128, 3, 24], F32)
    fcb = K.sb("fcb", [128, 24], F32)
    def dma_cols(dram2d, c, sb_ap, load, group=None):
        d = dram2d[:, c * 128:(c + 1) * 128].rearrange("j p -> p j")
        if load:
            dma(SP, out=sb_ap, in_=d, allow_slow_non_contiguous=True, group=group)
        else:
            dma(SP, out=d, in_=sb_ap, allow_slow_non_contiguous=True, group=group)

    for c in range(4):
        dma_cols(l_cw, c, lcw[:, :, c], True)
    dma(SP, out=lcb[:], in_=l_cb[:].rearrange("(c p) -> p c", p=128), allow_slow_non_contiguous=True)
    dma(SP, out=lba[:], in_=l_ba[:].rearrange("(c p) -> p c", p=128), allow_slow_non_contiguous=True)
    dma(SP, out=lbx[:], in_=l_bx[:].rearrange("(c p) -> p c", p=128), allow_slow_non_contiguous=True)
    dma(SP, out=lcl[:], in_=l_lam[:].rearrange("(c p) -> p c", p=128), allow_slow_non_contiguous=True)
    for c in range(24):
        dma_cols(f_cw, c, fcw[:, :, c], True)
    dma(SP, out=fcb[:], in_=f_cb[:].rearrange("(c p) -> p c", p=128), allow_slow_non_contiguous=True)
    op(ACT, "activation", out=lcl[:], in_=lcl[:], func=AF.Exp, scale=-1.0)
    op(ACT, "activation", out=lcl[:], in_=lcl[:], func=AF.Ln, bias=1.0)
    op(DVE, "tensor_scalar", out=lcl[:], in0=lcl[:], scalar1=-8.0, scalar2=None, op0=ALU.mult)

    ss = K.sb("ss", [128, 1], F32)
    rstd = K.sb("rstd", [128, 1], F32)


    def rmsnorm_to_bf(xt, n, g, hb, junk=None):
        junk = hb if junk is None else junk
        op(ACT, "activation", out=junk[0:n, :], in_=xt[0:n, :], func=AF.Square, accum_out=ss[0:n, :])
        op(DVE, "tensor_scalar", out=rstd[0:n, :], in0=ss[0:n, :], scalar1=1.0 / D, scalar2=EPS,
           op0=ALU.mult, op1=ALU.add)
        op(ACT, "activation", out=rstd[0:n, :], in_=rstd[0:n, :], func=AF.Sqrt)
        op(DVE, "reciprocal", out=rstd[0:n, :], in_=rstd[0:n, :])
        op(DVE, "scalar_tensor_tensor", out=hb[0:n, :], in0=xt[0:n, :], scalar=rstd[0:n, 0:1], in1=g[0:n, :],
           op0=ALU.mult, op1=ALU.mult)

    def load_cast_w(dst_fn, src_fn, nk, ncols, stagings):
        idx = 0
        for kc in range(nk):
            for c0 in range(0, ncols, 1024):
                w = min(1024, ncols - c0)
                if idx % 2 == 0:
                    dma(POOL, out=dst_fn(kc, c0, w), in_=src_fn(kc, c0, w))
                else:
                    st = stagings[(idx // 2) % len(stagings)]
                    dma(SP, out=st[:, 0:w], in_=src_fn(kc, c0, w))
                    op(ACT if (idx // 2) % 2 == 0 else DVE, "copy" if (idx // 2) % 2 == 0 else "tensor_copy",
                       out=dst_fn(kc, c0, w), in_=st[:, 0:w])
                idx += 1

    def transpose_cols(src, n, ncols, dst_fn):
        nb = ncols // 128
        for c0 in range(0, nb, 8):
            tb = tbank()
            m = min(8, nb - c0)
            for c in range(m):
                op(PE, "transpose", out=tb[:, c * 128:c * 128 + n], in_=src[0:n, (c0 + c) * 128:(c0 + c + 1) * 128],
                   identity=idb[0:n, 0:n])
            for c in range(m):
                op(ACT, "copy", out=dst_fn(c0 + c), in_=tb[:, c * 128:c * 128 + n])

    def rope(src_ps, n, nh, cosb, sinb, dst, tmp):
        v = src_ps.rearrange("p (h two d) -> p h two d", h=nh, two=2)
        x1, x2 = v[:, :, 0, :], v[:, :, 1, :]
        cb = cosb[0:n, :].unsqueeze(1).broadcast_to([n, nh, 32])
        sb_ = sinb[0:n, :].unsqueeze(1).broadcast_to([n, nh, 32])
        t1, t2 = tmp[0][0:n, 0:nh, :], tmp[1][0:n, 0:nh, :]
        op(DVE, "tensor_tensor", out=t1, in0=x1, in1=cb, op=ALU.mult)
        op(DVE, "tensor_tensor", out=t2, in0=x2, in1=sb_, op=ALU.mult)
        op(DVE, "tensor_tensor", out=dst[:, :, 0:32], in0=t1, in1=t2, op=ALU.subtract)
        op(DVE, "tensor_tensor", out=t1, in0=x2, in1=cb, op=ALU.mult)
        op(DVE, "tensor_tensor", out=t2, in0=x1, in1=sb_, op=ALU.mult)
        op(DVE, "tensor_tensor", out=dst[:, :, 32:64], in0=t1, in1=t2, op=ALU.add)

    with contextlib.ExitStack() as so:
        attA = K.scoped(so, "attA", [128, 4, SEQ], BF16)
        s_q = K.scoped(so, "s_q", [NS, 512], F32)
        s_qi = K.scoped(so, "s_qi", [NS, 8, 128], BF16)
        s_kib = K.scoped(so, "s_kib", [NS, 64], BF16)
        s_wi = K.scoped(so, "s_wi", [NS, 8], F32)
        s_attT = K.scoped(so, "s_attT", [128, 4, NS], BF16)
        xt = K.scoped(so, "xt", [128, D], F32)
        hb = K.scoped(so, "hb", [128, D], BF16)
        hT = K.scoped(so, "hT", [128, 8, 128], BF16)

        def load_norm_T(n, xsrc):
            dma(SP, out=xt[0:n, :], in_=xsrc)
            rmsnorm_to_bf(xt, n, gbc, hb)
            transpose_cols(hb, n, D, lambda c: hT[:, c, 0:n])

        with contextlib.ExitStack() as s1:
            NA = 2120
            win = K.scoped(s1, "win_a", [128, 8, NA], BF16)
            for kc in range(8):
                for c0 in range(0, NA, 1060):
                    dma(POOL, out=win[:, kc, c0:c0 + 1060], in_=w_in[kc * 128:(kc + 1) * 128, c0:c0 + 1060])
            KTb = [K.scoped(s1, f"KT{j}", [128, 4, 128], BF16) for j in range(NT)]
            VAb = [K.scoped(s1, f"VA{j}", [128, 8, 65], BF16) for j in range(NT)]
            kiT = K.scoped(s1, "kiT", [128, SEQ], BF16)
            for j in range(NT):
                op(POOL, "memset", ap=VAb[j][:], constant=1.0)
            rt = [K.scoped(s1, f"rt{i}", [128, 8, 32], F32) for i in range(2)]
            cosb = K.scoped(s1, "cosb", [128, 32], F32)
            sinb = K.scoped(s1, "sinb", [128, 32], F32)
            qb = K.scoped(s1, "qb", [128, 8, 64], BF16)
            qTs = [K.scoped(s1, f"qT{i}", [128, 4, 128], BF16) for i in range(3)]
            kf = K.scoped(s1, "kf", [128, 8, 64], F32)
            kb = K.scoped(s1, "kb", [128, 512], BF16)
            vf = K.scoped(s1, "vf", [128, 512], F32)
            qib = K.scoped(s1, "qib", [128, 8, 64], BF16)
            qiT = K.scoped(s1, "qiT", [128, 4, 128], BF16)
            kif = K.scoped(s1, "kif", [128, 1, 64], F32)
            kib2 = K.scoped(s1, "kib2", [128, 128], BF16)
            wif = K.scoped(s1, "wif", [128, 8], F32)
            dg = K.scoped(s1, "dg", [128, 8, 128], BF16)
            Rh = [K.scoped(s1, f"Rh{i}", [128, 512], BF16) for i in range(4)]
            scs = [K.scoped(s1, f"sc{i}", [128, SEQ], F32) for i in range(2)]
            bis = K.scoped(s1, "bis", [128, 8], F32)
            dl = K.scoped(s1, "dl", [128, NBIS + 2], F32)
            mk = K.scoped(s1, "mk", [128, SEQ], BF16)
            mkTs = [K.scoped(s1, f"mkT{i}", [128, NT, 128], BF16) for i in range(2)]
            PTb = [K.scoped(s1, f"PTb{i}", [128, 512], BF16) for i in range(3)]
            pT1f = pT[1][:].bitcast(F32)
            rcp = K.scoped(s1, "rcp", [128, 8, 1], F32)
            att = K.scoped(s1, "att", [128, 8, 64], BF16)
            rrR = {"r": 0, "p": 0, "ga": 0, "gc": 0}
            GA = [GB[0], GB[1]]
            SCB = GB[2]
            GC = [GB[3], GB[3]]
            tb_mode["single"] = True

            def gbankA():
                rrR["ga"] = (rrR["ga"] + 1) % 2
                return GA[rrR["ga"]]

            def front_a(n, xsrc, cos_d, sin_d, is_prompt, i):
                load_norm_T(n, xsrc)
                dma(SP, out=cosb[0:n, :], in_=cos_d)
                dma(SP, out=sinb[0:n, :], in_=sin_d)
                yield

                def tm_group(c0, c1):
                    g = gbankA()
                    for kc in range(8):
                        op(PE, "matmul", out=g[0:n, 0:c1 - c0], lhsT=hT[:, kc, 0:n], rhs=win[:, kc, c0:c1],
                           start=(kc == 0), stop=(kc == 7))
                    return g

                g = tm_group(0, 512)
                if is_prompt:
                    rope(g[0:n, :], n, 8, cosb, sinb, qb[0:n], rt)
                else:
                    rope(g[0:n, :], n, 8, cosb, sinb, s_q[0:n, :].rearrange("p (h d) -> p h d", h=8), rt)
                yield
                g = tm_group(512, 1024)
                rope(g[0:n, :], n, 8, cosb, sinb, kf[0:n], rt)
                if is_prompt:
                    dma(SP, out=k_p[i * 128:(i + 1) * 128, :], in_=kf[:].rearrange("p h d -> p (h d)"))
                    op(POOL, "tensor_copy", out=kb[0:n, :], in_=kf[:].rearrange("p h d -> p (h d)"))
                else:
                    dma(SP, out=k_s[:, :], in_=kf[0:n].rearrange("p h d -> p (h d)"))
                yield
                g = tm_group(1024, 1536)
                op(ACT, "copy", out=vf[0:n, :], in_=g[0:n, :])
                if is_prompt:
                    dma(SP, out=v_p[i * 128:(i + 1) * 128, :], in_=vf[:, :])
                    op(POOL, "tensor_copy", out=VAb[i][:, :, 0:64], in_=vf[:].rearrange("p (h d) -> p h d", h=8))
                else:
                    dma(SP, out=v_s[:, :], in_=vf[0:n, :])
                yield
                g = tm_group(1536, 2048)
                rope(g[0:n, :], n, 8, cosb, sinb, (qib if is_prompt else s_qi)[0:n], rt)
                yield
                g = tm_group(2048, 2120)
                rope(g[0:n, 0:64], n, 1, cosb, sinb, kif[0:n], rt)
                if is_prompt:
                    dma(SP, out=ki_p[i * 128:(i + 1) * 128, :], in_=kif[:, 0, :])
                    op(POOL, "tensor_copy", out=kib2[0:n, 0:64], in_=kif[0:n, 0, :])
                    op(POOL, "tensor_copy", out=kib2[0:n, 64:128], in_=kif[0:n, 0, :])
                    op(ACT, "copy", out=wif[0:n, :], in_=g[0:n, 64:72])
                else:
                    dma(SP, out=ki_s[:, :], in_=kif[0:n, 0, :])
                    op(POOL, "tensor_copy", out=s_kib[0:n, :], in_=kif[0:n, 0, :])
                    op(POOL, "tensor_copy", out=s_qi[0:n, :, 64:128], in_=s_qi[0:n, :, 0:64])
                    op(ACT, "copy", out=s_wi[0:n, :], in_=g[0:n, 64:72])
                yield

            if do_sample:
                for _ in front_a(NS, x_s[:, :], c_coss[:, :], c_sins[:, :], False, 0):
                    pass

            def stageA(i):
                n = 128
                L = 128 * (i + 1)
                sc = scs[i % 2]
                qT = qTs[i % 3]
                yield from front_a(n, x_p[i * 128:(i + 1) * 128, :], c_cosp[i * 128:(i + 1) * 128, :],
                                   c_sinp[i * 128:(i + 1) * 128, :], True, i)
                transpose_cols(qb[:].rearrange("p h d -> p (h d)"), n, 512, lambda c: qT[:, c, :])
                yield
                transpose_cols(kb, n, 512, lambda c: KTb[i][:, c, :])
                yield
                transpose_cols(qib[:].rearrange("p h d -> p (h d)"), n, 512, lambda c: qiT[:, c, :])
                transpose_cols(kib2, n, 128, lambda c: kiT[:, i * 128:(i + 1) * 128])
                for h in range(8):
                    op(POOL, "tensor_scalar", out=dg[:, h, :], in0=idf[:, :], scalar1=wif[:, h:h + 1], scalar2=1.0,
                       op0=ALU.mult, op1=ALU.mult)
                yield
                for c0 in range(0, L, 512):
                    c1 = min(L, c0 + 512)
                    wd = c1 - c0
                    acc = SCB
                    gs = {}

                    def issueS(h):
                        pr, hf = h // 2, h % 2
                        g = gbankA()
                        op(PE, "matmul", out=g[:, 0:wd], lhsT=qiT[hf * 64:(hf + 1) * 64, pr, :],
                           rhs=kiT[hf * 64:(hf + 1) * 64, c0:c1], start=True, stop=True)
                        gs[h] = g

                    issueS(0)
                    Rs_ = {}
                    for h in range(8):
                        if h + 1 < 8:
                            issueS(h + 1)
                        Rs_[h] = Rh[h % 4]
                        op(ACT, "activation", out=Rs_[h][:, 0:wd], in_=gs[h][:, 0:wd], func=AF.Relu)
                        if h >= 1:
                            op(PE, "matmul", out=acc[:, 0:wd], lhsT=dg[:, h - 1, :], rhs=Rs_[h - 1][:, 0:wd],
                               start=(h - 1 == 0), stop=False)
                        if h % 2 == 1:
                            yield
                    op(PE, "matmul", out=acc[:, 0:wd], lhsT=dg[:, 7, :], rhs=Rs_[7][:, 0:wd], start=False, stop=True)
                    if c1 == L:
                        if wd > 128:
                            op(ACT, "copy", out=sc[:, c0:c1 - 128], in_=acc[:, 0:wd - 128])
                        op(DVE, "tensor_tensor", out=sc[:, c1 - 128:c1], in0=acc[:, wd - 128:wd], in1=cmask[:, :], op=ALU.add)
                    else:
                        op(ACT, "copy", out=sc[:, c0:c1], in_=acc[:, 0:wd])
                    yield

            def stageB(i):
                n = 128
                L = 128 * (i + 1)
                sc = scs[i % 2]
                mkT = mkTs[i % 2]
                tcol = bis[:, 3:4]
                if i < 2:
                    op(DVE, "memset", ap=tcol, constant=-1.0e29)
                else:
                    op(DVE, "tensor_reduce", out=bis[:, 0:1], in_=sc[:, 0:L], axis=AX.X, op=ALU.max)
                    op(DVE, "tensor_reduce", out=bis[:, 1:2], in_=sc[:, 0:L - 128], axis=AX.X, op=ALU.min)
                    yield
                    op(DVE, "tensor_tensor", out=bis[:, 2:3], in0=bis[:, 0:1], in1=bis[:, 1:2], op=ALU.subtract)
                    op(DVE, "tensor_scalar", out=dl[:, :], in0=pow2[:, :], scalar1=bis[:, 2:3], scalar2=None, op0=ALU.mult)
                    op(DVE, "tensor_tensor", out=tcol, in0=bis[:, 1:2], in1=dl[:, 1:2], op=ALU.add)
                    yield
                    for k in range(2, NBIS + 2):
                        op(DVE, "tensor_scalar", out=mk[:, 0:L], in0=sc[:, 0:L], scalar1=tcol, scalar2=None,
                           op0=ALU.is_ge, op1=ALU.add, accum_out=bis[:, 4:5])
                        op(DVE, "tensor_scalar", out=bis[:, 5:6], in0=bis[:, 4:5], scalar1=255.5, scalar2=dl[:, k - 1:k],
                           op0=ALU.is_ge, op1=ALU.mult)
                        op(DVE, "scalar_tensor_tensor", out=tcol, in0=tcol, scalar=dl[:, k:k + 1], in1=bis[:, 5:6],
                           op0=ALU.subtract, op1=ALU.add)
                        yield
                    op(DVE, "tensor_tensor", out=tcol, in0=tcol, in1=dl[:, NBIS + 1:NBIS + 2], op=ALU.subtract)
                    yield
                op(DVE, "tensor_scalar", out=mk[:, 0:L], in0=sc[:, 0:L], scalar1=tcol, scalar2=None, op0=ALU.is_ge)
                yield
                for c0 in range(0, i + 1, 8):
                    transpose_cols(mk[:, c0 * 128:min(L, (c0 + 8) * 128)], n, min(L, (c0 + 8) * 128) - c0 * 128,
                                   lambda c: mkT[:, c0 + c, :])
                    yield

            def stageC(i):
                n = 128
                qT = qTs[i % 3]
                mkT = mkTs[i % 2]
                units = [(h, j0, min(i + 1, j0 + 4)) for h in range(8) for j0 in range(0, i + 1, 4)]
                GC = [GB[3], pT1f]
                sg, Pu = {}, {}

                def issueST(u):
                    h, j0, j1 = units[u]
                    pr, hf = h // 2, h % 2
                    g = GC[u % 2]
                    for j in range(j0, j1):
                        op(PE, "matmul", out=g[:, (j - j0) * 128:(j - j0 + 1) * 128],
                           lhsT=KTb[j][hf * 64:(hf + 1) * 64, pr, :],
                           rhs=qT[hf * 64:(hf + 1) * 64, pr, :], start=True, stop=True)
                    sg[u] = g

                def issuePV(u):
                    h, j0, j1 = units[u]
                    ob = OB[h // 4]
                    for j in range(j0, j1):
                        op(PE, "matmul", out=ob[:, (h % 4) * 65:(h % 4) * 65 + 65], lhsT=Pu[u][:, (j - j0) * 128:(j - j0 + 1) * 128],
                           rhs=VAb[j][:, h, :], start=(j == 0), stop=(j == i))

                issueST(0)
                yield
                for u, (h, j0, j1) in enumerate(units):
                    if u + 1 < len(units):
                        issueST(u + 1)
                    g = sg.pop(u)
                    Pu[u] = PTb[u % 3]
                    wd = (j1 - j0) * 128
                    op(ACT, "activation", out=Pu[u][:, 0:wd], in_=g[:, 0:wd], func=AF.Exp, scale=0.125)
                    op(POOL, "tensor_tensor", out=Pu[u][:, 0:wd], in0=Pu[u][:, 0:wd],
                       in1=mkT[:, j0:j1, :].rearrange("p j q -> p (j q)"), op=ALU.mult)
                    if u >= 1:
                        issuePV(u - 1)
                    yield
                issuePV(len(units) - 1)
                yield
                for hh in range(2):
                    ov = OB[hh][:, 0:260].rearrange("p (h d) -> p h d", h=4)
                    op(DVE, "reciprocal", out=rcp[:, hh * 4:hh * 4 + 4, :], in_=ov[:, :, 64:65])
                    op(DVE, "tensor_tensor", out=att[:, hh * 4:hh * 4 + 4, :], in0=ov[:, :, 0:64],
                       in1=rcp[:, hh * 4:hh * 4 + 4, :].broadcast_to([128, 4, 64]), op=ALU.mult)
                yield
                transpose_cols(att[:].rearrange("p h d -> p (h d)"), n, 512, lambda c: attA[:, c, i * 128:(i + 1) * 128])
                yield

            stages = [stageA, stageB, stageC]
            for step in range(NT + len(stages) - 1):
                alive = [stages[s](step - s) for s in range(len(stages)) if 0 <= step - s < NT]
                while alive:
                    nxt = []
                    for gnr in alive:
                        try:
                            next(gnr)
                            nxt.append(gnr)
                        except StopIteration:
                            pass
                    alive = nxt
            tb_mode["single"] = False
        K.barrier()

        wdn = K.sb("wdn", [128, 24, D], BF16, side="right")
        for c in range(24):
            dma(POOL, out=wdn[:, c, :], in_=w_dn[c * 128:(c + 1) * 128, :])
        if do_sample:
            with contextlib.ExitStack() as s3:
                NBS = 34
                BR = 16384.0
                misc = K.scoped(s3, "misc", [128, 1024], F32)
                blk = K.scoped(s3, "blk", [128, 128], BF16)
                selq = K.scoped(s3, "selq", [NS, NSB, 128], F32)
                dma(SP, out=misc[:], in_=c_misc[:])
                dma(SP, out=blk[:], in_=c_blk[:])
                dma(SP, out=selq[:], in_=c_selq[:].rearrange("t (b m) -> t b m", b=NSB))
                slotbase = misc[:, 0:1]
                negnew = misc[:, 1:5]
                iota_pg = misc[:, 16:144]
                agg = misc[:, 160:224].rearrange("p (b t) -> p b t", b=NSB)
                qiTs = K.scoped(s3, "qiTs", [128, NSB, 8 * NST], BF16)
                tb = tbank()
                for h in range(8):
                    op(PE, "transpose", out=tb[:, h * NS:(h + 1) * NS], in_=s_qi[0:NS, h, :], identity=idb[0:NS, 0:NS])
                op(ACT, "copy", out=qiTs[:].rearrange("p b (h t) -> p h b t", h=8), in_=tb[:, 0:8 * NS].rearrange("p (h b t) -> p h b t", h=8, b=NSB))
                kinT = K.scoped(s3, "kinT", [64, NS], BF16)
                tb = tbank()
                op(PE, "transpose", out=tb[0:64, 0:NS], in_=s_kib[0:NS, :], identity=idb[0:NS, 0:NS])
                op(ACT, "copy", out=kinT[:, :], in_=tb[0:64, 0:NS])
                wcol = K.scoped(s3, "wcol", [32, NSB], F32)
                dma(SP, out=wi_d[:, :], in_=s_wi[:, :])
                for h in range(8):
                    dma(SP, out=wcol[h * 4:(h + 1) * 4, :], in_=wi_d[:, h].rearrange("(b t) -> t b", b=NSB),
                        allow_slow_non_contiguous=True)
                ptc = K.scoped(s3, "ptc", [128, NSB], I32)
                for b in range(NSB):
                    dma(SP, out=ptc[:, b:b + 1], in_=p_t[b, :].rearrange("(j o) -> j o", o=1), allow_slow_non_contiguous=True)
                ptri = K.scoped(s3, "ptri", [128, NSB * 128], I32)
                dma(SP, out=ptri[:, :], in_=p_t[:, :].rearrange("(o b) j -> o (b j)", o=1).broadcast_to([128, NSB * 128]))
                ptrf = K.scoped(s3, "ptrf", [128, NSB, 128], F32)
                op(DVE, "tensor_copy", out=ptrf[:].rearrange("p b j -> p (b j)"), in_=ptri[:, :])
                Rs = [K.scoped(s3, f"Rs{i}", [32, 512], BF16) for i in range(4)]
                scw = K.scoped(s3, "scw", [128, 512], F32)
                candv = K.scoped(s3, "candv", [128, NSB, 36], F32)
                candi = K.scoped(s3, "candi", [128, NSB, 32], U32)
                s3a = contextlib.ExitStack()
                s3a.__enter__()
                p0 = K.scoped(s3a, "p0", [32, 32 * 128], BF16)
                dma(SP, out=p0[:], in_=c_p0[:])
                Wp = K.scoped(s3a, "Wp", [32, 32, 128], BF16)
                pg = K.scoped(s3a, "pg", [128, 8192], F32)
                pgb = K.scoped(s3a, "pgb", [128, 8192], BF16)
                kTs = K.scoped(s3a, "kTs", [128, 64, 128], BF16)
                rsx = 0
                for b in range(NSB):
                    op(DVE, "tensor_scalar", out=Wp[:].rearrange("p s m -> p (s m)"), in0=p0[:, :], scalar1=wcol[:, b:b + 1],
                       scalar2=None, op0=ALU.mult)
                    dma(POOL, out=pg[:, :], in_=c_i[:, :].rearrange("(n s) d -> n (s d)", s=128), indirect=ptc[:, b:b + 1],
                        in_offset=bass.IndirectOffsetOnAxis(ap=ptc[:, b:b + 1], axis=0))
                    op(DVE, "tensor_copy", out=pgb[:, :], in_=pg[:, :])
                    for s0 in range(0, 64, 8):
                        tb = tbank()
                        for s_ in range(8):
                            op(PE, "transpose", out=tb[:, s_ * 128:(s_ + 1) * 128],
                               in_=pgb[:, (s0 + s_) * 128:(s0 + s_ + 1) * 128], identity=idb[:, :])
                        op(ACT, "copy", out=kTs[:, s0:s0 + 8, :], in_=tb[:, :].rearrange("p (s j) -> p s j", s=8))
                    accb = OB[0]
                    gsq = {}

                    def issue1(seg):
                        gq, e = seg // 2, seg % 2
                        g = gbank()
                        op(PE, "matmul", out=g[0:32, :], lhsT=qiTs[e * 64:(e + 1) * 64, b, :],
                           rhs=kTs[e * 64:(e + 1) * 64, gq * 4:(gq + 1) * 4, :], start=True, stop=True)
                        gsq[seg] = g

                    issue1(0)
                    for seg in range(32):
                        if seg + 1 < 32:
                            issue1(seg + 1)
                        op(ACT, "activation", out=Rs[seg % 4][:, :], in_=gsq.pop(seg)[0:32, :], func=AF.Relu)
                        if seg >= 1:
                            op(PE, "matmul", out=accb[:, :], lhsT=Wp[:, seg - 1, :], rhs=Rs[(seg - 1) % 4][:, :],
                               start=(seg - 1 == 0), stop=False)
                    op(PE, "matmul", out=accb[:, :], lhsT=Wp[:, 31, :], rhs=Rs[31 % 4][:, :], start=False, stop=True)
                    op(ACT, "copy", out=scw[:, :], in_=accb[:, :])
                    g = gbank()
                    op(PE, "matmul", out=g[0:32, 0:NST], lhsT=qiTs[0:64, b, :],
                       rhs=kinT[:, b * NST:(b + 1) * NST], start=True, stop=True)
                    rsx = (rsx + 1) % 2
                    op(ACT, "activation", out=Rs[rsx][:, 0:NST], in_=g[0:32, 0:NST], func=AF.Relu)
                    g2 = gbank()
                    op(PE, "matmul", out=g2[:, 0:NST], lhsT=Wp[:, 0, :], rhs=Rs[rsx][:, 0:NST], start=True, stop=True)
                    op(DVE, "tensor_tensor", out=candv[:, b, 32:36], in0=g2[:, 0:NST], in1=negnew, op=ALU.add)
                    for r in range(CANDR):
                        op(DVE, "max", out=candv[:, b, r * 8:(r + 1) * 8], in_=scw[:, :])
                        op(DVE, "max_index", out=candi[:, b, r * 8:(r + 1) * 8], in_max=candv[:, b, r * 8:(r + 1) * 8], in_values=scw[:, :])
                        if r + 1 < CANDR:
                            op(DVE, "match_replace", out=scw[:, :], in_to_replace=candv[:, b, r * 8:(r + 1) * 8], in_values=scw[:, :],
                               imm_value=NEG)
                s3a.__exit__(None, None, None)
                K.barrier()
                thr = K.scoped(s3, "thr", [128, NSB, 1], F32)
                cmpb = K.scoped(s3, "cmpb", [128, NSB, 36], F32)
                cnt = K.scoped(s3, "cnt", [128, NSB], F32)
                cntb = K.scoped(s3, "cntb", [128, NSB], BF16)
                stp = K.scoped(s3, "stp", [128, NSB], F32)
                op(DVE, "memset", ap=thr[:], constant=0.0)
                for k in range(1, NBS + 1):
                    dlt = BR * (2.0 ** -k)
                    op(DVE, "tensor_tensor", out=cmpb[:], in0=candv[:], in1=thr[:].broadcast_to([128, NSB, 36]), op=ALU.is_ge)
                    op(DVE, "tensor_reduce", out=cnt[:, :], in_=cmpb[:], axis=AX.X, op=ALU.add)
                    op(DVE, "tensor_copy", out=cntb[:, :], in_=cnt[:, :])
                    g = gbank()
                    op(PE, "matmul", out=g[:, 0:NSB], lhsT=blk[:, :], rhs=cntb[:, :], start=True, stop=True)
                    op(DVE, "tensor_scalar", out=stp[:, :], in0=g[:, 0:NSB], scalar1=255.5, scalar2=2.0 * dlt, op0=ALU.is_ge, op1=ALU.mult)
                    op(DVE, "scalar_tensor_tensor", out=thr[:, :, 0], in0=thr[:, :, 0], scalar=-dlt, in1=stp[:, :], op0=ALU.add, op1=ALU.add)
                op(DVE, "tensor_scalar", out=thr[:, :, 0], in0=thr[:, :, 0], scalar1=-BR * (2.0 ** -NBS), scalar2=None, op0=ALU.add)
                sel = K.scoped(s3, "sel", [128, NSB, 36], F32)
                op(DVE, "tensor_tensor", out=sel[:], in0=candv[:], in1=thr[:].broadcast_to([128, NSB, 36]), op=ALU.is_ge)
                idxf = K.scoped(s3, "idxf", [128, NSB, 32], F32)
                spl = K.scoped(s3, "spl", [128, NSB, 32], F32)
                tmp3 = K.scoped(s3, "tmp3", [128, NSB, 32], F32)
                pgf = K.scoped(s3, "pgf", [128, NSB, 32], F32)
                op(DVE, "tensor_copy", out=idxf[:], in_=candi[:])
                op(DVE, "tensor_scalar", out=spl[:], in0=idxf[:], scalar1=127.5, scalar2=None, op0=ALU.is_ge)
                for thv in (255.5, 383.5):
                    op(DVE, "tensor_scalar", out=tmp3[:], in0=idxf[:], scalar1=thv, scalar2=None, op0=ALU.is_ge)
                    op(DVE, "tensor_tensor", out=spl[:], in0=spl[:], in1=tmp3[:], op=ALU.add)
                op(DVE, "scalar_tensor_tensor", out=pgf[:], in0=spl[:], scalar=-128.0, in1=idxf[:], op0=ALU.mult, op1=ALU.add)
                oh = K.scoped(s3, "oh", [128, 32, 128], F32)
                phys = K.scoped(s3, "phys", [128, NSB, 32], F32)
                for b in range(NSB):
                    op(DVE, "tensor_tensor", out=oh[:], in0=pgf[:, b, :].unsqueeze(2).broadcast_to([128, 32, 128]),
                       in1=iota_pg.unsqueeze(1).broadcast_to([128, 32, 128]), op=ALU.is_equal)
                    op(DVE, "tensor_tensor", out=oh[:], in0=oh[:], in1=ptrf[:, b, :].unsqueeze(1).broadcast_to([128, 32, 128]), op=ALU.mult)
                    op(DVE, "tensor_reduce", out=phys[:, b, :], in_=oh[:], axis=AX.X, op=ALU.add)
                rowf = K.scoped(s3, "rowf", [128, NSB, 32], F32)
                op(DVE, "tensor_scalar", out=rowf[:], in0=phys[:], scalar1=128.0, scalar2=slotbase, op0=ALU.mult, op1=ALU.add)
                op(DVE, "scalar_tensor_tensor", out=rowf[:], in0=spl[:], scalar=2.0, in1=rowf[:], op0=ALU.mult, op1=ALU.add)
                BIG = 1.0e9
                op(DVE, "tensor_scalar", out=tmp3[:], in0=sel[:, :, 0:32], scalar1=-BIG, scalar2=BIG, op0=ALU.mult, op1=ALU.add)
                op(DVE, "tensor_tensor", out=rowf[:], in0=rowf[:], in1=sel[:, :, 0:32], op=ALU.mult)
                op(DVE, "tensor_tensor", out=rowf[:], in0=rowf[:], in1=tmp3[:], op=ALU.add)
                rowi = K.scoped(s3, "rowi", [128, NSB, 32], I32)
                op(DVE, "tensor_copy", out=rowi[:], in_=rowf[:])
                CH = 4
                Kgs = [K.scoped(s3, f"Kg{i}", [128, CH, 512], F32) for i in range(2)]
                Vgs = [K.scoped(s3, f"Vg{i}", [128, CH, 512], F32) for i in range(2)]
                prod = K.scoped(s3, "prod", [128, CH, 512], F32)
                qsb = K.scoped(s3, "qsb", [128, 512], F32)
                S_ = K.scoped(s3, "S_", [128, CH, 8], F32)
                Pm = K.scoped(s3, "Pm", [128, CH, 8], F32)
                Oacc = K.scoped(s3, "Oacc", [128, 512], F32)
                Opart = K.scoped(s3, "Opart", [128, 512], F32)
                racc = K.scoped(s3, "racc", [128, 8], F32)
                rpart = K.scoped(s3, "rpart", [128, 8], F32)
                for i_ in range(2):
                    op(DVE, "memset", ap=Kgs[i_][:], constant=0.0)
                    op(DVE, "memset", ap=Vgs[i_][:], constant=0.0)
                cix = 0
                nrows = n_phys * 128
                bc_reg = nc.gpsimd.to_reg(nrows - 1)
                for b in range(NSB):
                    g = gbank()
                    op(PE, "matmul", out=g[:, :], lhsT=selq[:, b, :], rhs=s_q[:, :], start=True, stop=True)
                    op(ACT, "copy", out=qsb[:, :], in_=g[:, :])
                    op(DVE, "memset", ap=Oacc[:], constant=0.0)
                    op(DVE, "memset", ap=racc[:], constant=0.0)
                    chunks = [(c0, CH, False) for c0 in range(0, 32, CH)] + [(32, NST, True)]
                    for (c0, w_, isnew) in chunks:
                        cix += 1
                        Kg, Vg = Kgs[cix % 2], Vgs[cix % 2]
                        if isnew:
                            dma(SP, out=Kg[:, 0:NST, :].rearrange("p t f -> p (t f)"),
                                in_=k_s[b * NST:(b + 1) * NST, :].rearrange("(o t) f -> o (t f)", o=1).broadcast_to([128, NST * 512]))
                            dma(SP, out=Vg[:, 0:NST, :].rearrange("p t f -> p (t f)"),
                                in_=v_s[b * NST:(b + 1) * NST, :].rearrange("(o t) f -> o (t f)", o=1).broadcast_to([128, NST * 512]))
                        else:
                            for kk in range(w_):
                                for (dst, src) in ((Kg, c_k), (Vg, c_v)):
                                    dma(POOL, out=dst[:, kk, :], in_=src[:, :], indirect=rowi[:, b, c0 + kk:c0 + kk + 1],
                                        in_offset=bass.IndirectOffsetOnAxis(ap=rowi[:, b, c0 + kk:c0 + kk + 1], axis=0),
                                        bounds_check=bc_reg, oob_is_err=False)
                        selc = sel[:, b, c0:c0 + w_]
                        op(DVE, "tensor_tensor", out=prod[:, 0:w_, :], in0=Kg[:, 0:w_, :],
                           in1=qsb[:, :].unsqueeze(1).broadcast_to([128, w_, 512]), op=ALU.mult)
                        op(DVE, "tensor_reduce", out=S_[:, 0:w_, :], in_=prod[:, 0:w_, :].rearrange("p k (h d) -> p k h d", h=8),
                           axis=AX.X, op=ALU.add)
                        op(DVE, "tensor_tensor", out=S_[:, 0:w_, :], in0=S_[:, 0:w_, :],
                           in1=selc.unsqueeze(2).broadcast_to([128, w_, 8]), op=ALU.mult)
                        op(ACT, "activation", out=Pm[:, 0:w_, :], in_=S_[:, 0:w_, :], func=AF.Exp, scale=0.125)
                        op(DVE, "tensor_tensor", out=Pm[:, 0:w_, :], in0=Pm[:, 0:w_, :],
                           in1=selc.unsqueeze(2).broadcast_to([128, w_, 8]), op=ALU.mult)
                        op(DVE, "tensor_tensor", out=prod[:, 0:w_, :].rearrange("p k (h d) -> p k h d", h=8),
                           in0=Vg[:, 0:w_, :].rearrange("p k (h d) -> p k h d", h=8),
                           in1=Pm[:, 0:w_, :].unsqueeze(3).broadcast_to([128, w_, 8, 64]), op=ALU.mult)
                        op(DVE, "tensor_reduce", out=Opart[:, :], in_=prod[:, 0:w_, :].rearrange("p k f -> p f k"), axis=AX.X, op=ALU.add)
                        op(DVE, "tensor_tensor", out=Oacc[:, :], in0=Oacc[:, :], in1=Opart[:, :], op=ALU.add)
                        op(DVE, "tensor_reduce", out=rpart[:, :], in_=Pm[:, 0:w_, :].rearrange("p k h -> p h k"), axis=AX.X, op=ALU.add)
                        op(DVE, "tensor_tensor", out=racc[:, :], in0=racc[:, :], in1=rpart[:, :], op=ALU.add)
                    op(PE, "matmul", out=OB[0][0:NS, :], lhsT=agg[:, b, :], rhs=Oacc[:, :], start=(b == 0), stop=(b == NSB - 1))
                    op(PE, "matmul", out=OB[1][0:NS, 0:8], lhsT=agg[:, b, :], rhs=racc[:, :], start=(b == 0), stop=(b == NSB - 1))
                rcs = K.scoped(s3, "rcs", [NS, 8, 1], F32)
                atts = K.scoped(s3, "atts", [NS, 8, 64], BF16)
                op(DVE, "reciprocal", out=rcs[:, :, 0], in_=OB[1][0:NS, 0:8])
                op(DVE, "tensor_tensor", out=atts[:], in0=OB[0][0:NS, :].rearrange("p (h d) -> p h d", h=8),
                   in1=rcs[:].broadcast_to([NS, 8, 64]), op=ALU.mult)
                transpose_cols(atts[:].rearrange("p h d -> p (h d)"), NS, 512, lambda c: s_attT[:, c, 0:NS])
            K.barrier()

        with contextlib.ExitStack() as s4:
            NB0 = 2120
            winb = K.scoped(s4, "win_b", [128, 8, 3072], BF16)
            wpa = K.scoped(s4, "wpa", [128, 4, D], BF16)
            wpb = K.scoped(s4, "wpb", [128, 4, D], BF16)
            wo = K.scoped(s4, "wo", [128, 8, D], BF16)
            wabd = K.scoped(s4, "wabd", [128, 4, 128], BF16)
            wxbd = K.scoped(s4, "wxbd", [128, 4, 128], BF16)
            dma(POOL, out=wpa[:], in_=w_pa[:].rearrange("(c p) n -> p c n", p=128))
            dma(POOL, out=wpb[:], in_=w_pb[:].rearrange("(c p) n -> p c n", p=128))
            for kc in range(8):
                dma(POOL, out=wo[:, kc, :], in_=w_o[kc * 128:(kc + 1) * 128, :])
            op(DVE, "memset", ap=wabd[:], constant=0.0)
            op(DVE, "memset", ap=wxbd[:], constant=0.0)
            for blk_ in range(8):
                c, j = blk_ // 2, blk_ % 2
                dma(POOL, out=wabd[j * 64:(j + 1) * 64, c, j * 64:(j + 1) * 64], in_=l_wa[blk_])
                dma(POOL, out=wxbd[j * 64:(j + 1) * 64, c, j * 64:(j + 1) * 64], in_=l_wx[blk_])
            hst = K.scoped(s4, "hst", [128, 4], F32)
            xexts = [K.scoped(s4, f"xext{i}", [128, 4, 131], F32) for i in range(2)]
            exts = K.scoped(s4, "exts", [128, 4, NSB, 7], F32)
            h0T = K.scoped(s4, "h0T", [128, 4, NSB], F32)
            hls = K.scoped(s4, "hls", [128, 4, NSB], F32)
            xtbs = [K.scoped(s4, f"xtb{i}", [128, D], F32) for i in range(2)]
            ggs = [K.scoped(s4, f"gg{i}", [128, 4, 128], BF16) for i in range(2)]
            sgas = [K.scoped(s4, f"sga{i}", [128, 8, 128], BF16) for i in range(2)]
            sgbs = [K.scoped(s4, f"sgb{i}", [128, 8, 128], BF16) for i in range(2)]
            xc = K.scoped(s4, "xc", [128, 4, 128], F32)
            xcb = K.scoped(s4, "xcb", [128, 4, 128], BF16)
            ra = K.scoped(s4, "ra", [128, 4, 128], F32)
            ih = K.scoped(s4, "ih", [128, 4, 128], F32)
            mu = K.scoped(s4, "mu", [128, 4, 128], F32)
            hl = K.scoped(s4, "hl", [128, 4, 128], F32)
            ybi = K.scoped(s4, "ybi", [128, 4, 128], BF16)
            mg = K.scoped(s4, "mg", [128, 8, 128], BF16)
            t1 = K.scoped(s4, "t1", [128, 4, 128], F32)
            t2 = K.scoped(s4, "t2", [128, 4, 128], F32)
            op(DVE, "memset", ap=hst[:], constant=0.0)
            op(DVE, "memset", ap=xexts[0][:], constant=0.0)
            op(DVE, "memset", ap=xexts[1][:], constant=0.0)
            for kc in range(8):
                for c0 in range(0, 3072, 1536):
                    dma(POOL, out=winb[:, kc, c0:c0 + 1536], in_=w_in[kc * 128:(kc + 1) * 128, NB0 + c0:NB0 + c0 + 1536])

            def front_b(n, xsrc, is_prompt, par):
                xt_, gg, sga, sgb, xext = xtbs[par], ggs[par], sgas[par], sgbs[par], xexts[par]
                dma(SP, out=xt_[0:n, :], in_=xsrc)
                rmsnorm_to_bf(xt_, n, gbc, hb)
                transpose_cols(hb, n, D, lambda c: hT[:, c, 0:n])
                yield

                def fm_bank(col0, nchunk):
                    g = gbank()
                    for cc in range(nchunk):
                        for kc in range(8):
                            op(PE, "matmul", out=g[:, cc * n:(cc + 1) * n],
                               lhsT=winb[:, kc, col0 + cc * 128:col0 + (cc + 1) * 128],
                               rhs=hT[:, kc, 0:n], start=(kc == 0), stop=(kc == 7))
                    return g[:, 0:nchunk * n].rearrange("p (c n) -> p c n", c=nchunk)

                gv = fm_bank(0, 4)
                if is_prompt:
                    op(POOL, "tensor_copy", out=xext[:, :, 0:3], in_=xexts[1 - par][:, :, 128:131])
                    op(ACT, "copy", out=xext[:, :, 3:3 + n], in_=gv)
                else:
                    for c in range(4):
                        op(ACT, "copy", out=exts[:, c, :, 3:7], in_=gv[:, c, :].rearrange("p (b t) -> p b t", b=NSB))
                yield
                gv = fm_bank(512, 4)
                op(ACT, "activation", out=gg[:, :, 0:n], in_=gv, func=AF.Gelu_apprx_tanh)
                yield
                for hh in range(2):
                    gv = fm_bank(1024 + hh * 512, 4)
                    op(ACT, "activation", out=sga[:, hh * 4:hh * 4 + 4, 0:n], in_=gv, func=AF.Sigmoid)
                    yield
                for hh in range(2):
                    gv = fm_bank(2048 + hh * 512, 4)
                    op(ACT, "activation", out=sgb[:, hh * 4:hh * 4 + 4, 0:n], in_=gv, func=AF.Sigmoid)
                    yield

            def mixer_back(n, attT_fn, conv_fn, scan_fn, x1row0, par):
                xt_, gg, sga, sgb = xtbs[par], ggs[par], sgas[par], sgbs[par]
                conv_fn()
                op(POOL, "tensor_copy", out=xcb[:, :, 0:n], in_=xc[:, :, 0:n])
                yield
                gb_ = gbank()
                for c in range(4):
                    op(PE, "matmul", out=gb_[:, c * n:(c + 1) * n], lhsT=wabd[:, c, :], rhs=xcb[:, c, 0:n], start=True, stop=True)
                for c in range(4):
                    op(ACT, "activation", out=ra[:, c, 0:n], in_=gb_[:, c * n:(c + 1) * n], func=AF.Sigmoid, bias=lba[:, c:c + 1])
                gb2 = gbank()
                for c in range(4):
                    op(PE, "matmul", out=gb2[:, c * n:(c + 1) * n], lhsT=wxbd[:, c, :], rhs=xcb[:, c, 0:n], start=True, stop=True)
                for c in range(4):
                    op(ACT, "activation", out=ih[:, c, 0:n], in_=gb2[:, c * n:(c + 1) * n], func=AF.Sigmoid, bias=lbx[:, c:c + 1])
                yield
                for c in range(4):
                    op(ACT, "activation", out=ra[:, c, 0:n], in_=ra[:, c, 0:n], func=AF.Exp, scale=lcl[:, c:c + 1])
                yield
                op(DVE, "tensor_tensor", out=mu[:, :, 0:n], in0=ra[:, :, 0:n], in1=ra[:, :, 0:n], op=ALU.mult)
                op(DVE, "tensor_scalar", out=mu[:, :, 0:n], in0=mu[:, :, 0:n], scalar1=-1.0, scalar2=1.0, op0=ALU.mult, op1=ALU.add)
                op(DVE, "tensor_scalar", out=mu[:, :, 0:n], in0=mu[:, :, 0:n], scalar1=0.0, scalar2=None, op0=ALU.max)
                yield
                op(ACT, "activation", out=mu[:, :, 0:n], in_=mu[:, :, 0:n], func=AF.Sqrt)
                yield
                op(DVE, "tensor_tensor", out=mu[:, :, 0:n], in0=mu[:, :, 0:n], in1=ih[:, :, 0:n], op=ALU.mult)
                op(DVE, "tensor_tensor", out=mu[:, :, 0:n], in0=mu[:, :, 0:n], in1=xc[:, :, 0:n], op=ALU.mult)
                yield
                scan_fn()
                yield
                op(DVE, "tensor_tensor", out=ybi[:, :, 0:n], in0=hl[:, :, 0:n], in1=gg[:, :, 0:n], op=ALU.mult)
                yield
                for half in range(2):
                    ga_ = gbank()
                    for cc in range(4):
                        col = half * 4 + cc
                        for c in range(4):
                            op(PE, "matmul", out=ga_[:, cc * n:(cc + 1) * n], lhsT=wpa[:, c, col * 128:(col + 1) * 128],
                               rhs=attT_fn(c), start=(c == 0), stop=(c == 3))
                    op(DVE, "tensor_tensor", out=t1[:, :, 0:n], in0=ga_[:, 0:4 * n].rearrange("p (c n) -> p c n", c=4),
                       in1=sga[:, half * 4:half * 4 + 4, 0:n], op=ALU.mult)
                    gb3 = gbank()
                    for cc in range(4):
                        col = half * 4 + cc
                        for c in range(4):
                            op(PE, "matmul", out=gb3[:, cc * n:(cc + 1) * n], lhsT=wpb[:, c, col * 128:(col + 1) * 128],
                               rhs=ybi[:, c, 0:n], start=(c == 0), stop=(c == 3))
                    op(DVE, "tensor_tensor", out=t2[:, :, 0:n], in0=gb3[:, 0:4 * n].rearrange("p (c n) -> p c n", c=4),
                       in1=sgb[:, half * 4:half * 4 + 4, 0:n], op=ALU.mult)
                    yield
                    op(DVE, "tensor_tensor", out=mg[:, half * 4:half * 4 + 4, 0:n], in0=t1[:, :, 0:n], in1=t2[:, :, 0:n], op=ALU.add)
                    yield
                for half in range(2):
                    ob = OB[half]
                    for kc in range(8):
                        op(PE, "matmul", out=ob[0:n, :], lhsT=mg[:, kc, 0:n], rhs=wo[:, kc, half * 512:(half + 1) * 512],
                           start=(kc == 0), stop=(kc == 7))
                    yield
                    op(DVE, "tensor_tensor", out=xt_[0:n, half * 512:(half + 1) * 512], in0=ob[0:n, :],
                       in1=xt_[0:n, half * 512:(half + 1) * 512], op=ALU.add)
                dma(SP, out=x1_d[x1row0:x1row0 + n, :], in_=xt_[0:n, :])
                yield

            def stageF(i):
                yield from front_b(128, x_p[i * 128:(i + 1) * 128, :], True, i % 2)

            def stageM(i):
                par = i % 2
                xext = xexts[par]

                def conv_fn():
                    for c in range(4):
                        op(DVE, "tensor_scalar", out=xc[:, c, :], in0=xext[:, c, 0:128], scalar1=lcw[:, 0, c:c + 1],
                           scalar2=lcb[:, c:c + 1], op0=ALU.mult, op1=ALU.add)
                        for j in range(1, 4):
                            op(DVE, "scalar_tensor_tensor", out=xc[:, c, :], in0=xext[:, c, j:j + 128], scalar=lcw[:, j, c:c + 1],
                               in1=xc[:, c, :], op0=ALU.mult, op1=ALU.add)

                def scan_fn():
                    for c in range(4):
                        op(DVE, "tensor_tensor_scan", out=hl[:, c, :], data0=ra[:, c, :], data1=mu[:, c, :],
                           initial=hst[:, c:c + 1], op0=ALU.mult, op1=ALU.add)
                    op(DVE, "tensor_copy", out=hst[:, :], in_=hl[:, :, 127])

                yield from mixer_back(128, lambda c: attA[:, c, i * 128:(i + 1) * 128], conv_fn, scan_fn, i * 128, par)
                if i == NT - 1:
                    for c in range(4):
                        dma_cols(lc_p, c, xext[:, c, 128:131], False)
                    dma(SP, out=lh_p[:].rearrange("(c p) -> p c", p=128), in_=hst[:, :], allow_slow_non_contiguous=True)

            stages = [stageF, stageM]
            for step in range(NT + len(stages) - 1):
                alive = [stages[s](step - s) for s in range(len(stages)) if 0 <= step - s < NT]
                while alive:
                    nxt = []
                    for gnr in alive:
                        try:
                            next(gnr)
                            nxt.append(gnr)
                        except StopIteration:
                            pass
                    alive = nxt

            if do_sample:
                for c in range(4):
                    for b in range(NSB):
                        dma_cols(s_lc[b], c, exts[:, c, b, 0:3], True)
                    dma_cols(s_lh, c, h0T[:, c, :], True)
                for _ in front_b(NS, x_s[:, :], False, 0):
                    pass

                def conv_s():
                    for c in range(4):
                        xv = xc[:, c, 0:NS].rearrange("p (b t) -> p b t", b=NSB)
                        op(DVE, "tensor_scalar", out=xv, in0=exts[:, c, :, 0:NST], scalar1=lcw[:, 0, c:c + 1],
                           scalar2=lcb[:, c:c + 1], op0=ALU.mult, op1=ALU.add)
                        for j in range(1, 4):
                            op(DVE, "scalar_tensor_tensor", out=xv, in0=exts[:, c, :, j:j + NST], scalar=lcw[:, j, c:c + 1],
                               in1=xv, op0=ALU.mult, op1=ALU.add)

                def scan_s():
                    for c in range(4):
                        for b in range(NSB):
                            op(DVE, "tensor_tensor_scan", out=hl[:, c, b * NST:(b + 1) * NST], data0=ra[:, c, b * NST:(b + 1) * NST],
                               data1=mu[:, c, b * NST:(b + 1) * NST], initial=h0T[:, c, b:b + 1], op0=ALU.mult, op1=ALU.add)
                    op(DVE, "tensor_copy", out=hls[:, :, :], in_=hl[:, :, 0:NS].rearrange("p c (b t) -> p c b t", b=NSB)[:, :, :, NST - 1])

                for _ in mixer_back(NS, lambda c: s_attT[:, c, 0:NS], conv_s, scan_s, SEQ, 0):
                    pass
                for c in range(4):
                    for b in range(NSB):
                        dma_cols(lc_s[b], c, exts[:, c, b, 4:7], False)
                    dma_cols(lh_s, c, hls[:, c, :], False)
        K.barrier()

    if do_ffn:
        with contextlib.ExitStack() as s2:
            wup = K.scoped(s2, "wup", [128, 8, 2 * DFF], BF16)
            dma(SP, out=gbc[:], in_=g_ffn[:].broadcast_to([128, D]))
            gbc2 = K.scoped(s2, "gbc2", [128, D], F32)
            dma(SP, out=gbc2[:], in_=g_fin[:].broadcast_to([128, D]))
            x1ts = [K.scoped(s2, f"x1t{i}", [128, D], F32) for i in range(2)]
            h2 = K.scoped(s2, "h2", [128, D], BF16)
            h2Ts = [K.scoped(s2, f"h2T{i}", [128, 8, 128], BF16) for i in range(2)]
            uexl = [K.scoped(s2, f"uex{c}", [128, 130], F32) for c in range(24)]
            uexsl = [K.scoped(s2, f"uexs{c}", [128, NSB, 6], F32) for c in range(24)]
            accs = [K.scoped(s2, f"facc{i}", [128, 128], F32) for i in range(4)]
            gls = [K.scoped(s2, f"fgl{i}", [128, 128], F32) for i in range(4)]
            gTs = [K.scoped(s2, f"gT{i}", [128, 24, 128], BF16) for i in range(2)]
            x2 = K.scoped(s2, "x2", [128, D], F32)
            for c in range(24):
                op(POOL, "memset", ap=uexl[c][:, 0:2], constant=0.0)
            stg = K.scoped(s2, "stg", [8, 512], F32)
            for kc in range(8):
                for c0 in range(0, 2 * DFF, 2048):
                    dma(POOL, out=wup[:, kc, c0:c0 + 2048], in_=w_up[kc * 128:(kc + 1) * 128, c0:c0 + 2048])
            rf = {"a": 0}

            def prep(n, row0, par):
                x1t, h2T = x1ts[par], h2Ts[par]
                dma(SP, out=x1t[0:n, :], in_=x1_d[row0:row0 + n, :])
                rmsnorm_to_bf(x1t, n, gbc, h2)
                transpose_cols(h2, n, D, lambda c: h2T[:, c, 0:n])

            ubs = [K.scoped(s2, f"ubs{i}", [128, 128], BF16) for i in range(6)]

            def up_chunks(n, par, prompt):
                h2T, gT = h2Ts[par], gTs[par]
                banks = {}

                def s1(c):
                    g = gbank()
                    for part in range(2):
                        for kc in range(8):
                            op(PE, "matmul", out=g[:, part * n:(part + 1) * n],
                               lhsT=wup[:, kc, part * DFF + c * 128: part * DFF + (c + 1) * 128],
                               rhs=h2T[:, kc, 0:n], start=(kc == 0), stop=(kc == 7))
                    banks[c] = g

                def s2(c):
                    g = banks.pop(c)
                    if prompt:
                        op(ACT, "copy", out=uexl[c][:, 2:2 + n], in_=g[:, 0:n])
                    else:
                        op(ACT, "copy", out=uexsl[c][:, :, 2:6], in_=g[:, 0:n].rearrange("p (b t) -> p b t", b=NSB))
                    op(ACT, "copy", out=ubs[c % 6][:, 0:n], in_=g[:, n:2 * n])

                def s3(c):
                    acc = accs[c % 4]
                    if prompt:
                        ue = uexl[c]
                        taps = [ue[:, j:j + n] for j in range(3)]
                        av = acc[:, 0:n]
                    else:
                        ue = uexsl[c]
                        taps = [ue[:, :, j:j + NST] for j in range(3)]
                        av = acc[:, 0:n].rearrange("p (b t) -> p b t", b=NSB)
                    op(ACT, "activation", out=av, in_=taps[0], func=AF.Identity, scale=fcw[:, 0, c:c + 1], bias=fcb[:, c:c + 1])
                    for j in (1, 2):
                        op(DVE, "scalar_tensor_tensor", out=av, in0=taps[j], scalar=fcw[:, j, c:c + 1], in1=av,
                           op0=ALU.mult, op1=ALU.add)
                    if prompt:
                        op(POOL, "tensor_copy", out=ue[:, 0:2], in_=ue[:, 128:130])

                def s4(c):
                    op(ACT, "activation", out=gls[c % 4][:, 0:n], in_=accs[c % 4][:, 0:n], func=AF.Gelu_apprx_tanh)

                def s5(c):
                    op(POOL, "tensor_tensor", out=gT[:, c, 0:n], in0=ubs[c % 6][:, 0:n], in1=gls[c % 4][:, 0:n], op=ALU.mult)

                stages_ = [s1, s2, s3, s4, s5]
                for k in range(24 + 4):
                    for si, fn in enumerate(stages_):
                        c = k - si
                        if 0 <= c < 24:
                            fn(c)

            def down_final(n, par, yout):
                x1t, gT = x1ts[par], gTs[par]
                for half in range(2):
                    ob = OB[half]
                    for c in range(24):
                        op(PE, "matmul", out=ob[0:n, :], lhsT=gT[:, c, 0:n], rhs=wdn[:, c, half * 512:(half + 1) * 512],
                           start=(c == 0), stop=(c == 23))
                    op(DVE, "tensor_tensor", out=x2[0:n, half * 512:(half + 1) * 512], in0=ob[0:n, :],
                       in1=x1t[0:n, half * 512:(half + 1) * 512], op=ALU.add)
                rmsnorm_to_bf(x2, n, gbc2, x2, junk=h2)
                dma(SP, out=yout, in_=x2[0:n, :])

            prep(128, 0, 0)
            for i in range(NT):
                par = i % 2
                up_chunks(128, par, True)
                if i == NT - 1:
                    if do_sample:
                        for pc in range(6):
                            dma(SP, out=stg[:, :], in_=s_fc[:, :, pc * 512:(pc + 1) * 512].rearrange("b j f -> (b j) f"))
                            g = gbank()
                            for cc in range(4):
                                op(PE, "transpose", out=g[:, cc * 8:(cc + 1) * 8], in_=stg[0:8, cc * 128:(cc + 1) * 128],
                                   identity=idf[0:8, 0:8])
                            for cc in range(4):
                                op(ACT, "copy", out=uexsl[pc * 4 + cc][:, :, 0:2],
                                   in_=g[:, cc * 8:(cc + 1) * 8].rearrange("p (b j) -> p b j", b=NSB))
                        prep(NS, SEQ, 1 - par)
                else:
                    prep(128, (i + 1) * 128, 1 - par)
                down_final(128, par, y_p[i * 128:(i + 1) * 128, :])
            if do_sample:
                par = NT % 2
                up_chunks(NS, par, False)
                down_final(NS, par, y_s[:, :])
            def out_rows(src_fn, nrows, dram_fn):
                stgs = [x1ts[0], x1ts[1], x2]
                for piece in range(3):
                    st = stgs[piece]
                    for half in range(2):
                        g = gbank()
                        for cc in range(4):
                            c = piece * 8 + half * 4 + cc
                            op(PE, "transpose", out=g[0:nrows, cc * 128:(cc + 1) * 128], in_=src_fn(c), identity=idf[:, :])
                        op(ACT, "copy", out=st[0:nrows, half * 512:(half + 1) * 512], in_=g[0:nrows, 0:512])
                    dma(SP, out=dram_fn(piece), in_=st[0:nrows, :])

            out_rows(lambda c: uexl[c][:, 0:2], 2, lambda pc: fc_p[:, pc * 1024:(pc + 1) * 1024])
            if do_sample:
                cmps = [K.scoped(s2, f"cmp{i}", [128, 8], F32) for i in range(4)]

                def src_s(c):
                    t_ = cmps[c % 4]
                    op(POOL, "tensor_copy", out=t_[:, :].rearrange("p (b j) -> p b j", b=NSB), in_=uexsl[c][:, :, 4:6])
                    return t_[:, :]

                out_rows(src_s, 8, lambda pc: fc_s[:, :, pc * 1024:(pc + 1) * 1024].rearrange("b j f -> (b j) f"))
    K.finish()
    return nc


def _host_consts():
    inv = np.power(np.float32(10000.0), -np.arange(32, dtype=np.float32) / np.float32(32)).astype(np.float32)

    def tab(pos):
        ang = pos.astype(np.float32)[:, None] * inv[None, :]
        return np.cos(ang).astype(np.float32), np.sin(ang).astype(np.float32)

    cp, sp_ = tab(np.arange(SEQ))
    pos_s = np.tile(PAST + np.arange(NST), NSB)
    cs, ss_ = tab(pos_s)
    q = np.arange(128)
    cmask = np.where(q[None, :] <= q[:, None], 0.0, NEG).astype(np.float32)
    pow2 = np.tile((2.0 ** -np.arange(NBIS + 2, dtype=np.float64)).astype(np.float32)[None, :], (128, 1))
    blk = (q[:, None] // 32 == q[None, :] // 32).astype(np.float32).astype(ml_dtypes.bfloat16)
    misc = np.zeros((128, 1024), np.float32)
    seg, tq = q % 32, q // 32
    misc[:, 0] = 8 * (seg // 2) + (seg % 2)
    tp = np.arange(4)
    misc[:, 1:5] = np.where((seg[:, None] == 0) & (tp[None, :] <= tq[:, None]), 0.0, NEG)
    misc[:, 16:144] = np.arange(128)[None, :]
    agg = np.zeros((128, NSB, NS), np.float32)
    selq = np.zeros((NS, NSB, 128), np.float32)
    for b in range(NSB):
        agg[q, b, NST * b + tq] = 1.0
        selq[NST * b + tq, b, q] = 1.0
    misc[:, 160:160 + NSB * NS] = agg.reshape(128, -1)
    p0 = np.zeros((8, 4, 32, 128), np.float32)
    for t in range(4):
        for s_ in range(32):
            p0[:, t, s_, t * 32 + s_] = 1.0
    p0 = p0.reshape(32, 32 * 128).astype(ml_dtypes.bfloat16)
    return {
        "c_ident_bf": np.eye(128, dtype=np.float32).astype(ml_dtypes.bfloat16),
        "c_ident_f": np.eye(128, dtype=np.float32),
        "c_cos_p": cp, "c_sin_p": sp_, "c_cos_s": cs, "c_sin_s": ss_,
        "c_cmask": cmask, "c_pow2": pow2, "c_misc": misc, "c_blk": blk,
        "c_p0": p0, "c_selq": selq.reshape(NS, NSB * 128),
    }


_OUT_NAMES = ["y_prompt", "y_sample", "k_prompt", "v_prompt", "kidx_prompt", "lru_conv_prompt", "lru_h_prompt",
              "ffn_conv_prompt", "k_sample", "v_sample", "kidx_sample", "lru_conv_sample", "lru_h_sample",
              "ffn_conv_sample"]


def make_in_map(inputs, c, consts):
    f = lambda a: np.ascontiguousarray(np.asarray(a))
    n_phys = inputs["cache_k"].shape[1]
    m = {
        "x_prompt": f(inputs["x_prompt"][c]),
        "x_sample": f(inputs["x_sample"][NSB * c:NSB * (c + 1)]).reshape(NS, D),
        "cache_k": np.asarray(inputs["cache_k"]).reshape(n_phys * 128, 512),
        "cache_v": np.asarray(inputs["cache_v"]).reshape(n_phys * 128, 512),
        "cache_kidx": np.asarray(inputs["cache_kidx"]).reshape(n_phys * 128, 64),
        "page_table": f(inputs["page_table"][NSB * c:NSB * (c + 1)]).astype(np.int32),
        "state_lru_conv": f(inputs["state_lru_conv"][0, NSB * c:NSB * (c + 1)]),
        "state_lru_h": f(inputs["state_lru_h"][0, NSB * c:NSB * (c + 1)]),
        "state_ffn_conv": f(inputs["state_ffn_conv"][0, NSB * c:NSB * (c + 1)]),
        "norm_mix_g": f(inputs["norm_mix_g"]).reshape(1, D),
        "w_in": f(inputs["w_in"][0]),
        "lru_conv_w": f(inputs["lru_conv_w"][0]),
        "lru_conv_b": f(inputs["lru_conv_b"][0]),
        "lru_wa": f(inputs["lru_wa"][0]),
        "lru_ba": f(inputs["lru_ba"][0]),
        "lru_wx": f(inputs["lru_wx"][0]),
        "lru_bx": f(inputs["lru_bx"][0]),
        "lru_lambda": f(inputs["lru_lambda"][0]),
        "w_proj_a": f(inputs["w_proj_a"][0]),
        "w_proj_b": f(inputs["w_proj_b"][0]),
        "w_out": f(inputs["w_out"][0]),
        "norm_ffn_g": f(inputs["norm_ffn_g"]).reshape(1, D),
        "w_up": f(inputs["w_up"][0]),
        "ffn_conv_w": f(inputs["ffn_conv_w"][0]),
        "ffn_conv_b": f(inputs["ffn_conv_b"][0]),
        "w_down": f(inputs["w_down"][0]),
        "norm_final_g": f(inputs["norm_final_g"]).reshape(1, D),
    }
    m.update(consts)
    return m


def assemble(results):
    n = len(results)
    g = lambda name: [np.asarray(r[name]) for r in results]
    y_p = np.stack(g("y_prompt"))
    y_s = np.concatenate([a.reshape(NSB, NST, D) for a in g("y_sample")])
    k_p = np.stack([a.reshape(SEQ, 8, 64) for a in g("k_prompt")])[None]
    v_p = np.stack([a.reshape(SEQ, 8, 64) for a in g("v_prompt")])[None]
    ki_p = np.stack(g("kidx_prompt"))[None]
    lc_p = np.stack(g("lru_conv_prompt"))[None]
    lh_p = np.stack([a.reshape(512) for a in g("lru_h_prompt")])[None]
    fc_p = np.stack(g("ffn_conv_prompt"))[None]
    k_s = np.concatenate([a.reshape(NSB, NST, 8, 64) for a in g("k_sample")])[None]
    v_s = np.concatenate([a.reshape(NSB, NST, 8, 64) for a in g("v_sample")])[None]
    ki_s = np.concatenate([a.reshape(NSB, NST, 64) for a in g("kidx_sample")])[None]
    lc_s = np.concatenate(g("lru_conv_sample"))[None]
    lh_s = np.concatenate(g("lru_h_sample"))[None]
    fc_s = np.concatenate(g("ffn_conv_sample"))[None]
    outs = (y_p, y_s, k_p, v_p, ki_p, lc_p, lh_p, fc_p, k_s, v_s, ki_s, lc_s, lh_s, fc_s)
    return tuple(np.ascontiguousarray(o, dtype=np.float32) for o in outs)


def kernel(**inputs):
    n_cores = 8
    n_phys = int(np.asarray(inputs["cache_k"]).shape[1])
    consts = _host_consts()
    nc = build(n_phys)
    in_maps = [make_in_map(inputs, c, consts) for c in range(n_cores)]
    res = run_bass_kernel_spmd(nc, in_maps, core_ids=list(range(n_cores)))
    return assemble(res.results)
```
